# Optimizing a Trainium2 kernel written in Bass

```python
import math
import jax, jax.numpy as jnp
from jax import lax
import numpy as np

D_MODEL = 1024
BATCH = 4
SEQ = 4096
DEPTH = 2
DEC_BATCH = 128
DEC_SEQ = 4
PAST_LEN = 2048
PAGE_SIZE = 128

ATT_HEADS = 8
ATT_HEAD_DIM = 64
ATT_WIDTH = ATT_HEADS * ATT_HEAD_DIM
MOBA_BLOCK = 256
MOBA_TOPK = 3
MOBA_Q_CHUNK = 32
CONV_CH = 512
CONV_GROUPS = 8
CONV_LEN = 31
HGRN_HEADS = 8
HGRN_DK = 128
HGRN_DV = 128
HGRN_KEY_WIDTH = HGRN_HEADS * HGRN_DK
HGRN_VAL_WIDTH = HGRN_HEADS * HGRN_DV
HGRN_CHUNK = 64
EPS = 1e-6

AB_SPLITS = [ATT_WIDTH] * 4 + [CONV_CH] * 3
C_SPLITS = [HGRN_KEY_WIDTH, HGRN_KEY_WIDTH, HGRN_VAL_WIDTH, HGRN_VAL_WIDTH]

kernel_name = 'moba_conformer_hgrn2_hybrid_step'


def _split(z, sizes):
    return jnp.split(z, [int(c) for c in np.cumsum(sizes)[:-1]], axis=-1)


def _rms(x, g):
    xf = x.astype(jnp.float32)
    y = xf * lax.rsqrt(jnp.mean(xf * xf, axis=-1, keepdims=True) + EPS)
    return (y * g.astype(jnp.float32)).astype(x.dtype)


def _layernorm(x, g, b):
    xf = x.astype(jnp.float32)
    mu = jnp.mean(xf, axis=-1, keepdims=True)
    var = jnp.mean(jnp.square(xf - mu), axis=-1, keepdims=True)
    y = (xf - mu) * lax.rsqrt(var + EPS) * g.astype(jnp.float32) + b.astype(jnp.float32)
    return y.astype(x.dtype)


def _moba_combine(s_own, v_own, s_sel=None, v_sel=None):
    if s_sel is None:
        p = jax.nn.softmax(s_own, axis=-1)
        return jnp.einsum('bhqk,bhkd->bhqd', p.astype(v_own.dtype), v_own)
    shp = s_sel.shape
    n_sel = shp[-2] * shp[-1]
    s = jnp.concatenate([s_sel.reshape(shp[:-2] + (n_sel,)), s_own], axis=-1)
    p = jax.nn.softmax(s, axis=-1)
    p_sel = p[..., :n_sel].reshape(shp).astype(v_sel.dtype)
    p_own = p[..., n_sel:].astype(v_own.dtype)
    return (jnp.einsum('bhqnk,bhqnkd->bhqd', p_sel, v_sel)
            + jnp.einsum('bhqk,bhkd->bhqd', p_own, v_own))


def _moba_prompt(q, k, v):
    B, S, H, Dh = q.shape
    nb = -(-S // MOBA_BLOCK)
    pad = nb * MOBA_BLOCK - S
    scale = Dh ** -0.5

    def blocks(a):
        a = jnp.pad(a, ((0, 0), (0, pad), (0, 0), (0, 0)))
        return a.reshape(B, nb, MOBA_BLOCK, H, Dh).transpose(0, 3, 1, 2, 4)

    kb, vb = blocks(k), blocks(v)
    k_mean = jnp.mean(kb.astype(jnp.float32), axis=3)
    topk = min(MOBA_TOPK, nb)
    qh = q.transpose(0, 2, 1, 3)
    qlen = math.gcd(S, MOBA_Q_CHUNK)
    b_idx = jnp.arange(B)[:, None, None, None]
    h_idx = jnp.arange(H)[None, :, None, None]
    blk_ids = jnp.arange(nb)

    def one_chunk(c):
        t0 = c * qlen
        qc = lax.dynamic_slice_in_dim(qh, t0, qlen, axis=2)
        j = t0 // MOBA_BLOCK
        gate = jnp.einsum('bhqd,bhnd->bhqn', qc.astype(jnp.float32), k_mean)
        gate = jnp.where(blk_ids < j, gate, -jnp.inf)
        _, sel = lax.top_k(gate, topk)
        valid = sel < j
        k_sel = kb[b_idx, h_idx, sel]
        v_sel = vb[b_idx, h_idx, sel]
        s_sel = jnp.einsum('bhqd,bhqnkd->bhqnk', qc, k_sel).astype(jnp.float32) * scale
        s_sel = jnp.where(valid[..., None], s_sel, -jnp.inf)
        k_own = lax.dynamic_index_in_dim(kb, j, axis=2, keepdims=False)
        v_own = lax.dynamic_index_in_dim(vb, j, axis=2, keepdims=False)
        s_own = jnp.einsum('bhqd,bhkd->bhqk', qc, k_own).astype(jnp.float32) * scale
        q_pos = t0 + jnp.arange(qlen)
        k_pos = j * MOBA_BLOCK + jnp.arange(MOBA_BLOCK)
        s_own = jnp.where(k_pos[None, :] <= q_pos[:, None], s_own, -jnp.inf)
        return _moba_combine(s_own, v_own, s_sel, v_sel)

    out = lax.map(one_chunk, jnp.arange(S // qlen))
    return out.transpose(1, 0, 3, 2, 4).reshape(B, S, H * Dh)


def _moba_sample(q, k_new, v_new, cache_k, cache_v, page_table):
    N, T, H, Dh = q.shape
    n_pages = page_table.shape[1]
    past = n_pages * PAGE_SIZE
    ppb = MOBA_BLOCK // PAGE_SIZE
    j = past // MOBA_BLOCK
    scale = Dh ** -0.5
    qh = q.transpose(0, 2, 1, 3)
    own_pages = page_table[:, j * ppb:]
    n_own = own_pages.shape[1] * PAGE_SIZE
    k_own = jnp.concatenate([cache_k[own_pages].reshape(N, n_own, H, Dh), k_new], axis=1)
    v_own = jnp.concatenate([cache_v[own_pages].reshape(N, n_own, H, Dh), v_new], axis=1)
    k_own = k_own.transpose(0, 2, 1, 3)
    v_own = v_own.transpose(0, 2, 1, 3)
    L = k_own.shape[2]
    s_own = jnp.einsum('bhqd,bhkd->bhqk', qh, k_own).astype(jnp.float32) * scale
    q_pos = past + jnp.arange(T)
    k_pos = j * MOBA_BLOCK + jnp.arange(L)
    s_own = jnp.where(k_pos[None, :] <= q_pos[:, None], s_own, -jnp.inf)
    if j == 0:
        out = _moba_combine(s_own, v_own)
    else:
        past_pages = page_table[:, :j * ppb]
        k_mean = jnp.mean(cache_k[past_pages].astype(jnp.float32)
                          .reshape(N, j, ppb * PAGE_SIZE, H, Dh), axis=2)
        gate = jnp.einsum('bhqd,bnhd->bhqn', qh.astype(jnp.float32), k_mean)
        topk = min(MOBA_TOPK, j)
        _, sel = lax.top_k(gate, topk)
        phys = page_table[jnp.arange(N)[:, None, None, None, None],
                          sel[..., None] * ppb + jnp.arange(ppb)]
        h_idx = jnp.arange(H)[None, :, None, None, None]
        k_sel = cache_k[phys, :, h_idx].reshape(N, H, T, topk, MOBA_BLOCK, Dh)
        v_sel = cache_v[phys, :, h_idx].reshape(N, H, T, topk, MOBA_BLOCK, Dh)
        s_sel = jnp.einsum('bhqd,bhqnkd->bhqnk', qh, k_sel).astype(jnp.float32) * scale
        out = _moba_combine(s_own, v_own, s_sel, v_sel)
    return out.transpose(0, 2, 1, 3).reshape(N, T, H * Dh)


def _ab_in(x, norm_g, w_in, q_g, k_g):
    B, T, _ = x.shape
    h = _rms(x, norm_g)
    q, k, v, g_att, u_a, u_b, g_conv = _split(h @ w_in, AB_SPLITS)
    q = _rms(q.reshape(B, T, ATT_HEADS, ATT_HEAD_DIM), q_g)
    k = _rms(k.reshape(B, T, ATT_HEADS, ATT_HEAD_DIM), k_g)
    v = v.reshape(B, T, ATT_HEADS, ATT_HEAD_DIM)
    u = u_a * jax.nn.sigmoid(u_b)
    return q, k, v, g_att, u, g_conv


def _conv_branch(u, buf, conv_w, conv_b, ln_g, ln_b):
    up = jnp.concatenate([buf, u], axis=1)
    y = lax.conv_general_dilated(up, conv_w[:, None, :].astype(up.dtype), (1,), 'VALID',
                                 dimension_numbers=('NWC', 'WIO', 'NWC'),
                                 feature_group_count=CONV_CH) + conv_b.astype(up.dtype)
    y = jax.nn.silu(_layernorm(y, ln_g, ln_b))
    return y, up[:, -(CONV_LEN - 1):]


def _ab_out(x, o_att, g_att, o_conv, g_conv, w_out):
    m = jnp.concatenate([o_att * jax.nn.silu(g_att), o_conv * jax.nn.silu(g_conv)], axis=-1)
    return x + m @ w_out


def _hgrn2_scan(q, k, logf, v, S0):
    B, T, H, DK = q.shape
    DV = v.shape[-1]
    C = math.gcd(T, HGRN_CHUNK)
    n = T // C

    def to_chunks(a):
        return a.reshape(B, n, C, H, a.shape[-1]).transpose(1, 0, 3, 2, 4)

    causal = jnp.tril(jnp.ones((C, C), dtype=bool))[:, :, None]

    def step(S, inp):
        qc, kc, lc, vc = inp
        b = jnp.cumsum(lc, axis=2)
        o_inter = jnp.einsum('bhtk,bhkv->bhtv', qc * jnp.exp(b), S)
        diff = b[:, :, :, None, :] - b[:, :, None, :, :]
        decay = jnp.exp(jnp.where(causal, diff, -jnp.inf))
        a = jnp.einsum('bhtk,bhsk,bhtsk->bhts', qc, kc, decay)
        o = o_inter + jnp.einsum('bhts,bhsv->bhtv', a, vc)
        b_last = b[:, :, -1, :]
        S = (jnp.exp(b_last)[..., None] * S
             + jnp.einsum('bhsk,bhsv->bhkv', kc * jnp.exp(b_last[:, :, None, :] - b), vc))
        return S, o

    S, o = lax.scan(step, S0, (to_chunks(q), to_chunks(k), to_chunks(logf), to_chunks(v)))
    return o.transpose(1, 0, 3, 2, 4).reshape(B, T, H, DV), S


def _c_layer(x, S0, layer, norm_g, w_in, lb_logits, o_g, w_out):
    B, T, _ = x.shape
    h = _rms(x, norm_g)
    q, fz, i, g = _split(h @ w_in, C_SPLITS)
    p = jax.nn.softmax(lb_logits.astype(jnp.float32), axis=0)
    lb = (jnp.cumsum(p, axis=0) - p[0:1])[layer]
    f = lb + (1.0 - lb) * jax.nn.sigmoid(fz.astype(jnp.float32))
    logf = jnp.log(f).reshape(B, T, HGRN_HEADS, HGRN_DK)
    kk = (1.0 - f).reshape(B, T, HGRN_HEADS, HGRN_DK)
    qq = jax.nn.silu(q.astype(jnp.float32)).reshape(B, T, HGRN_HEADS, HGRN_DK)
    vv = i.astype(jnp.float32).reshape(B, T, HGRN_HEADS, HGRN_DV)
    o, S = _hgrn2_scan(qq, kk, logf, vv, S0.astype(jnp.float32))
    o = _rms(o, o_g).reshape(B, T, HGRN_VAL_WIDTH).astype(x.dtype)
    return x + (o * jax.nn.silu(g)) @ w_out, S.astype(x.dtype)


def setup_inputs(seed: int = 0) -> dict:
    key = jax.random.key(seed)
    ks = jax.random.split(key, 24)
    n_pages = PAST_LEN // PAGE_SIZE
    n_used = DEC_BATCH * n_pages
    n_pool = n_used + n_used // 4

    def nrm(k, shape, s):
        return s * jax.random.normal(k, shape, jnp.float32)

    page_table = jax.random.permutation(ks[6], n_pool)[:n_used].reshape(DEC_BATCH, n_pages).astype(jnp.int32)
    return {
        'x_prompt': nrm(ks[0], (BATCH, SEQ, D_MODEL), 1.0),
        'x_sample': nrm(ks[1], (DEC_BATCH, DEC_SEQ, D_MODEL), 1.0),
        'cache_k': nrm(ks[2], (n_pool, PAGE_SIZE, ATT_HEADS, ATT_HEAD_DIM), 1.0),
        'cache_v': nrm(ks[3], (n_pool, PAGE_SIZE, ATT_HEADS, ATT_HEAD_DIM), 1.0),
        'state_conv': nrm(ks[4], (DEC_BATCH, CONV_LEN - 1, CONV_CH), 0.5),
        'state_hgrn': nrm(ks[5], (DEC_BATCH, HGRN_HEADS, HGRN_DK, HGRN_DV), 0.3),
        'page_table': page_table,
        'norm_0': 1.0 + nrm(ks[7], (D_MODEL,), 0.02),
        'w_in_0': nrm(ks[8], (D_MODEL, sum(AB_SPLITS)), D_MODEL ** -0.5),
        'q_norm_0': 1.0 + nrm(ks[9], (ATT_HEAD_DIM,), 0.02),
        'k_norm_0': 1.0 + nrm(ks[10], (ATT_HEAD_DIM,), 0.02),
        'conv_w_0': nrm(ks[11], (CONV_LEN, CONV_CH), CONV_LEN ** -0.5),
        'conv_b_0': nrm(ks[12], (CONV_CH,), 0.01),
        'conv_ln_g_0': 1.0 + nrm(ks[13], (CONV_CH,), 0.02),
        'conv_ln_b_0': nrm(ks[14], (CONV_CH,), 0.01),
        'w_out_0': nrm(ks[15], (ATT_WIDTH + CONV_CH, D_MODEL), (ATT_WIDTH + CONV_CH) ** -0.5),
        'norm_1': 1.0 + nrm(ks[16], (D_MODEL,), 0.02),
        'w_in_1': nrm(ks[17], (D_MODEL, sum(C_SPLITS)), D_MODEL ** -0.5),
        'lb_logits': nrm(ks[18], (DEPTH, HGRN_KEY_WIDTH), 0.5),
        'o_norm_1': 1.0 + nrm(ks[19], (HGRN_DV,), 0.02),
        'w_out_1': nrm(ks[20], (HGRN_VAL_WIDTH, D_MODEL), HGRN_VAL_WIDTH ** -0.5),
    }


def reference(x_prompt, x_sample, cache_k, cache_v, state_conv, state_hgrn, page_table,
              norm_0, w_in_0, q_norm_0, k_norm_0, conv_w_0, conv_b_0, conv_ln_g_0, conv_ln_b_0, w_out_0,
              norm_1, w_in_1, lb_logits, o_norm_1, w_out_1):
    xp, xs = x_prompt, x_sample
    for layer in range(DEPTH):
        if layer % 2 == 0:
            q, k_prompt, v_prompt, ga, u, gc = _ab_in(xp, norm_0, w_in_0, q_norm_0, k_norm_0)
            oa = _moba_prompt(q, k_prompt, v_prompt)
            buf0 = jnp.zeros((xp.shape[0], CONV_LEN - 1, CONV_CH), u.dtype)
            oc, conv_prompt = _conv_branch(u, buf0, conv_w_0, conv_b_0, conv_ln_g_0, conv_ln_b_0)
            xp = _ab_out(xp, oa, ga, oc, gc, w_out_0)
            q, k_sample, v_sample, ga, u, gc = _ab_in(xs, norm_0, w_in_0, q_norm_0, k_norm_0)
            oa = _moba_sample(q, k_sample, v_sample, cache_k, cache_v, page_table)
            oc, conv_sample = _conv_branch(u, state_conv.astype(u.dtype), conv_w_0, conv_b_0,
                                           conv_ln_g_0, conv_ln_b_0)
            xs = _ab_out(xs, oa, ga, oc, gc, w_out_0)
        else:
            s0 = jnp.zeros((xp.shape[0], HGRN_HEADS, HGRN_DK, HGRN_DV), jnp.float32)
            xp, hgrn_prompt = _c_layer(xp, s0, layer, norm_1, w_in_1, lb_logits, o_norm_1, w_out_1)
            xs, hgrn_sample = _c_layer(xs, state_hgrn, layer, norm_1, w_in_1, lb_logits, o_norm_1, w_out_1)
    return (xp, xs, k_prompt, v_prompt, k_sample, v_sample, conv_prompt, conv_sample, hgrn_prompt, hgrn_sample)
```

```python
import contextlib
import numpy as np
import ml_dtypes
import concourse.bass as bass
import concourse.mybir as mybir
from concourse.bass_utils import run_bass_kernel_spmd

F32 = mybir.dt.float32
BF16 = mybir.dt.bfloat16
I32 = mybir.dt.int32
AF = mybir.ActivationFunctionType
ALU = mybir.AluOpType
AX = mybir.AxisListType

N_DMA_SEMS = 40
EPS = 1e-6
SEQ = 4096
NT = SEQ // 128
NS = 16
TS = 64
NEG = -1.0e30


class Buf:
    __slots__ = ("name", "last_w", "readers", "excl")

    def __init__(self, name="", excl=False):
        self.name = name
        self.last_w = None
        self.readers = []
        self.excl = excl


class Op:
    __slots__ = ("eng", "fn", "deps", "is_dma", "signal", "semval", "dsem", "dtarget", "prev_on_sem")

    def __init__(self, eng, fn, is_dma):
        self.eng = eng
        self.fn = fn
        self.deps = []
        self.is_dma = is_dma
        self.signal = False
        self.semval = None
        self.dsem = None
        self.dtarget = None
        self.prev_on_sem = None


class _Rec:
    def __init__(self):
        self.call = None

    def __getattr__(self, name):
        def f(*a, **k):
            self.call = (name, a, k)
            return None
        return f


class Prog:
    ENGS = ("pe", "act", "dve", "pool", "sp")

    def __init__(self, nc):
        self.nc = nc
        self.ops = []
        self.dma_rr = 0
        self.dma_last = [None] * N_DMA_SEMS
        self.dma_val = [0] * N_DMA_SEMS
        self.last_compute = {e: None for e in self.ENGS}

    def _add(self, eng, fn, reads, writes, is_dma):
        rec = _Rec()
        fn(rec)
        assert rec.call is not None
        op = Op(eng, rec.call, is_dma)
        ex = [b for b in reads if b.excl]
        if ex:
            reads = [b for b in reads if not b.excl]
            writes = list(writes) + ex
        deps = {}
        for b in reads:
            if b.last_w is not None:
                deps[id(b.last_w)] = b.last_w
        for b in writes:
            if b.last_w is not None:
                deps[id(b.last_w)] = b.last_w
            for r in b.readers:
                deps[id(r)] = r
        for d in deps.values():
            if d is op:
                continue
            if (not d.is_dma) and d.eng == eng and not is_dma and eng == "pe":
                continue
            op.deps.append(d)
            if not d.is_dma:
                d.signal = True
        for b in reads:
            b.readers.append(op)
        for b in writes:
            b.last_w = op
            b.readers = []
        if is_dma:
            s = self.dma_rr
            self.dma_rr = (self.dma_rr + 1) % N_DMA_SEMS
            op.dsem = s
            op.prev_on_sem = self.dma_last[s]
            self.dma_val[s] += 16
            op.dtarget = self.dma_val[s]
            self.dma_last[s] = op
        else:
            self.last_compute[eng] = op
        self.ops.append(op)
        return op

    def op(self, eng, fn, reads=(), writes=()):
        return self._add(eng, fn, reads, writes, False)

    def dma(self, eng, fn, reads=(), writes=()):
        return self._add(eng, fn, reads, writes, True)

    def barrier(self):
        deps = []
        for e in self.ENGS:
            o = self.last_compute[e]
            if o is not None:
                o.signal = True
                deps.append(o)
        for o in self.dma_last:
            if o is not None:
                deps.append(o)
        for e in self.ENGS:
            op = Op(e, None, False)
            op.deps = [d for d in deps if d.is_dma or d.eng != e]
            self.ops.append(op)

    def emit(self):
        nc = self.nc
        cnt = {e: 0 for e in self.ENGS}
        for op in self.ops:
            if op.fn is not None and not op.is_dma and op.signal:
                cnt[op.eng] += 1
                op.semval = cnt[op.eng]
        with contextlib.ExitStack() as st:
            esem = {e: st.enter_context(nc.semaphore("s_" + e)) for e in self.ENGS}
            dsem = [st.enter_context(nc.semaphore("d%d" % i)) for i in range(N_DMA_SEMS)]
            block = st.enter_context(nc.Block())
            ops = self.ops
            dma_final = [(dsem[i], self.dma_val[i]) for i in range(N_DMA_SEMS) if self.dma_val[i] > 0]

            def run(engname, eng):
                waited = {}

                def wait(key, sem, val):
                    if waited.get(key, 0) >= val:
                        return
                    eng.wait_ge(sem, val)
                    waited[key] = val

                for op in ops:
                    if op.eng != engname:
                        continue
                    for d in op.deps:
                        if d.is_dma:
                            wait(("d", d.dsem), dsem[d.dsem], d.dtarget)
                        else:
                            wait(("e", d.eng), esem[d.eng], d.semval)
                    if op.fn is None:
                        continue
                    if op.is_dma:
                        p = op.prev_on_sem
                        if p is not None:
                            wait(("d", p.dsem), dsem[p.dsem], p.dtarget)
                        nm, a_, k_ = op.fn
                        ins = getattr(eng, nm)(*a_, **k_)
                        ins.then_inc(dsem[op.dsem], 16)
                    else:
                        nm, a_, k_ = op.fn
                        ins = getattr(eng, nm)(*a_, **k_)
                        if op.signal:
                            ins.then_inc(esem[engname], 1)
                if engname == "sp":
                    for (s, v) in dma_final:
                        eng.wait_ge(s, v)

            @block.tensor
            def _(e):
                run("pe", e)

            @block.scalar
            def _(e):
                run("act", e)

            @block.vector
            def _(e):
                run("dve", e)

            @block.gpsimd
            def _(e):
                run("pool", e)

            @block.sync
            def _(e):
                run("sp", e)


class T:
    def __init__(self, t, name):
        self.t = t
        self.b = Buf(name)

    def __getitem__(self, k):
        return self.t[k]


def build_nc(n_tiles=NT, do_sample=True, n_blk_s=8, n_pool=2560, lvl=9):
    nc = bass.Bass("TRN2", target_bir_lowering=False)
    P = Prog(nc)

    def din(name, shape, dt=F32):
        return nc.dram_tensor(name, list(shape), dt, kind="ExternalInput").ap()

    def dout(name, shape, dt=F32):
        return nc.dram_tensor(name, list(shape), dt, kind="ExternalOutput").ap()

    xp = din("xp", [SEQ, 1024]); xs_d = din("xs", [TS, 1024])
    ckv = din("ckv", [n_pool * 128, 1024])
    stc = din("stc", [NS, 30, 512]); sth = din("sth", [NS, 8, 128, 128])
    ptab = din("ptab", [NS * 16], I32)
    w_in_0 = din("w_in_0", [1024, 3584]); w_out_0 = din("w_out_0", [1024, 1024])
    w_in_1 = din("w_in_1", [1024, 4096]); w_out_1 = din("w_out_1", [1024, 1024])
    n0_d = din("n0", [128, 8]); n1_d = din("n1", [128, 8])
    qg_d = din("qg", [512]); kg_d = din("kg", [512]); og_d = din("og", [1024])
    cwT_d = din("cwT", [128, 4 * 31]); cb_d = din("cb", [128, 4]); lng_d = din("lng", [128, 4]); lnb_d = din("lnb", [128, 4])
    lb_d = din("lb", [128, 16])
    identb_d = din("identb", [128, 128], BF16); ident32_d = din("ident32", [128, 128]); ones32_d = din("ones32", [128, 128])
    tri_d = din("tri", [128, 128], BF16); mnew_d = din("mnew", [64, 64], BF16); blkc_d = din("blkc", [128, 128], BF16)
    rm64_d = din("rm64", [128, 512]); rm4_d = din("rm4", [128, 64]); seqm_d = din("seqm", [64, 16])

    y_p = dout("y_p", [SEQ, 1024]); y_s = dout("y_s", [TS, 1024])
    k_p = dout("k_p", [SEQ, 512]); v_p = dout("v_p", [SEQ, 512])
    k_s = dout("k_s", [TS, 512]); v_s = dout("v_s", [TS, 512])
    conv_p = dout("conv_p", [30, 512]); conv_s = dout("conv_s", [NS, 30, 512])
    hg_p = dout("hg_p", [8, 128, 128]); hg_s = dout("hg_s", [NS, 8, 128, 128])
    x1a = nc.dram_tensor("x1a", [SEQ + TS, 1024], F32, kind="ExternalOutput").ap()
    x1b = nc.dram_tensor("x1b", [SEQ + TS, 1024], F32).ap()
    b_x1a = [Buf() for _ in range(NT + 1)]
    b_x1b = [Buf() for _ in range(NT + 1)]
    b_out = Buf("out")

    with contextlib.ExitStack() as st0:
        def mk(st, name, shape, dt=F32):
            return T(st.enter_context(nc.sbuf_tensor("sb_" + name, list(shape), dt)), name)

        def mkp(st, name, shape, dt=F32):
            t = T(st.enter_context(nc.psum_tensor("pp_" + name, list(shape), dt)), name)
            t.b.excl = True
            return t

        ps_tr = mkp(st0, "ps_tr", [128, 8, 128], BF16)
        ps_m = [mkp(st0, "ps_m%d" % i, [128, 512]) for i in range(3)]
        ps_s = [mkp(st0, "ps_s%d" % i, [128, 512]) for i in range(2)]
        ps_o = [mkp(st0, "ps_o%d" % i, [128, 512]) for i in range(2)]
        rr = {"m": 0, "s": 0, "o": 0}

        def next_ps(kind):
            lst = {"m": ps_m, "s": ps_s, "o": ps_o}[kind]
            i = rr[kind]
            rr[kind] = (i + 1) % len(lst)
            return lst[i]

        identb = mk(st0, "identb", [128, 128], BF16); ident32 = mk(st0, "ident32", [128, 128]); ones32 = mk(st0, "ones32", [128, 128])
        tri = mk(st0, "tri", [128, 128], BF16); mnew = mk(st0, "mnew", [64, 64], BF16); blkc = mk(st0, "blkc", [128, 128], BF16)
        rm64 = mk(st0, "rm64", [128, 512]); rm4 = mk(st0, "rm4", [128, 64]); seqm = mk(st0, "seqm", [64, 16])
        n0 = mk(st0, "n0", [128, 8]); n1 = mk(st0, "n1", [128, 8])
        qg = mk(st0, "qg", [128, 512]); kg = mk(st0, "kg", [128, 512]); og = mk(st0, "og", [128, 1024])
        cwT = mk(st0, "cwT", [128, 4 * 31]); cb = mk(st0, "cb", [128, 4]); lng = mk(st0, "lng", [128, 4]); lnb = mk(st0, "lnb", [128, 4])
        lbl = mk(st0, "lbl", [128, 16]); lbv = mk(st0, "lbv", [128, 8]); oml = mk(st0, "oml", [128, 8]); noml = mk(st0, "noml", [128, 8])
        for (t, d) in [(identb, identb_d), (ident32, ident32_d), (ones32, ones32_d), (tri, tri_d), (mnew, mnew_d), (blkc, blkc_d),
                       (rm64, rm64_d), (rm4, rm4_d), (seqm, seqm_d), (n0, n0_d), (n1, n1_d), (cwT, cwT_d), (cb, cb_d),
                       (lng, lng_d), (lnb, lnb_d), (lbl, lb_d)]:
            P.dma("sp", lambda e, t=t, d=d: e.dma_start(out=t[:], in_=d), writes=[t.b])
        for (t, d) in [(qg, qg_d), (kg, kg_d), (og, og_d)]:
            P.dma("sp", lambda e, t=t, d=d: e.dma_start(out=t[:], in_=d.partition_broadcast(128)), writes=[t.b])
        P.op("dve", lambda e: e.tensor_sub(out=lbv[:], in0=lbl[:, 8:16], in1=lbl[:, 0:8]), reads=[lbl.b], writes=[lbv.b])
        P.op("act", lambda e: e.activation(out=lbv[:], in_=lbv[:], func=AF.Sigmoid), reads=[lbv.b], writes=[lbv.b])
        P.op("dve", lambda e: e.tensor_scalar(out=oml[:], in0=lbv[:], scalar1=-1.0, scalar2=1.0, op0=ALU.mult, op1=ALU.add), reads=[lbv.b], writes=[oml.b])
        P.op("dve", lambda e: e.tensor_scalar(out=noml[:], in0=oml[:], scalar1=-1.0, scalar2=None, op0=ALU.mult), reads=[oml.b], writes=[noml.b])

        def load_w(st, dst, wd, row0, nk, col0, ncols, scale, stage):
            i = 0
            for k in range(nk):
                for c0 in range(0, ncols, 2048):
                    cw = min(2048, ncols - c0)
                    sg = stage[i % 2]; i += 1
                    P.dma("sp", lambda e, sg=sg, k=k, c0=c0, cw=cw: e.dma_start(
                        out=sg[:, 0:cw], in_=wd[row0 + k * 128:row0 + (k + 1) * 128, col0 + c0:col0 + c0 + cw]), writes=[sg.b])
                    if scale is None:
                        P.op("act", lambda e, sg=sg, k=k, c0=c0, cw=cw: e.activation(out=dst[:, k, c0:c0 + cw], in_=sg[:, 0:cw], func=AF.Copy),
                             reads=[sg.b], writes=[dst.b])
                    else:
                        P.op("act", lambda e, sg=sg, k=k, c0=c0, cw=cw: e.activation(out=dst[:, k, c0:c0 + cw], in_=sg[:, 0:cw], func=AF.Copy,
                                                                                 scale=scale[:, k:k + 1]),
                             reads=[sg.b, scale.b], writes=[dst.b])

        rms_rr = [0]

        def rms_T(xt, R, xsb, hT, col0, small):
            ssq, rt, rstd = small
            if isinstance(xsb, list):
                rms_rr[0] += 1
                xsb = xsb[rms_rr[0] % len(xsb)]
            P.op("act", lambda e: e.activation(out=xsb[:R, :], in_=xt[:R, :], func=AF.Square, accum_out=ssq[:R, 0:1]),
                 reads=[xt.b], writes=[xsb.b, ssq.b])
            P.op("act", lambda e: e.activation(out=rt[:R, 0:1], in_=ssq[:R, 0:1], func=AF.Sqrt, scale=1.0 / 1024, bias=EPS),
                 reads=[ssq.b], writes=[rt.b])
            P.op("dve", lambda e: e.reciprocal(out=rstd[:R, 0:1], in_=rt[:R, 0:1]), reads=[rt.b], writes=[rstd.b])
            P.op("act", lambda e: e.activation(out=xsb[:R, :], in_=xt[:R, :], func=AF.Copy, scale=rstd[:R, 0:1]),
                 reads=[xt.b, rstd.b], writes=[xsb.b])
            for k in range(8):
                P.op("pe", lambda e, k=k: e.transpose(out=ps_tr[:, k, 0:R], in_=xsb[:R, k * 128:(k + 1) * 128], identity=identb[:R, :R]),
                     reads=[xsb.b, identb.b], writes=[ps_tr.b])
            P.op("dve", lambda e: e.tensor_copy(out=hT[:, :, col0:col0 + R], in_=ps_tr[:, :, 0:R]), reads=[ps_tr.b], writes=[hT.b])

        def proj_tm(ps, R, hT, col0, w, wc0, ncols, nk=8):
            for k in range(nk):
                P.op("pe", lambda e, k=k: e.matmul(ps[:R, 0:ncols], lhsT=hT[:, k, col0:col0 + R], rhs=w[:, k, wc0:wc0 + ncols],
                                                   start=(k == 0), stop=(k == nk - 1)),
                     reads=[hT.b, w.b], writes=[ps.b])

        def proj_fm(ps, N, hT, col0, w, wc0, nk=8):
            for k in range(nk):
                P.op("pe", lambda e, k=k: e.matmul(ps[:, 0:N], lhsT=w[:, k, wc0:wc0 + 128], rhs=hT[:, k, col0:col0 + N],
                                                   start=(k == 0), stop=(k == nk - 1)),
                     reads=[hT.b, w.b], writes=[ps.b])

        def head_norm(ps, R, nh, d, gt, out, small, sq):
            ss, rt, rs = small
            W = nh * d
            P.op("act", lambda e: e.activation(out=sq[:R, 0:W], in_=ps[:R, 0:W], func=AF.Square), reads=[ps.b], writes=[sq.b])
            P.op("dve", lambda e: e.tensor_reduce(out=ss[:R, 0:nh], in_=sq[:R, 0:W].rearrange("p (h d) -> p h d", d=d), axis=AX.X, op=ALU.add),
                 reads=[sq.b], writes=[ss.b])
            P.op("act", lambda e: e.activation(out=rt[:R, 0:nh], in_=ss[:R, 0:nh], func=AF.Sqrt, scale=1.0 / d, bias=EPS),
                 reads=[ss.b], writes=[rt.b])
            P.op("dve", lambda e: e.reciprocal(out=rs[:R, 0:nh], in_=rt[:R, 0:nh]), reads=[rt.b], writes=[rs.b])
            P.op("dve", lambda e: e.tensor_tensor(out=out[:R, 0:W].rearrange("p (h d) -> p h d", d=d),
                                                  in0=ps[:R, 0:W].rearrange("p (h d) -> p h d", d=d),
                                                  in1=rs[:R, 0:nh].unsqueeze(2).to_broadcast([R, nh, d]), op=ALU.mult),
                 reads=[ps.b, rs.b], writes=[out.b])
            P.op("dve", lambda e: e.tensor_tensor(out=out[:R, 0:W], in0=out[:R, 0:W], in1=gt[:R, 0:W], op=ALU.mult),
                 reads=[out.b, gt.b], writes=[out.b])

        def out_proj_res(R, mT, mcol0, nkc, wo, res, ydst):
            for n in range(2):
                ps = next_ps("m")
                for c in range(nkc):
                    P.op("pe", lambda e, c=c, n=n, ps=ps: e.matmul(ps[:R, :], lhsT=mT[:, c, mcol0:mcol0 + R], rhs=wo[:, c, n * 512:(n + 1) * 512],
                                                                   start=(c == 0), stop=(c == nkc - 1)),
                         reads=[mT.b, wo.b], writes=[ps.b])
                P.op("dve", lambda e, n=n, ps=ps: e.tensor_tensor(out=ydst[:R, n * 512:(n + 1) * 512], in0=ps[:R, :], in1=res[:R, n * 512:(n + 1) * 512], op=ALU.add),
                     reads=[ps.b, res.b], writes=[ydst.b])

        small_i = [0]

        def smalls(st, n, w):
            small_i[0] += 1
            return tuple(mk(st, "sm%d_%d" % (small_i[0], j), [128, w]) for j in range(n))

        with contextlib.ExitStack() as stA:
            wA = mk(stA, "wA", [128, 8, 2048], BF16)
            woA = mk(stA, "woA", [128, 4, 1024], BF16)
            xt = [mk(stA, "xtA%d" % i, [128, 1024]) for i in range(2)]
            xsb = mk(stA, "xsbA", [128, 1024], BF16)
            hT = [mk(stA, "hTA%d" % i, [128, 8, 128], BF16) for i in range(2)]
            sq = mk(stA, "sqA", [128, 512])
            qn = mk(stA, "qnA", [128, 512])
            kn = [mk(stA, "knA%d" % i, [128, 512]) for i in range(2)]
            vsb = [mk(stA, "vsbA%d" % i, [128, 512]) for i in range(2)]
            sgt = mk(stA, "sgA", [128, 512])
            QT = mk(stA, "QTA", [128, 4, 128], BF16)
            qT32 = mk(stA, "qT32A", [128, 4, 128])
            x1t = [mk(stA, "x1tA%d" % i, [128, 1024]) for i in range(2)]
            sel = mk(stA, "selA", [128, 17]); sel_default = sel; gm = mk(stA, "gmA", [128, 16]); top8 = mk(stA, "top8A", [128, 8])
            acc = mk(stA, "accA", [128, 65]); tmpO = mk(stA, "tmpOA", [128, 7, 65]); part = mk(stA, "partA", [128, 65]); rl = mk(stA, "rlA", [128, 1])
            oatt = mk(stA, "oattA", [128, 512]); matt = mk(stA, "mattA", [128, 512], BF16); mT = mk(stA, "mTA", [128, 4, 128], BF16)
            sm_rms = smalls(stA, 3, 1); sm_q = smalls(stA, 3, 8); sm_k = smalls(stA, 3, 8)
            with contextlib.ExitStack() as stS:
                stage = [mk(stS, "stgA%d" % i, [128, 2048]) for i in range(2)]
                load_w(stS, wA, w_in_0, 0, 8, 0, 2048, n0, stage)
                load_w(stS, woA, w_out_0, 0, 4, 0, 1024, None, stage)
                P.barrier()

            def qkvg(R, xt_, hT_, kn_, vsb_, kdst, vdst):
                rms_T(xt_, R, xsb, hT_, 0, sm_rms)
                ps = next_ps("m"); proj_tm(ps, R, hT_, 0, wA, 0, 512); head_norm(ps, R, 8, 64, qg, qn, sm_q, sq)
                ps = next_ps("m"); proj_tm(ps, R, hT_, 0, wA, 512, 512); head_norm(ps, R, 8, 64, kg, kn_, sm_k, sq)
                P.dma("pool", lambda e: e.dma_start(out=kdst, in_=kn_[:R, :]), reads=[kn_.b], writes=[])
                ps = next_ps("m"); proj_tm(ps, R, hT_, 0, wA, 1024, 512)
                P.op("act", lambda e, ps=ps: e.activation(out=vsb_[:R, :], in_=ps[:R, :], func=AF.Copy), reads=[ps.b], writes=[vsb_.b])
                P.dma("pool", lambda e: e.dma_start(out=vdst, in_=vsb_[:R, :]), reads=[vsb_.b], writes=[])
                ps = next_ps("m"); proj_tm(ps, R, hT_, 0, wA, 1536, 512)
                P.op("act", lambda e, ps=ps: e.activation(out=sgt[:R, :], in_=ps[:R, :], func=AF.Silu), reads=[ps.b], writes=[sgt.b])
                ps = next_ps("m")
                for c in range(4):
                    P.op("pe", lambda e, c=c, ps=ps: e.transpose(out=ps[:, c * 128:c * 128 + R], in_=qn[:R, c * 128:(c + 1) * 128], identity=ident32[:R, :R]),
                         reads=[qn.b, ident32.b], writes=[ps.b])
                pv = ps[:, :].rearrange("p (c t) -> p c t", t=128)
                P.op("dve", lambda e, pv=pv, ps=ps: e.tensor_copy(out=qT32[:, :, 0:R], in_=pv[:, :, 0:R]), reads=[ps.b], writes=[qT32.b])
                P.op("act", lambda e, pv=pv, ps=ps: e.activation(out=QT[:, :, 0:R], in_=pv[:, :, 0:R], func=AF.Copy), reads=[ps.b], writes=[QT.b])

            def finish_head(R, h, O_list, nsel, sel=None):
                sel = sel_default if sel is None else sel
                first = True
                for (ps, nb, c0) in O_list:
                    ov = ps[:, 0:nb * 65].rearrange("p (b d) -> p b d", d=65)
                    P.op("dve", lambda e, ov=ov, nb=nb, c0=c0: e.tensor_tensor(out=tmpO[:R, 0:nb, :], in0=ov[:R], in1=sel[:R, c0:c0 + nb].unsqueeze(2).to_broadcast([R, nb, 65]), op=ALU.mult),
                         reads=[ps.b, sel.b], writes=[tmpO.b])
                    dst = acc if first else part
                    P.op("dve", lambda e, nb=nb, dst=dst: e.tensor_reduce(out=dst[:R, :], in_=tmpO[:R, 0:nb, :].rearrange("p b d -> p d b"), axis=AX.X, op=ALU.add),
                         reads=[tmpO.b], writes=[dst.b])
                    if not first:
                        P.op("dve", lambda e: e.tensor_add(out=acc[:R, :], in0=acc[:R, :], in1=part[:R, :]), reads=[acc.b, part.b], writes=[acc.b])
                    first = False
                P.op("dve", lambda e: e.reciprocal(out=rl[:R, :], in_=acc[:R, 64:65]), reads=[acc.b], writes=[rl.b])
                P.op("act", lambda e: e.activation(out=oatt[:R, h * 64:(h + 1) * 64], in_=acc[:R, 0:64], func=AF.Copy, scale=rl[:R, 0:1]),
                     reads=[acc.b, rl.b], writes=[oatt.b])

            def att_out(R, xt_, x1t_, dst, bdst):
                P.op("dve", lambda e: e.tensor_tensor(out=matt[:R, :], in0=oatt[:R, :], in1=sgt[:R, :], op=ALU.mult), reads=[oatt.b, sgt.b], writes=[matt.b])
                for c in range(4):
                    P.op("pe", lambda e, c=c: e.transpose(out=ps_tr[:, c, 0:R], in_=matt[:R, c * 128:(c + 1) * 128], identity=identb[:R, :R]),
                         reads=[matt.b, identb.b], writes=[ps_tr.b])
                P.op("dve", lambda e: e.tensor_copy(out=mT[:, :, 0:R], in_=ps_tr[:, 0:4, 0:R]), reads=[ps_tr.b], writes=[mT.b])
                out_proj_res(R, mT, 0, 4, woA, xt_, x1t_)
                P.dma("pool", lambda e: e.dma_start(out=dst, in_=x1t_[:R, :]), reads=[x1t_.b], writes=[bdst])

            CC = 8.0 * 1.3

            with contextlib.ExitStack() as stP:
                KT = mk(stP, "KT", [128, 4, SEQ], BF16)
                kmT = mk(stP, "kmT", [128, 4, 16]); kmp = mk(stP, "kmp", [128, 4])
                Vaug = mk(stP, "Vaug", [128, NT, 8, 65], BF16)
                PT4 = [mk(stP, "PT%d" % i, [128, NT, 128], BF16) for i in range(4)]
                sel4 = [sel] + [mk(stP, "selB%d" % i, [128, 17]) for i in range(3)]
                s4i = [0]
                for t_ in sel4:
                    P.op("pool", lambda e, t_=t_: e.memset(t_[:], 1.0), writes=[t_.b])
                P.op("pool", lambda e: e.memset(gm[:], NEG), writes=[gm.b])
                KTb = [Buf() for _ in range(NT)]; Vb = [Buf() for _ in range(NT)]; kmb = [Buf() for _ in range(16)]
                P.op("pool", lambda e: e.memset(Vaug[:, :, :, 64:65], 1.0), writes=Vb)
                qsets = [(qn, QT, qT32, sgt),
                         (mk(stP, "qnB", [128, 512]), mk(stP, "QTB", [128, 4, 128], BF16), mk(stP, "qT32B", [128, 4, 128]), mk(stP, "sgB", [128, 512]))]

                def use_q(par):
                    nonlocal qn, QT, qT32, sgt
                    qn, QT, qT32, sgt = qsets[par]

                def preamble(t):
                    use_q(t % 2)
                    jb_ = t // 2
                    xt_ = xt[t % 2]; hT_ = hT[t % 2]; kn_ = kn[t % 2]; vsb_ = vsb[t % 2]
                    P.dma("sp", lambda e: e.dma_start(out=xt_[:], in_=xp[t * 128:(t + 1) * 128, :]), writes=[xt_.b])
                    qkvg(128, xt_, hT_, kn_, vsb_, k_p[t * 128:(t + 1) * 128, :], v_p[t * 128:(t + 1) * 128, :])
                    ps = next_ps("m")
                    for c in range(4):
                        P.op("pe", lambda e: e.transpose(out=ps[:, c * 128:(c + 1) * 128], in_=kn_[:, c * 128:(c + 1) * 128], identity=ident32[:, :]),
                             reads=[kn_.b, ident32.b], writes=[ps.b])
                    pv = ps[:, :].rearrange("p (c t) -> p c t", t=128)
                    P.op("act", lambda e: e.activation(out=KT[:, :, t * 128:(t + 1) * 128], in_=pv, func=AF.Copy), reads=[ps.b], writes=[KTb[t]])
                    if t % 2 == 0:
                        P.op("dve", lambda e: e.tensor_reduce(out=kmp[:, :], in_=pv, axis=AX.X, op=ALU.add), reads=[ps.b], writes=[kmp.b])
                    else:
                        P.op("dve", lambda e: e.tensor_reduce(out=kmT[:, :, jb_], in_=pv, axis=AX.X, op=ALU.add), reads=[ps.b], writes=[kmb[jb_]])
                        P.op("dve", lambda e: e.tensor_tensor(out=kmT[:, :, jb_], in0=kmT[:, :, jb_], in1=kmp[:, :], op=ALU.add), reads=[kmb[jb_], kmp.b], writes=[kmb[jb_]])
                        P.op("dve", lambda e: e.tensor_scalar(out=kmT[:, :, jb_], in0=kmT[:, :, jb_], scalar1=1.0 / 256, scalar2=None, op0=ALU.mult), reads=[kmb[jb_]], writes=[kmb[jb_]])
                    P.op("pool", lambda e: e.tensor_copy(out=Vaug[:, t, :, 0:64], in_=vsb_[:, :].rearrange("p (h d) -> p h d", d=64)),
                         reads=[vsb_.b], writes=[Vb[t]])

                if n_tiles > 0:
                    preamble(0)
                for tt in range(n_tiles):
                    jb = tt // 2
                    xt_ = xt[tt % 2]; x1t_ = x1t[tt % 2]
                    nkt = tt + 1
                    use_q(tt % 2)
                    ps_s4 = [ps_s[0], ps_s[1], ps_m[1], ps_m[2]]

                    def st_scores(c):
                        pb = (c % 2) * 2
                        for hl in range(2):
                            r0 = hl * 64
                            sel_ = sel4[pb + hl]
                            if jb >= 1:
                                psg = ps_m[0]
                                P.op("pe", lambda e: e.matmul(psg[:, 0:jb], lhsT=qT32[r0:r0 + 64, c, :], rhs=kmT[r0:r0 + 64, c, 0:jb], start=True, stop=True),
                                     reads=[qT32.b] + kmb[0:jb], writes=[psg.b])
                                P.op("dve", lambda e: e.tensor_copy(out=gm[:, 0:jb], in_=psg[:, 0:jb]), reads=[psg.b], writes=[gm.b])
                                P.op("dve", lambda e: e.max(out=top8[:], in_=gm[:]), reads=[gm.b], writes=[top8.b])
                                P.op("dve", lambda e: e.tensor_scalar(out=sel_[:, 0:jb], in0=gm[:, 0:jb], scalar1=top8[:, 2:3], scalar2=None, op0=ALU.is_ge),
                                     reads=[gm.b, top8.b], writes=[sel_.b])
                        for k0 in range(0, nkt, 4):
                            nk4 = min(4, nkt - k0)
                            pp = [ps_s4[s4i[0] % 4], ps_s4[(s4i[0] + 1) % 4]]
                            s4i[0] += 2
                            for i in range(nk4):
                                kt = k0 + i
                                for hl in range(2):
                                    r0 = hl * 64
                                    P.op("pe", lambda e: e.matmul(pp[hl][:, i * 128:(i + 1) * 128], lhsT=KT[r0:r0 + 64, c, kt * 128:(kt + 1) * 128],
                                                                  rhs=QT[r0:r0 + 64, c, :], start=True, stop=True),
                                         reads=[KTb[kt], QT.b], writes=[pp[hl].b])
                            for hl in range(2):
                                PT_ = PT4[pb + hl]
                                P.op("act", lambda e: e.activation(out=PT_[:, k0:k0 + nk4, :], in_=pp[hl][:, 0:nk4 * 128].rearrange("p (k t) -> p k t", t=128),
                                                                   func=AF.Exp, scale=0.125, bias=-CC),
                                     reads=[pp[hl].b], writes=[PT_.b])
                        for hl in range(2):
                            PT_ = PT4[pb + hl]
                            P.op("pool", lambda e: e.tensor_tensor(out=PT_[:, tt, :], in0=PT_[:, tt, :], in1=tri[:, :], op=ALU.mult),
                                 reads=[PT_.b, tri.b], writes=[PT_.b])

                    def st_pv(c):
                        pb = (c % 2) * 2
                        for hl in range(2):
                            h = 2 * c + hl
                            PT_ = PT4[pb + hl]
                            O_list = []
                            for b0 in range(0, jb + 1, 7):
                                nb = min(7, jb + 1 - b0)
                                pso = next_ps("o")
                                for bi in range(nb):
                                    b = b0 + bi
                                    kts = [kt for kt in (2 * b, 2 * b + 1) if kt <= tt]
                                    for j, kt in enumerate(kts):
                                        P.op("pe", lambda e: e.matmul(pso[:, bi * 65:(bi + 1) * 65], lhsT=PT_[:, kt, :], rhs=Vaug[:, kt, h, :],
                                                                      start=(j == 0), stop=(j == len(kts) - 1)),
                                             reads=[PT_.b, Vb[kt]], writes=[pso.b])
                                O_list.append((pso, nb, b0))
                            finish_head(128, h, O_list, jb + 1, sel4[pb + hl])

                    st_scores(0); st_scores(1); st_pv(0)
                    if tt + 1 < n_tiles:
                        preamble(tt + 1)
                        use_q(tt % 2)
                    st_scores(2); st_pv(1); st_scores(3); st_pv(2); st_pv(3)
                    att_out(128, xt_, x1t_, x1a[tt * 128:(tt + 1) * 128, :], b_x1a[tt])
                use_q(0)
                P.barrier()

            with contextlib.ExitStack() as stQ:
              if do_sample:
                KVpg = [mk(stQ, "KVpg%d" % i, [128, 1024]) for i in range(4)]
                KTs = [mk(stQ, "KTs%d" % i, [128, 4, 128], BF16) for i in range(2)]
                Vsb = [mk(stQ, "Vsb%d" % i, [128, 8, 65], BF16) for i in range(3)]
                PTp = [mk(stQ, "PTp%d" % i, [128, 8, 64], BF16) for i in range(2)]
                ptp_seq = [None, None]
                Osb = mk(stQ, "Osb", [64, 8, 8, 65])
                kms = mk(stQ, "kms", [128, 4, NS * 8]); kmsp = mk(stQ, "kmsp", [128, 4])
                KTn = mk(stQ, "KTn", [128, 4, 64], BF16); Vn = mk(stQ, "Vn", [64, 8, 65], BF16); PTn = mk(stQ, "PTn", [64, 8, 64], BF16)
                ptb = mk(stQ, "ptb", [128, NS * 16], I32); ptf = mk(stQ, "ptf", [128, NS * 16]); iot = mk(stQ, "iot", [128, 1], I32); iotf = mk(stQ, "iotf", [128, 1])
                idxf = mk(stQ, "idxf", [128, NS * 16]); idxi = mk(stQ, "idxi", [128, NS * 16], I32)
                idk = [mk(stQ, "idk%d" % i, [128, 1], I32) for i in range(NS * 16)]
                gs = mk(stQ, "gs", [64, 8, 8]); gsel = mk(stQ, "gsel", [64, 8, 9]); gtmp = mk(stQ, "gtmp", [64, NS, 8])
                for i in range(2):
                    P.op("pool", lambda e, i=i: e.memset(PTp[i][:], 0.0), writes=[PTp[i].b])
                for i in range(3):
                    P.op("pool", lambda e, i=i: e.memset(Vsb[i][:, :, 64:65], 1.0), writes=[Vsb[i].b])
                P.op("pool", lambda e: e.memset(Vn[:, :, 64:65], 1.0), writes=[Vn.b])
                P.op("pool", lambda e: e.memset(gsel[:], 1.0), writes=[gsel.b])
                P.dma("sp", lambda e: e.dma_start(out=ptb[:], in_=ptab.partition_broadcast(128)), writes=[ptb.b])
                P.op("pool", lambda e: e.iota(iot[:], pattern=[[0, 1]], base=0, channel_multiplier=1), writes=[iot.b])
                P.op("dve", lambda e: e.tensor_copy(out=ptf[:], in_=ptb[:]), reads=[ptb.b], writes=[ptf.b])
                P.op("dve", lambda e: e.tensor_copy(out=iotf[:], in_=iot[:]), reads=[iot.b], writes=[iotf.b])
                P.op("dve", lambda e: e.tensor_scalar(out=idxf[:], in0=ptf[:], scalar1=128.0, scalar2=iotf[:, 0:1], op0=ALU.mult, op1=ALU.add), reads=[ptf.b, iotf.b], writes=[idxf.b])
                P.op("dve", lambda e: e.tensor_copy(out=idxi[:], in_=idxf[:]), reads=[idxf.b], writes=[idxi.b])
                for col in range(NS * 16):
                    P.op("dve", lambda e, col=col: e.tensor_copy(out=idk[col][:], in_=idxi[:, col:col + 1]), reads=[idxi.b], writes=[idk[col].b])
                xt_ = xt[0]; hT_ = hT[0]; kn_ = kn[0]; vsb_ = vsb[0]; x1t_ = x1t[0]
                P.dma("sp", lambda e: e.dma_start(out=xt_[0:TS, :], in_=xs_d), writes=[xt_.b])
                qkvg(TS, xt_, hT_, kn_, vsb_, k_s, v_s)
                ps = next_ps("m")
                for c in range(4):
                    P.op("pe", lambda e, c=c, ps=ps: e.transpose(out=ps[:, c * 128:c * 128 + TS], in_=kn_[:TS, c * 128:(c + 1) * 128], identity=ident32[:TS, :TS]),
                         reads=[kn_.b, ident32.b], writes=[ps.b])
                P.op("act", lambda e, ps=ps: e.activation(out=KTn[:, :, :], in_=ps[:, :].rearrange("p (c t) -> p c t", t=128)[:, :, 0:TS], func=AF.Copy), reads=[ps.b], writes=[KTn.b])
                P.op("pool", lambda e: e.tensor_copy(out=Vn[:, :, 0:64], in_=vsb_[:TS, :].rearrange("p (h d) -> p h d", d=64)), reads=[vsb_.b], writes=[Vn.b])
                Qblk = mk(stQ, "Qblk", [128, 4, NS, 8], BF16)
                P.op("pool", lambda e: e.memset(Qblk[:], 0.0), writes=[Qblk.b])
                P.op("dve", lambda e: e.tensor_copy(out=Qblk[0:64, :, :, 0:4], in_=QT[0:64, :, 0:TS].rearrange("p c (n t) -> p c n t", t=4)), reads=[QT.b], writes=[Qblk.b])
                P.op("dve", lambda e: e.tensor_copy(out=Qblk[64:128, :, :, 4:8], in_=QT[64:128, :, 0:TS].rearrange("p c (n t) -> p c n t", t=4)), reads=[QT.b], writes=[Qblk.b])
                pages = [(b, n, pg) for b in range(8) for n in range(NS) for pg in range(2)]
                NPG = len(pages)
                st_ps = {}
                blk_state = {}

                def stage_T(i):
                    b, n, pg = pages[i]
                    col = n * 16 + 2 * b + pg
                    KVp = KVpg[i % 4]; KTs_ = KTs[i % 2]; Vsb_ = Vsb[i % 3]
                    ik = idk[col]
                    P.dma("pool", lambda e: e.indirect_dma_start(out=KVp[:], out_offset=None, in_=ckv, in_offset=bass.IndirectOffsetOnAxis(ap=ik[:, 0:1], axis=0)),
                          reads=[ik.b], writes=[KVp.b])
                    ps = next_ps("m")
                    for c in range(4):
                        P.op("pe", lambda e, c=c: e.transpose(out=ps[:, c * 128:(c + 1) * 128], in_=KVp[:, c * 128:(c + 1) * 128], identity=ident32[:, :]),
                             reads=[KVp.b, ident32.b], writes=[ps.b])
                    pv = ps[:, :].rearrange("p (c t) -> p c t", t=128)
                    P.op("act", lambda e: e.activation(out=KTs_[:, :, :], in_=pv, func=AF.Copy), reads=[ps.b], writes=[KTs_.b])
                    if pg == 0:
                        P.op("dve", lambda e: e.tensor_reduce(out=kmsp[:, :], in_=pv, axis=AX.X, op=ALU.add), reads=[ps.b], writes=[kmsp.b])
                    else:
                        kc = n * 8 + b
                        P.op("dve", lambda e: e.tensor_reduce(out=kms[:, :, kc], in_=pv, axis=AX.X, op=ALU.add), reads=[ps.b], writes=[kms.b])
                        P.op("dve", lambda e: e.tensor_tensor(out=kms[:, :, kc], in0=kms[:, :, kc], in1=kmsp[:, :], op=ALU.add), reads=[kms.b, kmsp.b], writes=[kms.b])
                        P.op("dve", lambda e: e.tensor_scalar(out=kms[:, :, kc], in0=kms[:, :, kc], scalar1=1.0 / 256, scalar2=None, op0=ALU.mult), reads=[kms.b], writes=[kms.b])
                    P.op("dve", lambda e: e.tensor_copy(out=Vsb_[:, :, 0:64], in_=KVp[:, 512:1024].rearrange("p (h d) -> p h d", d=64)), reads=[KVp.b], writes=[Vsb_.b])

                def stage_S(i):
                    b, n, pg = pages[i]
                    KTs_ = KTs[i % 2]; PTp_ = PTp[i % 2]
                    pss = next_ps("s")
                    for c in range(4):
                        P.op("pe", lambda e, c=c: e.matmul(pss[:, c * 8:(c + 1) * 8], lhsT=KTs_[:, c, :], rhs=Qblk[:, c, n, :], start=True, stop=True),
                             reads=[KTs_.b, Qblk.b], writes=[pss.b])
                    ls = ptp_seq[i % 2]
                    if ls is not None and ls != n:
                        P.op("dve", lambda e: e.memset(PTp_[:, :, ls * 4:(ls + 1) * 4], 0.0), writes=[PTp_.b])
                    ptp_seq[i % 2] = n
                    P.op("act", lambda e: e.activation(out=PTp_[:, :, n * 4:(n + 1) * 4], in_=pss[:, 0:32].rearrange("p (h t) -> p h t", t=4),
                                                       func=AF.Exp, scale=0.125, bias=-CC),
                         reads=[pss.b], writes=[PTp_.b])

                def stage_V(i):
                    b, n, pg = pages[i]
                    PTp_ = PTp[i % 2]; Vsb_ = Vsb[i % 3]
                    if b not in blk_state:
                        blk_state[b] = ([next_ps("o"), next_ps("o")], [True, True])
                    pso2, first_mm = blk_state[b]
                    for h in range(8):
                        hb = h // 4; hc = h % 4
                        pso = pso2[hb]
                        P.op("pe", lambda e, pso=pso, hc=hc, h=h, st_=first_mm[hb]: e.matmul(
                            pso[0:TS, hc * 65:(hc + 1) * 65], lhsT=PTp_[:, h, :], rhs=Vsb_[:, h, :], start=st_, stop=False, skip_group_check=True),
                             reads=[PTp_.b, Vsb_.b], writes=[pso.b])
                        first_mm[hb] = False
                    if n == NS - 1 and pg == 1:
                        for hb in range(2):
                            P.op("dve", lambda e, hb=hb: e.tensor_copy(out=Osb[:, b, hb * 4:(hb + 1) * 4, :], in_=pso2[hb][0:TS, 0:260].rearrange("p (h d) -> p h d", d=65)),
                                 reads=[pso2[hb].b], writes=[Osb.b])

                for i in range(NPG + 2):
                    if i < NPG:
                        stage_T(i)
                    if 0 <= i - 1 < NPG:
                        stage_S(i - 1)
                    if 0 <= i - 2 < NPG:
                        stage_V(i - 2)
                for h in range(8):
                    c = h // 2; r0 = (h % 2) * 64
                    pss = next_ps("s")
                    P.op("pe", lambda e, pss=pss, c=c, r0=r0: e.matmul(pss[0:TS, 0:TS], lhsT=KTn[r0:r0 + 64, c, :], rhs=QT[r0:r0 + 64, c, 0:TS], start=True, stop=True),
                         reads=[KTn.b, QT.b], writes=[pss.b])
                    P.op("act", lambda e, pss=pss, h=h: e.activation(out=PTn[:, h, :], in_=pss[0:TS, 0:TS], func=AF.Exp, scale=0.125, bias=-CC), reads=[pss.b], writes=[PTn.b])
                    P.op("pool", lambda e, h=h: e.tensor_tensor(out=PTn[:, h, :], in0=PTn[:, h, :], in1=mnew[:, :], op=ALU.mult), reads=[PTn.b, mnew.b], writes=[PTn.b])
                    pso = next_ps("o")
                    P.op("pe", lambda e, pso=pso, h=h: e.matmul(pso[0:TS, 0:65], lhsT=PTn[:, h, :], rhs=Vn[:, h, :], start=True, stop=True), reads=[PTn.b, Vn.b], writes=[pso.b])
                    psg = next_ps("m")
                    P.op("pe", lambda e, psg=psg, c=c, r0=r0: e.matmul(psg[0:TS, 0:NS * 8], lhsT=qT32[r0:r0 + 64, c, 0:TS], rhs=kms[r0:r0 + 64, c, :], start=True, stop=True),
                         reads=[qT32.b, kms.b], writes=[psg.b])
                    P.op("dve", lambda e, psg=psg: e.tensor_tensor(out=gtmp[:, :, :], in0=psg[0:TS, 0:NS * 8].rearrange("p (n b) -> p n b", b=8),
                                                                  in1=seqm[:, :].unsqueeze(2).to_broadcast([TS, NS, 8]), op=ALU.mult), reads=[psg.b, seqm.b], writes=[gtmp.b])
                    P.op("dve", lambda e, h=h: e.tensor_reduce(out=gs[:, h, :], in_=gtmp[:, :, :].rearrange("p n b -> p b n"), axis=AX.X, op=ALU.add), reads=[gtmp.b], writes=[gs.b])
                    P.op("dve", lambda e, h=h: e.max(out=top8[0:TS, :], in_=gs[:, h, :]), reads=[gs.b], writes=[top8.b])
                    P.op("dve", lambda e, h=h: e.tensor_scalar(out=gsel[:, h, 0:8], in0=gs[:, h, :], scalar1=top8[0:TS, 2:3], scalar2=None, op0=ALU.is_ge), reads=[gs.b, top8.b], writes=[gsel.b])
                    P.op("dve", lambda e, h=h: e.tensor_tensor(out=tmpO[0:TS, 0:7, :], in0=Osb[:, 0:7, h, :], in1=gsel[:, h, 0:7].unsqueeze(2).to_broadcast([TS, 7, 65]), op=ALU.mult),
                         reads=[Osb.b, gsel.b], writes=[tmpO.b])
                    P.op("dve", lambda e: e.tensor_reduce(out=acc[0:TS, :], in_=tmpO[0:TS, 0:7, :].rearrange("p b d -> p d b"), axis=AX.X, op=ALU.add), reads=[tmpO.b], writes=[acc.b])
                    P.op("dve", lambda e, h=h: e.scalar_tensor_tensor(out=acc[0:TS, :], in0=Osb[:, 7, h, :], scalar=gsel[:, h, 7:8], in1=acc[0:TS, :], op0=ALU.mult, op1=ALU.add),
                         reads=[Osb.b, gsel.b, acc.b], writes=[acc.b])
                    P.op("dve", lambda e, pso=pso: e.tensor_tensor(out=acc[0:TS, :], in0=pso[0:TS, 0:65], in1=acc[0:TS, :], op=ALU.add), reads=[pso.b, acc.b], writes=[acc.b])
                    P.op("dve", lambda e: e.reciprocal(out=rl[0:TS, :], in_=acc[0:TS, 64:65]), reads=[acc.b], writes=[rl.b])
                    P.op("dve", lambda e, h=h: e.tensor_scalar(out=oatt[0:TS, h * 64:(h + 1) * 64], in0=acc[0:TS, 0:64], scalar1=rl[0:TS, 0:1], scalar2=None, op0=ALU.mult),
                         reads=[acc.b, rl.b], writes=[oatt.b])
                att_out(TS, xt_, x1t_, x1a[SEQ:SEQ + TS, :], b_x1a[NT])
                P.barrier()
        if lvl >= 6:
          with contextlib.ExitStack() as stB:
            wC = mk(stB, "wC", [128, 8, 1536], BF16)
            woC = mk(stB, "woC", [128, 4, 1024], BF16)
            Dg = mk(stB, "Dg", [128, 124, 128], BF16)
            with contextlib.ExitStack() as stS:
                stage = [mk(stS, "stgB%d" % i, [128, 2048]) for i in range(2)]
                load_w(stS, wC, w_in_0, 0, 8, 2048, 1536, n0, stage)
                load_w(stS, woC, w_out_0, 512, 4, 0, 1024, None, stage)
                P.barrier()
            for i in range(124):
                P.op("dve", lambda e, i=i: e.tensor_scalar(out=Dg[:, i, :], in0=identb[:, :], scalar1=cwT[:, i:i + 1], scalar2=None, op0=ALU.mult),
                     reads=[identb.b, cwT.b], writes=[Dg.b])
            xt = [mk(stB, "xtB%d" % i, [128, 1024]) for i in range(2)]
            xsb = [mk(stB, "xsbB%d" % i, [128, 1024], BF16) for i in range(2)]
            hTs = [mk(stB, "hTB%d" % i, [128, 8, 512], BF16) for i in range(2)]
            hT = hTs[0]
            uT = mk(stB, "uT", [128, 4, 542], BF16)
            sgc = mk(stB, "sgc", [128, 4, 512])
            yT = mk(stB, "yT", [128, 4, 512]); ysq = mk(stB, "ysq", [128, 4, 512])
            sigt = mk(stB, "sigt", [128, 512]); mean = mk(stB, "mean", [128, 512]); msq = mk(stB, "msq", [128, 512]); rstdB = mk(stB, "rstdB", [128, 512])
            t1 = mk(stB, "t1B", [128, 512])
            mTc = mk(stB, "mTc", [128, 4, 512], BF16)
            x1aT = [mk(stB, "x1aT%d" % i, [128, 1024]) for i in range(2)]
            x1T = [mk(stB, "x1T%d" % i, [128, 1024]) for i in range(2)]
            utm = mk(stB, "utm", [128, 512]); sigm = mk(stB, "sigm", [128, 512])
            sm_rms = smalls(stB, 3, 1)
            P.op("pool", lambda e: e.memset(uT[:, :, 0:30], 0.0), writes=[uT.b])

            def u_tokmajor(R, col0):
                ps = next_ps("m"); proj_tm(ps, R, hT, col0, wC, 512, 512)
                P.op("act", lambda e, ps=ps: e.activation(out=sigm[:R, :], in_=ps[:R, :], func=AF.Sigmoid), reads=[ps.b], writes=[sigm.b])
                ps = next_ps("m"); proj_tm(ps, R, hT, col0, wC, 0, 512)
                P.op("dve", lambda e, ps=ps: e.tensor_tensor(out=utm[:R, :], in0=ps[:R, :], in1=sigm[:R, :], op=ALU.mult), reads=[ps.b, sigm.b], writes=[utm.b])

            def ln_gate(N):
                ps1 = next_ps("m"); ps2 = next_ps("m")
                for c in range(4):
                    P.op("pe", lambda e, c=c: e.matmul(ps1[:, 0:N], lhsT=ones32[:, :], rhs=yT[:, c, 0:N], start=(c == 0), stop=(c == 3)), reads=[ones32.b, yT.b], writes=[ps1.b])
                for c in range(4):
                    P.op("pe", lambda e, c=c: e.matmul(ps2[:, 0:N], lhsT=ones32[:, :], rhs=ysq[:, c, 0:N], start=(c == 0), stop=(c == 3)), reads=[ones32.b, ysq.b], writes=[ps2.b])
                P.op("dve", lambda e: e.tensor_scalar(out=mean[:, 0:N], in0=ps1[:, 0:N], scalar1=1.0 / 512, scalar2=None, op0=ALU.mult), reads=[ps1.b], writes=[mean.b])
                P.op("dve", lambda e: e.tensor_tensor(out=msq[:, 0:N], in0=mean[:, 0:N], in1=mean[:, 0:N], op=ALU.mult), reads=[mean.b], writes=[msq.b])
                P.op("dve", lambda e: e.scalar_tensor_tensor(out=msq[:, 0:N], in0=ps2[:, 0:N], scalar=1.0 / 512, in1=msq[:, 0:N], op0=ALU.mult, op1=ALU.subtract),
                     reads=[ps2.b, msq.b], writes=[msq.b])
                P.op("act", lambda e: e.activation(out=msq[:, 0:N], in_=msq[:, 0:N], func=AF.Sqrt, bias=EPS), reads=[msq.b], writes=[msq.b])
                P.op("dve", lambda e: e.reciprocal(out=rstdB[:, 0:N], in_=msq[:, 0:N]), reads=[msq.b], writes=[rstdB.b])
                for c in range(4):
                    P.op("dve", lambda e, c=c: e.tensor_tensor(out=t1[:, 0:N], in0=yT[:, c, 0:N], in1=mean[:, 0:N], op=ALU.subtract), reads=[yT.b, mean.b], writes=[t1.b])
                    P.op("dve", lambda e, c=c: e.tensor_tensor(out=t1[:, 0:N], in0=t1[:, 0:N], in1=rstdB[:, 0:N], op=ALU.mult), reads=[t1.b, rstdB.b], writes=[t1.b])
                    P.op("act", lambda e, c=c: e.activation(out=t1[:, 0:N], in_=t1[:, 0:N], func=AF.Silu, scale=lng[:, c:c + 1], bias=lnb[:, c:c + 1]),
                         reads=[t1.b, lng.b, lnb.b], writes=[t1.b])
                    P.op("dve", lambda e, c=c: e.tensor_tensor(out=mTc[:, c, 0:N], in0=t1[:, 0:N], in1=sgc[:, c, 0:N], op=ALU.mult), reads=[t1.b, sgc.b], writes=[mTc.b])

            n_st = (n_tiles + 3) // 4
            def load_rms_B(ST_):
                for j in range(4):
                    tt = ST_ * 4 + j
                    xt_ = xt[tt % 2]
                    P.dma("sp", lambda e: e.dma_start(out=xt_[:], in_=xp[tt * 128:(tt + 1) * 128, :]), writes=[xt_.b])
                    rms_T(xt_, 128, xsb, hTs[ST_ % 2], j * 128, sm_rms)

            load_rms_B(0)
            for ST in range(n_st):
                hT = hTs[ST % 2]
                for c in range(4):
                    ps = next_ps("m"); proj_fm(ps, 512, hT, 0, wC, 512 + c * 128)
                    P.op("act", lambda e, ps=ps: e.activation(out=sigt[:, :], in_=ps[:, :], func=AF.Sigmoid), reads=[ps.b], writes=[sigt.b])
                    ps = next_ps("m"); proj_fm(ps, 512, hT, 0, wC, c * 128)
                    P.op("dve", lambda e, ps=ps, c=c: e.tensor_tensor(out=uT[:, c, 30:542], in0=ps[:, :], in1=sigt[:, :], op=ALU.mult), reads=[ps.b, sigt.b], writes=[uT.b])
                    ps = next_ps("m"); proj_fm(ps, 512, hT, 0, wC, 1024 + c * 128)
                    P.op("act", lambda e, ps=ps, c=c: e.activation(out=sgc[:, c, :], in_=ps[:, :], func=AF.Silu), reads=[ps.b], writes=[sgc.b])
                for c in range(4):
                    ps = next_ps("m")
                    for j in range(31):
                        P.op("pe", lambda e, ps=ps, c=c, j=j: e.matmul(ps[:, :], lhsT=Dg[:, c * 31 + j, :], rhs=uT[:, c, j:j + 512], start=(j == 0), stop=(j == 30)),
                             reads=[Dg.b, uT.b], writes=[ps.b])
                    P.op("dve", lambda e, ps=ps, c=c: e.tensor_scalar(out=yT[:, c, :], in0=ps[:, :], scalar1=cb[:, c:c + 1], scalar2=None, op0=ALU.add), reads=[ps.b, cb.b], writes=[yT.b])
                    P.op("act", lambda e, c=c: e.activation(out=ysq[:, c, :], in_=yT[:, c, :], func=AF.Square), reads=[yT.b], writes=[ysq.b])
                P.op("pool", lambda e: e.tensor_copy(out=sigt[:, 0:120].rearrange("p (c r) -> p c r", r=30), in_=uT[:, :, 512:542]), reads=[uT.b], writes=[sigt.b])
                P.op("pool", lambda e: e.tensor_copy(out=uT[:, :, 0:30], in_=sigt[:, 0:120].rearrange("p (c r) -> p c r", r=30)), reads=[sigt.b], writes=[uT.b])
                if ST == NT // 4 - 1:
                    u_tokmajor(128, 384)
                    P.dma("pool", lambda e: e.dma_start(out=conv_p, in_=utm[98:128, :]), reads=[utm.b], writes=[])
                if ST + 1 < n_st:
                    load_rms_B(ST + 1)
                ln_gate(512)
                for j in range(4):
                    tt = ST * 4 + j
                    xa = x1aT[tt % 2]; xo = x1T[tt % 2]
                    P.dma("sp", lambda e, xa=xa, tt=tt: e.dma_start(out=xa[:], in_=x1a[tt * 128:(tt + 1) * 128, :]), reads=[b_x1a[tt]], writes=[xa.b])
                    out_proj_res(128, mTc, j * 128, 4, woC, xa, xo)
                    P.dma("pool", lambda e, xo=xo, tt=tt: e.dma_start(out=x1b[tt * 128:(tt + 1) * 128, :], in_=xo[:]), reads=[xo.b], writes=[b_x1b[tt]])
            hT = hTs[0]
            if do_sample:
              with contextlib.ExitStack() as stQ:
                stg = [mk(stQ, "stcg%d" % i, [120, 512]) for i in range(2)]
                upT = mk(stQ, "upT", [128, 4, NS, 34])
                xt_ = xt[0]
                P.dma("sp", lambda e: e.dma_start(out=xt_[0:TS, :], in_=xs_d), writes=[xt_.b])
                rms_T(xt_, TS, xsb, hT, 0, sm_rms)
                stc2 = stc.rearrange("n r c -> (n r) c")
                for g in range(4):
                    sg_ = stg[g % 2]
                    P.dma("sp", lambda e, sg_=sg_, g=g: e.dma_start(out=sg_[:, :], in_=stc2[g * 120:(g + 1) * 120, :]), writes=[sg_.b])
                    ps = next_ps("m")
                    for c in range(4):
                        P.op("pe", lambda e, ps=ps, c=c, sg_=sg_: e.transpose(out=ps[:, c * 120:(c + 1) * 120], in_=sg_[:, c * 128:(c + 1) * 128], identity=ident32[0:120, 0:120]),
                             reads=[sg_.b, ident32.b], writes=[ps.b])
                    for c in range(4):
                        P.op("dve", lambda e, ps=ps, c=c, g=g: e.tensor_copy(out=upT[:, c, g * 4:(g + 1) * 4, 0:30], in_=ps[:, c * 120:(c + 1) * 120].rearrange("p (n r) -> p n r", r=30)),
                             reads=[ps.b], writes=[upT.b])
                P.dma("pool", lambda e: e.dma_start(out=conv_s[:, 0:26, :], in_=stc[:, 4:30, :]), writes=[])
                for c in range(4):
                    ps = next_ps("m"); proj_fm(ps, TS, hT, 0, wC, 512 + c * 128)
                    P.op("act", lambda e, ps=ps: e.activation(out=sigt[:, 0:TS], in_=ps[:, 0:TS], func=AF.Sigmoid), reads=[ps.b], writes=[sigt.b])
                    ps = next_ps("m"); proj_fm(ps, TS, hT, 0, wC, c * 128)
                    P.op("dve", lambda e, ps=ps, c=c: e.tensor_tensor(out=upT[:, c, :, 30:34], in0=ps[:, 0:TS].rearrange("p (n t) -> p n t", t=4),
                                                                    in1=sigt[:, 0:TS].rearrange("p (n t) -> p n t", t=4), op=ALU.mult), reads=[ps.b, sigt.b], writes=[upT.b])
                    ps = next_ps("m"); proj_fm(ps, TS, hT, 0, wC, 1024 + c * 128)
                    P.op("act", lambda e, ps=ps, c=c: e.activation(out=sgc[:, c, 0:TS], in_=ps[:, 0:TS], func=AF.Silu), reads=[ps.b], writes=[sgc.b])
                for c in range(4):
                    yv = yT[:, c, 0:TS].rearrange("p (n t) -> p n t", t=4)
                    P.op("dve", lambda e, c=c, yv=yv: e.tensor_scalar(out=yv, in0=upT[:, c, :, 0:4], scalar1=cwT[:, c * 31:c * 31 + 1], scalar2=cb[:, c:c + 1], op0=ALU.mult, op1=ALU.add),
                         reads=[upT.b, cwT.b, cb.b], writes=[yT.b])
                    for j in range(1, 31):
                        P.op("dve", lambda e, c=c, j=j, yv=yv: e.scalar_tensor_tensor(out=yv, in0=upT[:, c, :, j:j + 4], scalar=cwT[:, c * 31 + j:c * 31 + j + 1], in1=yv, op0=ALU.mult, op1=ALU.add),
                             reads=[upT.b, cwT.b, yT.b], writes=[yT.b])
                    P.op("act", lambda e, c=c: e.activation(out=ysq[:, c, 0:TS], in_=yT[:, c, 0:TS], func=AF.Square), reads=[yT.b], writes=[ysq.b])
                ln_gate(TS)
                xa = x1aT[0]; xo = x1T[0]
                P.dma("sp", lambda e: e.dma_start(out=xa[0:TS, :], in_=x1a[SEQ:SEQ + TS, :]), reads=[b_x1a[NT]], writes=[xa.b])
                out_proj_res(TS, mTc, 0, 4, woC, xa, xo)
                P.dma("pool", lambda e: e.dma_start(out=x1b[SEQ:SEQ + TS, :], in_=xo[0:TS, :]), reads=[xo.b], writes=[b_x1b[NT]])
                u_tokmajor(TS, 0)
                for n in range(NS):
                    P.dma("pool", lambda e, n=n: e.dma_start(out=conv_s[n, 26:30, :], in_=utm[n * 4:(n + 1) * 4, :]), reads=[utm.b], writes=[])
            P.barrier()

        if lvl >= 7:
          with contextlib.ExitStack() as stC:
            w1 = mk(stC, "w1", [128, 8, 4096], BF16)
            wo1 = mk(stC, "wo1", [128, 8, 1024], BF16)
            with contextlib.ExitStack() as stS:
                stage = [mk(stS, "stgC%d" % i, [128, 2048]) for i in range(2)]
                load_w(stS, w1, w_in_1, 0, 8, 0, 4096, n1, stage)
                load_w(stS, wo1, w_out_1, 0, 8, 0, 1024, None, stage)
                P.barrier()
            xt = [mk(stC, "xtC%d" % i, [128, 1024]) for i in range(2)]
            xsb = [mk(stC, "xsbC%d" % i, [128, 1024], BF16) for i in range(2)]
            hT = mk(stC, "hTC", [128, 8, 512], BF16)
            qT = mk(stC, "qTC", [128, 8, 512], BF16); kT = mk(stC, "kTC", [128, 8, 512], BF16)
            dec = mk(stC, "dec", [128, 8, 16])
            sgm = mk(stC, "sgmC", [128, 512]); lf = mk(stC, "lfC", [128, 512]); bcum = mk(stC, "bcumC", [128, 512])
            ep = mk(stC, "epC", [128, 512]); em = mk(stC, "emC", [128, 512]); kk = mk(stC, "kkC", [128, 512]); sqq = mk(stC, "sqqC", [128, 512])
            vtm = mk(stC, "vtm", [128, 4, 1024], BF16); sg1 = mk(stC, "sg1", [128, 4, 1024], BF16)
            ktm = mk(stC, "ktm", [128, 8, 128], BF16)
            S = mk(stC, "S", [128, 8, 128]); Stmp = mk(stC, "Stmp", [128, 8, 128]); Sbf = [mk(stC, "Sbf%d" % i, [128, 8, 128], BF16) for i in range(2)]
            ATm = mk(stC, "ATm", [128, 8, 128], BF16)
            osb = mk(stC, "osb", [128, 1024]); osq = mk(stC, "osq", [128, 512]); mo = mk(stC, "mo", [128, 1024], BF16); mT1 = mk(stC, "mT1", [128, 8, 128], BF16)
            yt = [mk(stC, "ytC0", [128, 1024])] * 2
            sm_rms = smalls(stC, 3, 1); sm_o = smalls(stC, 3, 4)
            bA = [ps_m[0], ps_m[1]]; bO = [ps_m[2], ps_s[0]]; bKV = [ps_s[1], ps_o[0]]

            def gates(N, h, rmask):
                psq = [ps_o[1], ps_o[0], ps_s[1]][h % 3]; psf = [ps_m[0], ps_m[1], ps_m[2], ps_s[0]][h % 4]
                proj_fm(psf, N, hT, 0, w1, 1024 + h * 128)
                P.op("act", lambda e: e.activation(out=sgm[:, 0:N], in_=psf[:, 0:N], func=AF.Sigmoid), reads=[psf.b], writes=[sgm.b])
                P.op("act", lambda e: e.activation(out=lf[:, 0:N], in_=sgm[:, 0:N], func=AF.Identity, scale=oml[:, h:h + 1], bias=lbv[:, h:h + 1]),
                     reads=[sgm.b, oml.b, lbv.b], writes=[lf.b])
                P.op("pool", lambda e: e.tensor_scalar(out=kk[:, 0:N], in0=sgm[:, 0:N], scalar1=noml[:, h:h + 1], scalar2=oml[:, h:h + 1], op0=ALU.mult, op1=ALU.add),
                     reads=[sgm.b, noml.b, oml.b], writes=[kk.b])
                P.op("pool", lambda e: e.tensor_tensor(out=bcum[:, 0:N], in0=lf[:, 0:N], in1=rmask[:, 0:N], op=ALU.mult), reads=[lf.b, rmask.b], writes=[bcum.b])
                P.op("dve", lambda e: e.tensor_tensor_scan(out=ep[:, 0:N], data0=lf[:, 0:N], data1=bcum[:, 0:N], initial=1.0, op0=ALU.mult, op1=ALU.max),
                     reads=[bcum.b, lf.b], writes=[ep.b])
                P.op("dve", lambda e: e.reciprocal(out=em[:, 0:N], in_=ep[:, 0:N]), reads=[ep.b], writes=[em.b])
                proj_fm(psq, N, hT, 0, w1, h * 128)
                P.op("act", lambda e: e.activation(out=sqq[:, 0:N], in_=psq[:, 0:N], func=AF.Silu), reads=[psq.b], writes=[sqq.b])
                P.op("dve", lambda e: e.tensor_tensor(out=qT[:, h, 0:N], in0=sqq[:, 0:N], in1=ep[:, 0:N], op=ALU.mult), reads=[sqq.b, ep.b], writes=[qT.b])
                P.op("pool", lambda e: e.tensor_tensor(out=kT[:, h, 0:N], in0=kk[:, 0:N], in1=em[:, 0:N], op=ALU.mult), reads=[kk.b, em.b], writes=[kT.b])

            def vg_tm(R, col0, j):
                for n in range(2):
                    ps = ps_o[1] if n == 0 else ps_m[0]
                    proj_tm(ps, R, hT, col0, w1, 2048 + n * 512, 512)
                    P.op("act", lambda e, ps=ps, n=n: e.activation(out=vtm[:R, j, n * 512:(n + 1) * 512], in_=ps[:R, :], func=AF.Copy), reads=[ps.b], writes=[vtm.b])
                for n in range(2):
                    ps = ps_m[1] if n == 0 else ps_m[2]
                    proj_tm(ps, R, hT, col0, w1, 3072 + n * 512, 512)
                    P.op("act", lambda e, ps=ps, n=n: e.activation(out=sg1[:R, j, n * 512:(n + 1) * 512], in_=ps[:R, :], func=AF.Silu), reads=[ps.b], writes=[sg1.b])

            def finish_part1(R, j, mo_):
                for hb in range(2):
                    head_norm(bO[hb], R, 4, 128, _og_half[hb], _osb_half[hb], sm_o, osq)
                P.op("dve", lambda e: e.tensor_tensor(out=mo_[:R, :], in0=osb[:R, :], in1=sg1[:R, j, :], op=ALU.mult), reads=[osb.b, sg1.b], writes=[mo_.b])

            def finish_part2(R, mo_, xres, yt_, ydst):
                for k in range(8):
                    P.op("pe", lambda e, k=k: e.transpose(out=ps_tr[:, k, 0:R], in_=mo_[:R, k * 128:(k + 1) * 128], identity=identb[:R, :R]), reads=[mo_.b, identb.b], writes=[ps_tr.b])
                P.op("dve", lambda e: e.tensor_copy(out=mT1[:, :, 0:R], in_=ps_tr[:, :, 0:R]), reads=[ps_tr.b], writes=[mT1.b])
                out_proj_res(R, mT1, 0, 8, wo1, xres, yt_)
                P.dma("pool", lambda e: e.dma_start(out=ydst, in_=yt_[:R, :]), reads=[yt_.b], writes=[])

            def finish_tokens(R, j, xres, yt_, ydst):
                finish_part1(R, j, mo)
                finish_part2(R, mo, xres, yt_, ydst)

            class _V:
                def __init__(self, parent, c0):
                    self.p = parent; self.c0 = c0; self.b = parent.b
                def __getitem__(self, k):
                    r, cs = k
                    return self.p.t[r, self.c0 + (cs.start or 0):self.c0 + cs.stop]
            _og_half = [_V(og, 0), _V(og, 512)]
            _osb_half = [_V(osb, 0), _V(osb, 512)]

            sm64 = mk(stC, "sm64", [128, 512]); sm4 = mk(stC, "sm4", [128, 64])
            P.op("dve", lambda e: e.tensor_scalar(out=sm64[:, :], in0=rm64[:, :], scalar1=-1.0, scalar2=1.0, op0=ALU.mult, op1=ALU.add), reads=[rm64.b], writes=[sm64.b])
            P.op("dve", lambda e: e.tensor_scalar(out=sm4[:, :], in0=rm4[:, :], scalar1=-1.0, scalar2=1.0, op0=ALU.mult, op1=ALU.add), reads=[rm4.b], writes=[sm4.b])
            P.op("pool", lambda e: e.memset(S[:], 0.0), writes=[S.b])
            P.op("pool", lambda e: e.memset(Sbf[0][:], 0.0), writes=[Sbf[0].b])
            sbi = 0
            pend = None
            mo2 = [mo, mk(stC, "mo_b", [128, 1024], BF16)]
            n_st = (n_tiles + 3) // 4
            for ST in range(n_st):
                for j in range(4):
                    tt = ST * 4 + j
                    xt_ = xt[tt % 2]
                    P.dma("sp", lambda e, xt_=xt_, tt=tt: e.dma_start(out=xt_[:], in_=x1b[tt * 128:(tt + 1) * 128, :]), reads=[b_x1b[tt]], writes=[xt_.b])
                    rms_T(xt_, 128, xsb, hT, j * 128, sm_rms)
                for h in range(8):
                    gates(512, h, sm64)
                    P.op("dve", lambda e, h=h: e.tensor_copy(out=dec[:, h, 0:8], in_=ep[:, :].rearrange("p (c t) -> p c t", t=64)[:, :, 63]), reads=[ep.b], writes=[dec.b])
                for j in range(4):
                    vg_tm(128, j * 128, j)
                for j in range(4):
                    tt = ST * 4 + j
                    c0 = j * 128
                    for h in range(8):
                        P.op("pe", lambda e, h=h, c0=c0: e.transpose(out=ps_tr[:, h, :], in_=kT[:, h, c0:c0 + 128], identity=identb[:, :]), reads=[kT.b, identb.b], writes=[ps_tr.b])
                    P.op("act", lambda e: e.activation(out=ktm[:, :, :], in_=ps_tr[:, :, :], func=AF.Copy), reads=[ps_tr.b], writes=[ktm.b])
                    for h in range(8):
                        pa = bA[h // 4]
                        P.op("pe", lambda e, h=h, pa=pa, c0=c0: e.matmul(pa[:, (h % 4) * 128:(h % 4 + 1) * 128], lhsT=kT[:, h, c0:c0 + 128], rhs=qT[:, h, c0:c0 + 128], start=True, stop=True),
                             reads=[kT.b, qT.b], writes=[pa.b])
                    for hb in range(2):
                        P.op("dve", lambda e, hb=hb: e.tensor_tensor(out=ATm[:, hb * 4:(hb + 1) * 4, :], in0=bA[hb][:, :].rearrange("p (h t) -> p h t", t=128),
                                                                  in1=blkc[:, :].unsqueeze(1).to_broadcast([128, 4, 128]), op=ALU.mult), reads=[bA[hb].b, blkc.b], writes=[ATm.b])
                    for h in range(8):
                        po = bO[h // 4]
                        P.op("pe", lambda e, h=h, po=po, j=j: e.matmul(po[:, (h % 4) * 128:(h % 4 + 1) * 128], lhsT=ATm[:, h, :], rhs=vtm[:, j, h * 128:(h + 1) * 128],
                                                                       start=(h % 4 == 0), stop=False, skip_group_check=True), reads=[ATm.b, vtm.b], writes=[po.b])
                    for ch in range(2):
                        r0 = ch * 64
                        cidx = j * 2 + ch
                        Sb = Sbf[sbi % 2]
                        for h in range(8):
                            po = bO[h // 4]
                            P.op("pe", lambda e, h=h, po=po, r0=r0, c0=c0, Sb=Sb: e.matmul(po[r0:r0 + 64, (h % 4) * 128:(h % 4 + 1) * 128], lhsT=qT[:, h, c0 + r0:c0 + r0 + 64], rhs=Sb[:, h, :],
                                                                                     start=False, stop=True, skip_group_check=True), reads=[qT.b, Sb.b], writes=[po.b])
                        for h in range(8):
                            pk = bKV[h // 4]
                            P.op("pe", lambda e, h=h, pk=pk, r0=r0, j=j: e.matmul(pk[:, (h % 4) * 128:(h % 4 + 1) * 128], lhsT=ktm[r0:r0 + 64, h, :], rhs=vtm[r0:r0 + 64, j, h * 128:(h + 1) * 128],
                                                                                 start=True, stop=True), reads=[ktm.b, vtm.b], writes=[pk.b])
                        for hb in range(2):
                            P.op("dve", lambda e, hb=hb: e.tensor_tensor(out=Stmp[:, hb * 4:(hb + 1) * 4, :], in0=bKV[hb][:, :].rearrange("p (h v) -> p h v", v=128), in1=S[:, hb * 4:(hb + 1) * 4, :], op=ALU.add),
                                 reads=[bKV[hb].b, S.b], writes=[Stmp.b])
                        P.op("dve", lambda e, cidx=cidx: e.tensor_tensor(out=S[:, :, :], in0=Stmp[:, :, :], in1=dec[:, :, cidx:cidx + 1].to_broadcast([128, 8, 128]), op=ALU.mult),
                             reads=[Stmp.b, dec.b], writes=[S.b])
                        sbi += 1
                        Sn = Sbf[sbi % 2]
                        P.op("act", lambda e, Sn=Sn: e.activation(out=Sn[:, :, :], in_=S[:, :, :], func=AF.Copy), reads=[S.b], writes=[Sn.b])
                    mo_ = mo2[tt % 2]
                    finish_part1(128, j, mo_)
                    if pend is not None:
                        pend()
                    xr = xt[tt % 2]
                    P.dma("sp", lambda e, xr=xr, tt=tt: e.dma_start(out=xr[:], in_=x1b[tt * 128:(tt + 1) * 128, :]), reads=[b_x1b[tt]], writes=[xr.b])
                    pend = (lambda mo_=mo_, xr=xr, tt=tt: finish_part2(128, mo_, xr, yt[tt % 2], y_p[tt * 128:(tt + 1) * 128, :]))
                    if j == 3:
                        pend(); pend = None
            P.dma("pool", lambda e: e.dma_start(out=hg_p.rearrange("h k v -> k h v"), in_=S[:, :, :]), reads=[S.b], writes=[])
            if do_sample:
              with contextlib.ExitStack() as stQ:
                qpad = [mk(stQ, "qpad%d" % i, [128, 8, 64], BF16) for i in range(2)]
                S0 = [mk(stQ, "S0_%d" % i, [128, 8, 128]) for i in range(2)]
                S0b = Sbf
                class _R:
                    def __init__(self, parent):
                        self.p = parent; self.b = parent.b
                    def __getitem__(self, k):
                        return self.p.t[:, :].rearrange("p (h v) -> p h v", v=128)
                Sn_ = [S, _R(osb)]
                vmk = [mo, mo]
                decs = mk(stQ, "decs", [128, 8, NS])
                xt_ = xt[0]
                P.dma("sp", lambda e: e.dma_start(out=xt_[0:TS, :], in_=x1b[SEQ:SEQ + TS, :]), reads=[b_x1b[NT]], writes=[xt_.b])
                rms_T(xt_, TS, xsb, hT, 0, sm_rms)
                for i in range(2):
                    P.op("pool", lambda e, i=i: e.memset(qpad[i][:], 0.0), writes=[qpad[i].b])
                for h in range(8):
                    gates(TS, h, sm4)
                    P.op("dve", lambda e, h=h: e.tensor_copy(out=decs[:, h, :], in_=ep[:, 0:TS].rearrange("p (n t) -> p n t", t=4)[:, :, 3]), reads=[ep.b], writes=[decs.b])
                vg_tm(TS, 0, 0)
                for h in range(8):
                    P.op("pe", lambda e, h=h: e.transpose(out=ps_tr[0:TS, h, :], in_=kT[:, h, 0:TS], identity=identb[:, :]), reads=[kT.b, identb.b], writes=[ps_tr.b])
                P.op("act", lambda e: e.activation(out=ktm[0:TS, :, :], in_=ps_tr[0:TS, :, :], func=AF.Copy), reads=[ps_tr.b], writes=[ktm.b])
                pa = bA[0]
                for h in range(8):
                    P.op("pe", lambda e, h=h: e.matmul(pa[0:TS, h * 64:(h + 1) * 64], lhsT=kT[:, h, 0:TS], rhs=qT[:, h, 0:TS], start=True, stop=True), reads=[kT.b, qT.b], writes=[pa.b])
                P.op("dve", lambda e: e.tensor_tensor(out=ATm[0:TS, :, 0:TS], in0=pa[0:TS, :].rearrange("p (h t) -> p h t", t=64), in1=mnew[:, :].unsqueeze(1).to_broadcast([TS, 8, TS]), op=ALU.mult),
                     reads=[pa.b, mnew.b], writes=[ATm.b])
                for h in range(8):
                    po = bO[h // 4]
                    P.op("pe", lambda e, h=h, po=po: e.matmul(po[0:TS, (h % 4) * 128:(h % 4 + 1) * 128], lhsT=ATm[0:TS, h, 0:TS], rhs=vtm[0:TS, 0, h * 128:(h + 1) * 128],
                                                              start=(h % 4 == 0), stop=False, skip_group_check=True), reads=[ATm.b, vtm.b], writes=[po.b])
                sth2 = sth.rearrange("n h k v -> n k h v")
                hg2 = hg_s.rearrange("n h k v -> n k h v")
                for n in range(NS):
                    s0 = S0[n % 2]; s0b = S0b[n % 2]; sn = Sn_[n % 2]; vm = vmk[n % 2]; qp = qpad[n % 2]
                    if n >= 2:
                        P.op("pool", lambda e, qp=qp, n=n: e.memset(qp[:, :, (n - 2) * 4:(n - 1) * 4], 0.0), writes=[qp.b])
                    P.op("pool", lambda e, qp=qp, n=n: e.tensor_copy(out=qp[:, :, n * 4:(n + 1) * 4], in_=qT[:, :, n * 4:(n + 1) * 4]), reads=[qT.b], writes=[qp.b])
                    P.dma("sp", lambda e, s0=s0, n=n: e.dma_start(out=s0[:, :, :], in_=sth2[n]), writes=[s0.b])
                    P.op("act", lambda e, s0=s0, s0b=s0b: e.activation(out=s0b[:, :, :], in_=s0[:, :, :], func=AF.Copy), reads=[s0.b], writes=[s0b.b])
                    for h in range(8):
                        po = bO[h // 4]
                        P.op("pe", lambda e, h=h, po=po, n=n, s0b=s0b, qp=qp: e.matmul(po[0:TS, (h % 4) * 128:(h % 4 + 1) * 128], lhsT=qp[:, h, :], rhs=s0b[:, h, :],
                                                                           start=False, stop=True, skip_group_check=True), reads=[qp.b, s0b.b], writes=[po.b])
                    P.op("dve", lambda e, vm=vm, n=n: e.tensor_scalar(out=vm[0:TS, :], in0=vtm[0:TS, 0, :], scalar1=seqm[:, n:n + 1], scalar2=None, op0=ALU.mult), reads=[vtm.b, seqm.b], writes=[vm.b])
                    for h in range(8):
                        pk = bKV[h // 4]
                        P.op("pe", lambda e, h=h, pk=pk, vm=vm: e.matmul(pk[:, (h % 4) * 128:(h % 4 + 1) * 128], lhsT=ktm[0:TS, h, :], rhs=vm[0:TS, h * 128:(h + 1) * 128], start=True, stop=True),
                             reads=[ktm.b, vm.b], writes=[pk.b])
                    for hb in range(2):
                        P.op("dve", lambda e, hb=hb, s0=s0: e.tensor_tensor(out=Stmp[:, hb * 4:(hb + 1) * 4, :], in0=bKV[hb][:, :].rearrange("p (h v) -> p h v", v=128), in1=s0[:, hb * 4:(hb + 1) * 4, :], op=ALU.add),
                             reads=[bKV[hb].b, s0.b], writes=[Stmp.b])
                    P.op("dve", lambda e, sn=sn, n=n: e.tensor_tensor(out=sn[:, :, :], in0=Stmp[:, :, :], in1=decs[:, :, n:n + 1].to_broadcast([128, 8, 128]), op=ALU.mult),
                         reads=[Stmp.b, decs.b], writes=[sn.b])
                    P.dma("pool", lambda e, sn=sn, n=n: e.dma_start(out=hg2[n], in_=sn[:, :, :]), reads=[sn.b], writes=[])
                xr = xt[1]
                P.dma("sp", lambda e: e.dma_start(out=xr[0:TS, :], in_=x1b[SEQ:SEQ + TS, :]), reads=[b_x1b[NT]], writes=[xr.b])
                finish_tokens(TS, 0, xr, yt[0], y_s)
        P.emit()
    return nc


_NC_CACHE = {}


def _consts():
    bf = ml_dtypes.bfloat16
    p = np.arange(128)
    tri = (p[:, None] <= p[None, :]).astype(np.float32)
    q = np.arange(64)
    mnew = ((q[:, None] // 4 == q[None, :] // 4) & (q[:, None] <= q[None, :])).astype(np.float32)
    blkc = ((p[:, None] // 64 == p[None, :] // 64) & (p[:, None] <= p[None, :])).astype(np.float32)
    rm64 = np.ones((128, 512), np.float32); rm64[:, ::64] = 0.0
    rm4 = np.ones((128, 64), np.float32); rm4[:, ::4] = 0.0
    seqm = (q[:, None] // 4 == np.arange(16)[None, :]).astype(np.float32)
    return {
        "identb": np.eye(128, dtype=np.float32).astype(bf), "ident32": np.eye(128, dtype=np.float32),
        "ones32": np.ones((128, 128), np.float32), "tri": tri.astype(bf), "mnew": mnew.astype(bf), "blkc": blkc.astype(bf),
        "rm64": rm64, "rm4": rm4, "seqm": seqm,
    }


def kernel(x_prompt, x_sample, cache_k, cache_v, state_conv, state_hgrn, page_table,
           norm_0, w_in_0, q_norm_0, k_norm_0, conv_w_0, conv_b_0, conv_ln_g_0, conv_ln_b_0, w_out_0,
           norm_1, w_in_1, lb_logits, o_norm_1, w_out_1):
    f = lambda a: np.ascontiguousarray(np.asarray(a, dtype=np.float32))
    if "nc" not in _NC_CACHE:
        _NC_CACHE["nc"] = build_nc()
    nc = _NC_CACHE["nc"]
    x_prompt = f(x_prompt); x_sample = f(x_sample)
    ckv = np.concatenate([f(cache_k).reshape(2560 * 128, 512), f(cache_v).reshape(2560 * 128, 512)], axis=1)
    state_conv = f(state_conv); state_hgrn = f(state_hgrn)
    page_table = np.ascontiguousarray(np.asarray(page_table, dtype=np.int32))
    pk = lambda v, k: np.ascontiguousarray(f(v).reshape(k, 128).T)
    shared = {
        "ckv": ckv,
        "w_in_0": f(w_in_0), "w_out_0": f(w_out_0), "w_in_1": f(w_in_1), "w_out_1": f(w_out_1),
        "n0": pk(norm_0, 8), "n1": pk(norm_1, 8),
        "qg": np.tile(f(q_norm_0), 8), "kg": np.tile(f(k_norm_0), 8), "og": np.tile(f(o_norm_1), 8),
        "cwT": np.ascontiguousarray(f(conv_w_0).T.reshape(4, 128, 31).transpose(1, 0, 2).reshape(128, 124)),
        "cb": pk(conv_b_0, 4), "lng": pk(conv_ln_g_0, 4), "lnb": pk(conv_ln_b_0, 4),
        "lb": np.ascontiguousarray(np.concatenate([pk(f(lb_logits)[0], 8), pk(f(lb_logits)[1], 8)], axis=1)),
    }
    shared.update(_consts())
    in_maps = []
    for c in range(8):
        m = dict(shared)
        m["xp"] = x_prompt[c // 2]
        m["xs"] = x_sample[c * NS:(c + 1) * NS].reshape(TS, 1024)
        m["stc"] = state_conv[c * NS:(c + 1) * NS]
        m["sth"] = state_hgrn[c * NS:(c + 1) * NS]
        m["ptab"] = page_table[c * NS:(c + 1) * NS].reshape(-1)
        in_maps.append(m)
    res = run_bass_kernel_spmd(nc, in_maps, core_ids=list(range(8)))
    R = res.results
    H = SEQ // 2

    def prompt(name, width):
        out = np.empty((4, SEQ, width), np.float32)
        for c in range(8):
            hlf = c % 2
            out[c // 2, hlf * H:(hlf + 1) * H] = R[c][name][hlf * H:(hlf + 1) * H]
        return out

    y_prompt = prompt("y_p", 1024)
    k_prompt = prompt("k_p", 512).reshape(4, SEQ, 8, 64)
    v_prompt = prompt("v_p", 512).reshape(4, SEQ, 8, 64)
    y_sample = np.concatenate([R[c]["y_s"].reshape(NS, 4, 1024) for c in range(8)], axis=0)
    k_sample = np.concatenate([R[c]["k_s"].reshape(NS, 4, 8, 64) for c in range(8)], axis=0)
    v_sample = np.concatenate([R[c]["v_s"].reshape(NS, 4, 8, 64) for c in range(8)], axis=0)
    conv_prompt = np.stack([R[2 * s + 1]["conv_p"] for s in range(4)], axis=0)
    conv_sample = np.concatenate([R[c]["conv_s"] for c in range(8)], axis=0)
    hgrn_prompt = np.stack([R[2 * s + 1]["hg_p"] for s in range(4)], axis=0)
    hgrn_sample = np.concatenate([R[c]["hg_s"] for c in range(8)], axis=0)
    return (y_prompt, y_sample, k_prompt, v_prompt, k_sample, v_sample,
            conv_prompt, conv_sample, hgrn_prompt, hgrn_sample)
```

```python
import contextlib
import numpy as np
import ml_dtypes
import concourse.bass as bass
import concourse.mybir as mybir
from concourse.bass_utils import run_bass_kernel_spmd

F32 = mybir.dt.float32
BF16 = mybir.dt.bfloat16
I32 = mybir.dt.int32
AF = mybir.ActivationFunctionType
ALU = mybir.AluOpType
AX = mybir.AxisListType

N_DMA_SEMS = 40
EPS = 1e-6
SEQ = 4096
NT = SEQ // 128
NS = 16
TS = 64
NEG = -1.0e30


class Buf:
    __slots__ = ("name", "last_w", "readers", "excl")

    def __init__(self, name="", excl=False):
        self.name = name
        self.last_w = None
        self.readers = []
        self.excl = excl


class Op:
    __slots__ = ("eng", "fn", "deps", "is_dma", "signal", "semval", "dsem", "dtarget", "prev_on_sem")

    def __init__(self, eng, fn, is_dma):
        self.eng = eng
        self.fn = fn
        self.deps = []
        self.is_dma = is_dma
        self.signal = False
        self.semval = None
        self.dsem = None
        self.dtarget = None
        self.prev_on_sem = None


class _Rec:
    def __init__(self):
        self.call = None

    def __getattr__(self, name):
        def f(*a, **k):
            self.call = (name, a, k)
            return None
        return f


class Prog:
    ENGS = ("pe", "act", "dve", "pool", "sp")

    def __init__(self, nc):
        self.nc = nc
        self.ops = []
        self.dma_rr = 0
        self.dma_last = [None] * N_DMA_SEMS
        self.dma_val = [0] * N_DMA_SEMS
        self.last_compute = {e: None for e in self.ENGS}

    def _add(self, eng, fn, reads, writes, is_dma):
        rec = _Rec()
        fn(rec)
        assert rec.call is not None
        op = Op(eng, rec.call, is_dma)
        ex = [b for b in reads if b.excl]
        if ex:
            reads = [b for b in reads if not b.excl]
            writes = list(writes) + ex
        deps = {}
        for b in reads:
            if b.last_w is not None:
                deps[id(b.last_w)] = b.last_w
        for b in writes:
            if b.last_w is not None:
                deps[id(b.last_w)] = b.last_w
            for r in b.readers:
                deps[id(r)] = r
        for d in deps.values():
            if d is op:
                continue
            if (not d.is_dma) and d.eng == eng and not is_dma and eng == "pe":
                continue
            op.deps.append(d)
            if not d.is_dma:
                d.signal = True
        for b in reads:
            b.readers.append(op)
        for b in writes:
            b.last_w = op
            b.readers = []
        if is_dma:
            s = self.dma_rr
            self.dma_rr = (self.dma_rr + 1) % N_DMA_SEMS
            op.dsem = s
            op.prev_on_sem = self.dma_last[s]
            self.dma_val[s] += 16
            op.dtarget = self.dma_val[s]
            self.dma_last[s] = op
        else:
            self.last_compute[eng] = op
        self.ops.append(op)
        return op

    def op(self, eng, fn, reads=(), writes=()):
        return self._add(eng, fn, reads, writes, False)

    def dma(self, eng, fn, reads=(), writes=()):
        return self._add(eng, fn, reads, writes, True)

    def barrier(self):
        deps = []
        for e in self.ENGS:
            o = self.last_compute[e]
            if o is not None:
                o.signal = True
                deps.append(o)
        for o in self.dma_last:
            if o is not None:
                deps.append(o)
        for e in self.ENGS:
            op = Op(e, None, False)
            op.deps = [d for d in deps if d.is_dma or d.eng != e]
            self.ops.append(op)

    def emit(self):
        nc = self.nc
        cnt = {e: 0 for e in self.ENGS}
        for op in self.ops:
            if op.fn is not None and not op.is_dma and op.signal:
                cnt[op.eng] += 1
                op.semval = cnt[op.eng]
        with contextlib.ExitStack() as st:
            esem = {e: st.enter_context(nc.semaphore("s_" + e)) for e in self.ENGS}
            dsem = [st.enter_context(nc.semaphore("d%d" % i)) for i in range(N_DMA_SEMS)]
            block = st.enter_context(nc.Block())
            ops = self.ops
            dma_final = [(dsem[i], self.dma_val[i]) for i in range(N_DMA_SEMS) if self.dma_val[i] > 0]

            def run(engname, eng):
                waited = {}

                def wait(key, sem, val):
                    if waited.get(key, 0) >= val:
                        return
                    eng.wait_ge(sem, val)
                    waited[key] = val

                for op in ops:
                    if op.eng != engname:
                        continue
                    for d in op.deps:
                        if d.is_dma:
                            wait(("d", d.dsem), dsem[d.dsem], d.dtarget)
                        else:
                            wait(("e", d.eng), esem[d.eng], d.semval)
                    if op.fn is None:
                        continue
                    if op.is_dma:
                        p = op.prev_on_sem
                        if p is not None:
                            wait(("d", p.dsem), dsem[p.dsem], p.dtarget)
                        nm, a_, k_ = op.fn
                        ins = getattr(eng, nm)(*a_, **k_)
                        ins.then_inc(dsem[op.dsem], 16)
                    else:
                        nm, a_, k_ = op.fn
                        ins = getattr(eng, nm)(*a_, **k_)
                        if op.signal:
                            ins.then_inc(esem[engname], 1)
                if engname == "sp":
                    for (s, v) in dma_final:
                        eng.wait_ge(s, v)

            @block.tensor
            def _(e):
                run("pe", e)

            @block.scalar
            def _(e):
                run("act", e)

            @block.vector
            def _(e):
                run("dve", e)

            @block.gpsimd
            def _(e):
                run("pool", e)

            @block.sync
            def _(e):
                run("sp", e)


class T:
    def __init__(self, t, name):
        self.t = t
        self.b = Buf(name)

    def __getitem__(self, k):
        return self.t[k]


def build_nc(n_tiles=NT, do_sample=True, n_blk_s=8, n_pool=2560, lvl=9):
    nc = bass.Bass("TRN2", target_bir_lowering=False)
    P = Prog(nc)

    def din(name, shape, dt=F32):
        return nc.dram_tensor(name, list(shape), dt, kind="ExternalInput").ap()

    def dout(name, shape, dt=F32):
        return nc.dram_tensor(name, list(shape), dt, kind="ExternalOutput").ap()

    xp = din("xp", [SEQ, 1024]); xs_d = din("xs", [TS, 1024])
    ckv = din("ckv", [n_pool * 128, 1024])
    stc = din("stc", [NS, 30, 512]); sth = din("sth", [NS, 8, 128, 128])
    ptab = din("ptab", [NS * 16], I32)
    w_in_0 = din("w_in_0", [1024, 3584]); w_out_0 = din("w_out_0", [1024, 1024])
    w_in_1 = din("w_in_1", [1024, 4096]); w_out_1 = din("w_out_1", [1024, 1024])
    n0_d = din("n0", [128, 8]); n1_d = din("n1", [128, 8])
    qg_d = din("qg", [512]); kg_d = din("kg", [512]); og_d = din("og", [1024])
    cwT_d = din("cwT", [128, 4 * 31]); cb_d = din("cb", [128, 4]); lng_d = din("lng", [128, 4]); lnb_d = din("lnb", [128, 4])
    lb_d = din("lb", [128, 16])
    identb_d = din("identb", [128, 128], BF16); ident32_d = din("ident32", [128, 128]); ones32_d = din("ones32", [128, 128])
    tri_d = din("tri", [128, 128], BF16); mnew_d = din("mnew", [64, 64], BF16); blkc_d = din("blkc", [128, 128], BF16)
    rm64_d = din("rm64", [128, 512]); rm4_d = din("rm4", [128, 64]); seqm_d = din("seqm", [64, 16])

    y_p = dout("y_p", [SEQ, 1024]); y_s = dout("y_s", [TS, 1024])
    k_p = dout("k_p", [SEQ, 512]); v_p = dout("v_p", [SEQ, 512])
    k_s = dout("k_s", [TS, 512]); v_s = dout("v_s", [TS, 512])
    conv_p = dout("conv_p", [30, 512]); conv_s = dout("conv_s", [NS, 30, 512])
    hg_p = dout("hg_p", [8, 128, 128]); hg_s = dout("hg_s", [NS, 8, 128, 128])
    x1a = nc.dram_tensor("x1a", [SEQ + TS, 1024], F32, kind="ExternalOutput").ap()
    x1b = nc.dram_tensor("x1b", [SEQ + TS, 1024], F32).ap()
    b_x1a = [Buf() for _ in range(NT + 1)]
    b_x1b = [Buf() for _ in range(NT + 1)]
    b_out = Buf("out")

    with contextlib.ExitStack() as st0:
        def mk(st, name, shape, dt=F32):
            return T(st.enter_context(nc.sbuf_tensor("sb_" + name, list(shape), dt)), name)

        def mkp(st, name, shape, dt=F32):
            t = T(st.enter_context(nc.psum_tensor("pp_" + name, list(shape), dt)), name)
            t.b.excl = True
            return t

        ps_tr = mkp(st0, "ps_tr", [128, 8, 128], BF16)
        ps_m = [mkp(st0, "ps_m%d" % i, [128, 512]) for i in range(3)]
        ps_s = [mkp(st0, "ps_s%d" % i, [128, 512]) for i in range(2)]
        ps_o = [mkp(st0, "ps_o%d" % i, [128, 512]) for i in range(2)]
        rr = {"m": 0, "s": 0, "o": 0}

        def next_ps(kind):
            lst = {"m": ps_m, "s": ps_s, "o": ps_o}[kind]
            i = rr[kind]
            rr[kind] = (i + 1) % len(lst)
            return lst[i]

        identb = mk(st0, "identb", [128, 128], BF16); ident32 = mk(st0, "ident32", [128, 128]); ones32 = mk(st0, "ones32", [128, 128])
        tri = mk(st0, "tri", [128, 128], BF16); mnew = mk(st0, "mnew", [64, 64], BF16); blkc = mk(st0, "blkc", [128, 128], BF16)
        rm64 = mk(st0, "rm64", [128, 512]); rm4 = mk(st0, "rm4", [128, 64]); seqm = mk(st0, "seqm", [64, 16])
        n0 = mk(st0, "n0", [128, 8]); n1 = mk(st0, "n1", [128, 8])
        qg = mk(st0, "qg", [128, 512]); kg = mk(st0, "kg", [128, 512]); og = mk(st0, "og", [128, 1024])
        cwT = mk(st0, "cwT", [128, 4 * 31]); cb = mk(st0, "cb", [128, 4]); lng = mk(st0, "lng", [128, 4]); lnb = mk(st0, "lnb", [128, 4])
        lbl = mk(st0, "lbl", [128, 16]); lbv = mk(st0, "lbv", [128, 8]); oml = mk(st0, "oml", [128, 8]); noml = mk(st0, "noml", [128, 8])
        for (t, d) in [(identb, identb_d), (ident32, ident32_d), (ones32, ones32_d), (tri, tri_d), (mnew, mnew_d), (blkc, blkc_d),
                       (rm64, rm64_d), (rm4, rm4_d), (seqm, seqm_d), (n0, n0_d), (n1, n1_d), (cwT, cwT_d), (cb, cb_d),
                       (lng, lng_d), (lnb, lnb_d), (lbl, lb_d)]:
            P.dma("sp", lambda e, t=t, d=d: e.dma_start(out=t[:], in_=d), writes=[t.b])
        for (t, d) in [(qg, qg_d), (kg, kg_d), (og, og_d)]:
            P.dma("sp", lambda e, t=t, d=d: e.dma_start(out=t[:], in_=d.partition_broadcast(128)), writes=[t.b])
        P.op("dve", lambda e: e.tensor_sub(out=lbv[:], in0=lbl[:, 8:16], in1=lbl[:, 0:8]), reads=[lbl.b], writes=[lbv.b])
        P.op("act", lambda e: e.activation(out=lbv[:], in_=lbv[:], func=AF.Sigmoid), reads=[lbv.b], writes=[lbv.b])
        P.op("dve", lambda e: e.tensor_scalar(out=oml[:], in0=lbv[:], scalar1=-1.0, scalar2=1.0, op0=ALU.mult, op1=ALU.add), reads=[lbv.b], writes=[oml.b])
        P.op("dve", lambda e: e.tensor_scalar(out=noml[:], in0=oml[:], scalar1=-1.0, scalar2=None, op0=ALU.mult), reads=[oml.b], writes=[noml.b])

        def load_w(st, dst, wd, row0, nk, col0, ncols, scale, stage):
            i = 0
            for k in range(nk):
                for c0 in range(0, ncols, 2048):
                    cw = min(2048, ncols - c0)
                    sg = stage[i % 2]; i += 1
                    P.dma("sp", lambda e, sg=sg, k=k, c0=c0, cw=cw: e.dma_start(
                        out=sg[:, 0:cw], in_=wd[row0 + k * 128:row0 + (k + 1) * 128, col0 + c0:col0 + c0 + cw]), writes=[sg.b])
                    if scale is None:
                        P.op("act", lambda e, sg=sg, k=k, c0=c0, cw=cw: e.activation(out=dst[:, k, c0:c0 + cw], in_=sg[:, 0:cw], func=AF.Copy),
                             reads=[sg.b], writes=[dst.b])
                    else:
                        P.op("act", lambda e, sg=sg, k=k, c0=c0, cw=cw: e.activation(out=dst[:, k, c0:c0 + cw], in_=sg[:, 0:cw], func=AF.Copy,
                                                                                 scale=scale[:, k:k + 1]),
                             reads=[sg.b, scale.b], writes=[dst.b])

        rms_rr = [0]

        def rms_T(xt, R, xsb, hT, col0, small):
            ssq, rt, rstd = small
            if isinstance(xsb, list):
                rms_rr[0] += 1
                xsb = xsb[rms_rr[0] % len(xsb)]
            P.op("act", lambda e: e.activation(out=xsb[:R, :], in_=xt[:R, :], func=AF.Square, accum_out=ssq[:R, 0:1]),
                 reads=[xt.b], writes=[xsb.b, ssq.b])
            P.op("act", lambda e: e.activation(out=rt[:R, 0:1], in_=ssq[:R, 0:1], func=AF.Sqrt, scale=1.0 / 1024, bias=EPS),
                 reads=[ssq.b], writes=[rt.b])
            P.op("dve", lambda e: e.reciprocal(out=rstd[:R, 0:1], in_=rt[:R, 0:1]), reads=[rt.b], writes=[rstd.b])
            P.op("act", lambda e: e.activation(out=xsb[:R, :], in_=xt[:R, :], func=AF.Copy, scale=rstd[:R, 0:1]),
                 reads=[xt.b, rstd.b], writes=[xsb.b])
            for k in range(8):
                P.op("pe", lambda e, k=k: e.transpose(out=ps_tr[:, k, 0:R], in_=xsb[:R, k * 128:(k + 1) * 128], identity=identb[:R, :R]),
                     reads=[xsb.b, identb.b], writes=[ps_tr.b])
            P.op("dve", lambda e: e.tensor_copy(out=hT[:, :, col0:col0 + R], in_=ps_tr[:, :, 0:R]), reads=[ps_tr.b], writes=[hT.b])

        def proj_tm(ps, R, hT, col0, w, wc0, ncols, nk=8):
            for k in range(nk):
                P.op("pe", lambda e, k=k: e.matmul(ps[:R, 0:ncols], lhsT=hT[:, k, col0:col0 + R], rhs=w[:, k, wc0:wc0 + ncols],
                                                   start=(k == 0), stop=(k == nk - 1)),
                     reads=[hT.b, w.b], writes=[ps.b])

        def proj_fm(ps, N, hT, col0, w, wc0, nk=8):
            for k in range(nk):
                P.op("pe", lambda e, k=k: e.matmul(ps[:, 0:N], lhsT=w[:, k, wc0:wc0 + 128], rhs=hT[:, k, col0:col0 + N],
                                                   start=(k == 0), stop=(k == nk - 1)),
                     reads=[hT.b, w.b], writes=[ps.b])

        def head_norm(ps, R, nh, d, gt, out, small, sq):
            ss, rt, rs = small
            W = nh * d
            P.op("act", lambda e: e.activation(out=sq[:R, 0:W], in_=ps[:R, 0:W], func=AF.Square), reads=[ps.b], writes=[sq.b])
            P.op("dve", lambda e: e.tensor_reduce(out=ss[:R, 0:nh], in_=sq[:R, 0:W].rearrange("p (h d) -> p h d", d=d), axis=AX.X, op=ALU.add),
                 reads=[sq.b], writes=[ss.b])
            P.op("act", lambda e: e.activation(out=rt[:R, 0:nh], in_=ss[:R, 0:nh], func=AF.Sqrt, scale=1.0 / d, bias=EPS),
                 reads=[ss.b], writes=[rt.b])
            P.op("dve", lambda e: e.reciprocal(out=rs[:R, 0:nh], in_=rt[:R, 0:nh]), reads=[rt.b], writes=[rs.b])
            P.op("dve", lambda e: e.tensor_tensor(out=out[:R, 0:W].rearrange("p (h d) -> p h d", d=d),
                                                  in0=ps[:R, 0:W].rearrange("p (h d) -> p h d", d=d),
                                                  in1=rs[:R, 0:nh].unsqueeze(2).to_broadcast([R, nh, d]), op=ALU.mult),
                 reads=[ps.b, rs.b], writes=[out.b])
            P.op("dve", lambda e: e.tensor_tensor(out=out[:R, 0:W], in0=out[:R, 0:W], in1=gt[:R, 0:W], op=ALU.mult),
                 reads=[out.b, gt.b], writes=[out.b])

        def out_proj_res(R, mT, mcol0, nkc, wo, res, ydst):
            for n in range(2):
                ps = next_ps("m")
                for c in range(nkc):
                    P.op("pe", lambda e, c=c, n=n, ps=ps: e.matmul(ps[:R, :], lhsT=mT[:, c, mcol0:mcol0 + R], rhs=wo[:, c, n * 512:(n + 1) * 512],
                                                                   start=(c == 0), stop=(c == nkc - 1)),
                         reads=[mT.b, wo.b], writes=[ps.b])
                P.op("dve", lambda e, n=n, ps=ps: e.tensor_tensor(out=ydst[:R, n * 512:(n + 1) * 512], in0=ps[:R, :], in1=res[:R, n * 512:(n + 1) * 512], op=ALU.add),
                     reads=[ps.b, res.b], writes=[ydst.b])

        small_i = [0]

        def smalls(st, n, w):
            small_i[0] += 1
            return tuple(mk(st, "sm%d_%d" % (small_i[0], j), [128, w]) for j in range(n))

        with contextlib.ExitStack() as stA:
            wA = mk(stA, "wA", [128, 8, 2048], BF16)
            woA = mk(stA, "woA", [128, 4, 1024], BF16)
            xt = [mk(stA, "xtA%d" % i, [128, 1024]) for i in range(2)]
            xsb = mk(stA, "xsbA", [128, 1024], BF16)
            hT = [mk(stA, "hTA%d" % i, [128, 8, 128], BF16) for i in range(2)]
            sq = mk(stA, "sqA", [128, 512])
            qn = mk(stA, "qnA", [128, 512])
            kn = [mk(stA, "knA%d" % i, [128, 512]) for i in range(2)]
            vsb = [mk(stA, "vsbA%d" % i, [128, 512]) for i in range(2)]
            sgt = mk(stA, "sgA", [128, 512])
            QT = mk(stA, "QTA", [128, 4, 128], BF16)
            qT32 = mk(stA, "qT32A", [128, 4, 128])
            x1t = [mk(stA, "x1tA%d" % i, [128, 1024]) for i in range(2)]
            sel = mk(stA, "selA", [128, 17]); sel_default = sel; gm = mk(stA, "gmA", [128, 16]); top8 = mk(stA, "top8A", [128, 8])
            acc = mk(stA, "accA", [128, 65]); tmpO = mk(stA, "tmpOA", [128, 7, 65]); part = mk(stA, "partA", [128, 65]); rl = mk(stA, "rlA", [128, 1])
            oatt = mk(stA, "oattA", [128, 512]); matt = mk(stA, "mattA", [128, 512], BF16); mT = mk(stA, "mTA", [128, 4, 128], BF16)
            sm_rms = smalls(stA, 3, 1); sm_q = smalls(stA, 3, 8); sm_k = smalls(stA, 3, 8)
            with contextlib.ExitStack() as stS:
                stage = [mk(stS, "stgA%d" % i, [128, 2048]) for i in range(2)]
                load_w(stS, wA, w_in_0, 0, 8, 0, 2048, n0, stage)
                load_w(stS, woA, w_out_0, 0, 4, 0, 1024, None, stage)
                P.barrier()

            def qkvg(R, xt_, hT_, kn_, vsb_, kdst, vdst):
                rms_T(xt_, R, xsb, hT_, 0, sm_rms)
                ps = next_ps("m"); proj_tm(ps, R, hT_, 0, wA, 0, 512); head_norm(ps, R, 8, 64, qg, qn, sm_q, sq)
                ps = next_ps("m"); proj_tm(ps, R, hT_, 0, wA, 512, 512); head_norm(ps, R, 8, 64, kg, kn_, sm_k, sq)
                P.dma("pool", lambda e: e.dma_start(out=kdst, in_=kn_[:R, :]), reads=[kn_.b], writes=[])
                ps = next_ps("m"); proj_tm(ps, R, hT_, 0, wA, 1024, 512)
                P.op("act", lambda e, ps=ps: e.activation(out=vsb_[:R, :], in_=ps[:R, :], func=AF.Copy), reads=[ps.b], writes=[vsb_.b])
                P.dma("pool", lambda e: e.dma_start(out=vdst, in_=vsb_[:R, :]), reads=[vsb_.b], writes=[])
                ps = next_ps("m"); proj_tm(ps, R, hT_, 0, wA, 1536, 512)
                P.op("act", lambda e, ps=ps: e.activation(out=sgt[:R, :], in_=ps[:R, :], func=AF.Silu), reads=[ps.b], writes=[sgt.b])
                ps = next_ps("m")
                for c in range(4):
                    P.op("pe", lambda e, c=c, ps=ps: e.transpose(out=ps[:, c * 128:c * 128 + R], in_=qn[:R, c * 128:(c + 1) * 128], identity=ident32[:R, :R]),
                         reads=[qn.b, ident32.b], writes=[ps.b])
                pv = ps[:, :].rearrange("p (c t) -> p c t", t=128)
                P.op("dve", lambda e, pv=pv, ps=ps: e.tensor_copy(out=qT32[:, :, 0:R], in_=pv[:, :, 0:R]), reads=[ps.b], writes=[qT32.b])
                P.op("act", lambda e, pv=pv, ps=ps: e.activation(out=QT[:, :, 0:R], in_=pv[:, :, 0:R], func=AF.Copy), reads=[ps.b], writes=[QT.b])

            def finish_head(R, h, O_list, nsel, sel=None):
                sel = sel_default if sel is None else sel
                first = True
                for (ps, nb, c0) in O_list:
                    ov = ps[:, 0:nb * 65].rearrange("p (b d) -> p b d", d=65)
                    P.op("dve", lambda e, ov=ov, nb=nb, c0=c0: e.tensor_tensor(out=tmpO[:R, 0:nb, :], in0=ov[:R], in1=sel[:R, c0:c0 + nb].unsqueeze(2).to_broadcast([R, nb, 65]), op=ALU.mult),
                         reads=[ps.b, sel.b], writes=[tmpO.b])
                    dst = acc if first else part
                    P.op("dve", lambda e, nb=nb, dst=dst: e.tensor_reduce(out=dst[:R, :], in_=tmpO[:R, 0:nb, :].rearrange("p b d -> p d b"), axis=AX.X, op=ALU.add),
                         reads=[tmpO.b], writes=[dst.b])
                    if not first:
                        P.op("dve", lambda e: e.tensor_add(out=acc[:R, :], in0=acc[:R, :], in1=part[:R, :]), reads=[acc.b, part.b], writes=[acc.b])
                    first = False
                P.op("dve", lambda e: e.reciprocal(out=rl[:R, :], in_=acc[:R, 64:65]), reads=[acc.b], writes=[rl.b])
                P.op("act", lambda e: e.activation(out=oatt[:R, h * 64:(h + 1) * 64], in_=acc[:R, 0:64], func=AF.Copy, scale=rl[:R, 0:1]),
                     reads=[acc.b, rl.b], writes=[oatt.b])

            def att_out(R, xt_, x1t_, dst, bdst):
                P.op("dve", lambda e: e.tensor_tensor(out=matt[:R, :], in0=oatt[:R, :], in1=sgt[:R, :], op=ALU.mult), reads=[oatt.b, sgt.b], writes=[matt.b])
                for c in range(4):
                    P.op("pe", lambda e, c=c: e.transpose(out=ps_tr[:, c, 0:R], in_=matt[:R, c * 128:(c + 1) * 128], identity=identb[:R, :R]),
                         reads=[matt.b, identb.b], writes=[ps_tr.b])
                P.op("dve", lambda e: e.tensor_copy(out=mT[:, :, 0:R], in_=ps_tr[:, 0:4, 0:R]), reads=[ps_tr.b], writes=[mT.b])
                out_proj_res(R, mT, 0, 4, woA, xt_, x1t_)
                P.dma("pool", lambda e: e.dma_start(out=dst, in_=x1t_[:R, :]), reads=[x1t_.b], writes=[bdst])

            CC = 8.0 * 1.3

            with contextlib.ExitStack() as stP:
                KT = mk(stP, "KT", [128, 4, SEQ], BF16)
                kmT = mk(stP, "kmT", [128, 4, 16]); kmp = mk(stP, "kmp", [128, 4])
                Vaug = mk(stP, "Vaug", [128, NT, 8, 65], BF16)
                PT4 = [mk(stP, "PT%d" % i, [128, NT, 128], BF16) for i in range(4)]
                P.op("pool", lambda e: e.memset(Vaug[:, :, :, 64:65], 1.0), writes=[Vaug.b])
                sel4 = [sel] + [mk(stP, "selB%d" % i, [128, 17]) for i in range(3)]
                s4i = [0]
                for t_ in sel4:
                    P.op("pool", lambda e, t_=t_: e.memset(t_[:], 1.0), writes=[t_.b])
                P.op("pool", lambda e: e.memset(gm[:], NEG), writes=[gm.b])
                for tt in range(n_tiles):
                    jb = tt // 2
                    xt_ = xt[tt % 2]; hT_ = hT[tt % 2]; kn_ = kn[tt % 2]; vsb_ = vsb[tt % 2]; x1t_ = x1t[tt % 2]
                    P.dma("sp", lambda e, xt_=xt_, tt=tt: e.dma_start(out=xt_[:], in_=xp[tt * 128:(tt + 1) * 128, :]), writes=[xt_.b])
                    qkvg(128, xt_, hT_, kn_, vsb_, k_p[tt * 128:(tt + 1) * 128, :], v_p[tt * 128:(tt + 1) * 128, :])
                    if lvl < 1:
                        continue
                    ps = next_ps("m")
                    for c in range(4):
                        P.op("pe", lambda e, c=c, ps=ps, kn_=kn_: e.transpose(out=ps[:, c * 128:(c + 1) * 128], in_=kn_[:, c * 128:(c + 1) * 128], identity=ident32[:, :]),
                             reads=[kn_.b, ident32.b], writes=[ps.b])
                    pv = ps[:, :].rearrange("p (c t) -> p c t", t=128)
                    P.op("act", lambda e, pv=pv, tt=tt: e.activation(out=KT[:, :, tt * 128:(tt + 1) * 128], in_=pv, func=AF.Copy), reads=[ps.b], writes=[KT.b])
                    if tt % 2 == 0:
                        P.op("dve", lambda e, pv=pv: e.tensor_reduce(out=kmp[:, :], in_=pv, axis=AX.X, op=ALU.add), reads=[ps.b], writes=[kmp.b])
                    else:
                        P.op("dve", lambda e, pv=pv, jb=jb: e.tensor_reduce(out=kmT[:, :, jb], in_=pv, axis=AX.X, op=ALU.add), reads=[ps.b], writes=[kmT.b])
                        P.op("dve", lambda e, jb=jb: e.tensor_tensor(out=kmT[:, :, jb], in0=kmT[:, :, jb], in1=kmp[:, :], op=ALU.add), reads=[kmT.b, kmp.b], writes=[kmT.b])
                        P.op("dve", lambda e, jb=jb: e.tensor_scalar(out=kmT[:, :, jb], in0=kmT[:, :, jb], scalar1=1.0 / 256, scalar2=None, op0=ALU.mult), reads=[kmT.b], writes=[kmT.b])
                    P.op("pool", lambda e, tt=tt, vsb_=vsb_: e.tensor_copy(out=Vaug[:, tt, :, 0:64], in_=vsb_[:, :].rearrange("p (h d) -> p h d", d=64)),
                         reads=[vsb_.b], writes=[Vaug.b])
                    nkt = tt + 1
                    if lvl < 2:
                        continue
                    ps_s4 = [ps_s[0], ps_s[1], ps_m[1], ps_m[2]]

                    def st_scores(c):
                        pb = (c % 2) * 2
                        for hl in range(2):
                            r0 = hl * 64
                            sel_ = sel4[pb + hl]
                            if jb >= 1:
                                psg = ps_m[0]
                                P.op("pe", lambda e: e.matmul(psg[:, 0:jb], lhsT=qT32[r0:r0 + 64, c, :], rhs=kmT[r0:r0 + 64, c, 0:jb], start=True, stop=True),
                                     reads=[qT32.b, kmT.b], writes=[psg.b])
                                P.op("dve", lambda e: e.tensor_copy(out=gm[:, 0:jb], in_=psg[:, 0:jb]), reads=[psg.b], writes=[gm.b])
                                P.op("dve", lambda e: e.max(out=top8[:], in_=gm[:]), reads=[gm.b], writes=[top8.b])
                                P.op("dve", lambda e: e.tensor_scalar(out=sel_[:, 0:jb], in0=gm[:, 0:jb], scalar1=top8[:, 2:3], scalar2=None, op0=ALU.is_ge),
                                     reads=[gm.b, top8.b], writes=[sel_.b])
                        for k0 in range(0, nkt, 4):
                            nk4 = min(4, nkt - k0)
                            pp = [ps_s4[s4i[0] % 4], ps_s4[(s4i[0] + 1) % 4]]
                            s4i[0] += 2
                            for i in range(nk4):
                                kt = k0 + i
                                for hl in range(2):
                                    r0 = hl * 64
                                    P.op("pe", lambda e: e.matmul(pp[hl][:, i * 128:(i + 1) * 128], lhsT=KT[r0:r0 + 64, c, kt * 128:(kt + 1) * 128],
                                                                  rhs=QT[r0:r0 + 64, c, :], start=True, stop=True),
                                         reads=[KT.b, QT.b], writes=[pp[hl].b])
                            for hl in range(2):
                                PT_ = PT4[pb + hl]
                                P.op("act", lambda e: e.activation(out=PT_[:, k0:k0 + nk4, :], in_=pp[hl][:, 0:nk4 * 128].rearrange("p (k t) -> p k t", t=128),
                                                                   func=AF.Exp, scale=0.125, bias=-CC),
                                     reads=[pp[hl].b], writes=[PT_.b])
                        for hl in range(2):
                            PT_ = PT4[pb + hl]
                            P.op("pool", lambda e: e.tensor_tensor(out=PT_[:, tt, :], in0=PT_[:, tt, :], in1=tri[:, :], op=ALU.mult),
                                 reads=[PT_.b, tri.b], writes=[PT_.b])

                    def st_pv(c):
                        pb = (c % 2) * 2
                        for hl in range(2):
                            h = 2 * c + hl
                            PT_ = PT4[pb + hl]
                            O_list = []
                            for b0 in range(0, jb + 1, 7):
                                nb = min(7, jb + 1 - b0)
                                pso = next_ps("o")
                                for bi in range(nb):
                                    b = b0 + bi
                                    kts = [kt for kt in (2 * b, 2 * b + 1) if kt <= tt]
                                    for j, kt in enumerate(kts):
                                        P.op("pe", lambda e: e.matmul(pso[:, bi * 65:(bi + 1) * 65], lhsT=PT_[:, kt, :], rhs=Vaug[:, kt, h, :],
                                                                      start=(j == 0), stop=(j == len(kts) - 1)),
                                             reads=[PT_.b, Vaug.b], writes=[pso.b])
                                O_list.append((pso, nb, b0))
                            finish_head(128, h, O_list, jb + 1, sel4[pb + hl])

                    st_scores(0)
                    for c in range(4):
                        if c + 1 < 4:
                            st_scores(c + 1)
                        st_pv(c)
                    if lvl >= 5:
                        att_out(128, xt_, x1t_, x1a[tt * 128:(tt + 1) * 128, :], b_x1a[tt])
                P.barrier()

            with contextlib.ExitStack() as stQ:
              if do_sample:
                KVpg = [mk(stQ, "KVpg%d" % i, [128, 1024]) for i in range(4)]
                KTs = [mk(stQ, "KTs%d" % i, [128, 4, 128], BF16) for i in range(2)]
                Vsb = [mk(stQ, "Vsb%d" % i, [128, 8, 65], BF16) for i in range(3)]
                PTp = [mk(stQ, "PTp%d" % i, [128, 8, 64], BF16) for i in range(2)]
                ptp_seq = [None, None]
                Osb = mk(stQ, "Osb", [64, 8, 8, 65])
                kms = mk(stQ, "kms", [128, 4, NS * 8]); kmsp = mk(stQ, "kmsp", [128, 4])
                KTn = mk(stQ, "KTn", [128, 4, 64], BF16); Vn = mk(stQ, "Vn", [64, 8, 65], BF16); PTn = mk(stQ, "PTn", [64, 8, 64], BF16)
                ptb = mk(stQ, "ptb", [128, NS * 16], I32); ptf = mk(stQ, "ptf", [128, NS * 16]); iot = mk(stQ, "iot", [128, 1], I32); iotf = mk(stQ, "iotf", [128, 1])
                idxf = mk(stQ, "idxf", [128, NS * 16]); idxi = mk(stQ, "idxi", [128, NS * 16], I32)
                idk = [mk(stQ, "idk%d" % i, [128, 1], I32) for i in range(NS * 16)]
                gs = mk(stQ, "gs", [64, 8, 8]); gsel = mk(stQ, "gsel", [64, 8, 9]); gtmp = mk(stQ, "gtmp", [64, NS, 8])
                for i in range(2):
                    P.op("pool", lambda e, i=i: e.memset(PTp[i][:], 0.0), writes=[PTp[i].b])
                for i in range(3):
                    P.op("pool", lambda e, i=i: e.memset(Vsb[i][:, :, 64:65], 1.0), writes=[Vsb[i].b])
                P.op("pool", lambda e: e.memset(Vn[:, :, 64:65], 1.0), writes=[Vn.b])
                P.op("pool", lambda e: e.memset(gsel[:], 1.0), writes=[gsel.b])
                P.dma("sp", lambda e: e.dma_start(out=ptb[:], in_=ptab.partition_broadcast(128)), writes=[ptb.b])
                P.op("pool", lambda e: e.iota(iot[:], pattern=[[0, 1]], base=0, channel_multiplier=1), writes=[iot.b])
                P.op("dve", lambda e: e.tensor_copy(out=ptf[:], in_=ptb[:]), reads=[ptb.b], writes=[ptf.b])
                P.op("dve", lambda e: e.tensor_copy(out=iotf[:], in_=iot[:]), reads=[iot.b], writes=[iotf.b])
                P.op("dve", lambda e: e.tensor_scalar(out=idxf[:], in0=ptf[:], scalar1=128.0, scalar2=iotf[:, 0:1], op0=ALU.mult, op1=ALU.add), reads=[ptf.b, iotf.b], writes=[idxf.b])
                P.op("dve", lambda e: e.tensor_copy(out=idxi[:], in_=idxf[:]), reads=[idxf.b], writes=[idxi.b])
                for col in range(NS * 16):
                    P.op("dve", lambda e, col=col: e.tensor_copy(out=idk[col][:], in_=idxi[:, col:col + 1]), reads=[idxi.b], writes=[idk[col].b])
                xt_ = xt[0]; hT_ = hT[0]; kn_ = kn[0]; vsb_ = vsb[0]; x1t_ = x1t[0]
                P.dma("sp", lambda e: e.dma_start(out=xt_[0:TS, :], in_=xs_d), writes=[xt_.b])
                qkvg(TS, xt_, hT_, kn_, vsb_, k_s, v_s)
                ps = next_ps("m")
                for c in range(4):
                    P.op("pe", lambda e, c=c, ps=ps: e.transpose(out=ps[:, c * 128:c * 128 + TS], in_=kn_[:TS, c * 128:(c + 1) * 128], identity=ident32[:TS, :TS]),
                         reads=[kn_.b, ident32.b], writes=[ps.b])
                P.op("act", lambda e, ps=ps: e.activation(out=KTn[:, :, :], in_=ps[:, :].rearrange("p (c t) -> p c t", t=128)[:, :, 0:TS], func=AF.Copy), reads=[ps.b], writes=[KTn.b])
                P.op("pool", lambda e: e.tensor_copy(out=Vn[:, :, 0:64], in_=vsb_[:TS, :].rearrange("p (h d) -> p h d", d=64)), reads=[vsb_.b], writes=[Vn.b])
                Qblk = mk(stQ, "Qblk", [128, 4, NS, 8], BF16)
                P.op("pool", lambda e: e.memset(Qblk[:], 0.0), writes=[Qblk.b])
                P.op("dve", lambda e: e.tensor_copy(out=Qblk[0:64, :, :, 0:4], in_=QT[0:64, :, 0:TS].rearrange("p c (n t) -> p c n t", t=4)), reads=[QT.b], writes=[Qblk.b])
                P.op("dve", lambda e: e.tensor_copy(out=Qblk[64:128, :, :, 4:8], in_=QT[64:128, :, 0:TS].rearrange("p c (n t) -> p c n t", t=4)), reads=[QT.b], writes=[Qblk.b])
                pages = [(b, n, pg) for b in range(8) for n in range(NS) for pg in range(2)]
                NPG = len(pages)
                st_ps = {}
                blk_state = {}

                def stage_T(i):
                    b, n, pg = pages[i]
                    col = n * 16 + 2 * b + pg
                    KVp = KVpg[i % 4]; KTs_ = KTs[i % 2]; Vsb_ = Vsb[i % 3]
                    ik = idk[col]
                    P.dma("pool", lambda e: e.indirect_dma_start(out=KVp[:], out_offset=None, in_=ckv, in_offset=bass.IndirectOffsetOnAxis(ap=ik[:, 0:1], axis=0)),
                          reads=[ik.b], writes=[KVp.b])
                    ps = next_ps("m")
                    for c in range(4):
                        P.op("pe", lambda e, c=c: e.transpose(out=ps[:, c * 128:(c + 1) * 128], in_=KVp[:, c * 128:(c + 1) * 128], identity=ident32[:, :]),
                             reads=[KVp.b, ident32.b], writes=[ps.b])
                    pv = ps[:, :].rearrange("p (c t) -> p c t", t=128)
                    P.op("act", lambda e: e.activation(out=KTs_[:, :, :], in_=pv, func=AF.Copy), reads=[ps.b], writes=[KTs_.b])
                    if pg == 0:
                        P.op("dve", lambda e: e.tensor_reduce(out=kmsp[:, :], in_=pv, axis=AX.X, op=ALU.add), reads=[ps.b], writes=[kmsp.b])
                    else:
                        kc = n * 8 + b
                        P.op("dve", lambda e: e.tensor_reduce(out=kms[:, :, kc], in_=pv, axis=AX.X, op=ALU.add), reads=[ps.b], writes=[kms.b])
                        P.op("dve", lambda e: e.tensor_tensor(out=kms[:, :, kc], in0=kms[:, :, kc], in1=kmsp[:, :], op=ALU.add), reads=[kms.b, kmsp.b], writes=[kms.b])
                        P.op("dve", lambda e: e.tensor_scalar(out=kms[:, :, kc], in0=kms[:, :, kc], scalar1=1.0 / 256, scalar2=None, op0=ALU.mult), reads=[kms.b], writes=[kms.b])
                    P.op("dve", lambda e: e.tensor_copy(out=Vsb_[:, :, 0:64], in_=KVp[:, 512:1024].rearrange("p (h d) -> p h d", d=64)), reads=[KVp.b], writes=[Vsb_.b])

                def stage_S(i):
                    b, n, pg = pages[i]
                    KTs_ = KTs[i % 2]; PTp_ = PTp[i % 2]
                    pss = next_ps("s")
                    for c in range(4):
                        P.op("pe", lambda e, c=c: e.matmul(pss[:, c * 8:(c + 1) * 8], lhsT=KTs_[:, c, :], rhs=Qblk[:, c, n, :], start=True, stop=True),
                             reads=[KTs_.b, Qblk.b], writes=[pss.b])
                    ls = ptp_seq[i % 2]
                    if ls is not None and ls != n:
                        P.op("dve", lambda e: e.memset(PTp_[:, :, ls * 4:(ls + 1) * 4], 0.0), writes=[PTp_.b])
                    ptp_seq[i % 2] = n
                    P.op("act", lambda e: e.activation(out=PTp_[:, :, n * 4:(n + 1) * 4], in_=pss[:, 0:32].rearrange("p (h t) -> p h t", t=4),
                                                       func=AF.Exp, scale=0.125, bias=-CC),
                         reads=[pss.b], writes=[PTp_.b])

                def stage_V(i):
                    b, n, pg = pages[i]
                    PTp_ = PTp[i % 2]; Vsb_ = Vsb[i % 3]
                    if b not in blk_state:
                        blk_state[b] = ([next_ps("o"), next_ps("o")], [True, True])
                    pso2, first_mm = blk_state[b]
                    for h in range(8):
                        hb = h // 4; hc = h % 4
                        pso = pso2[hb]
                        P.op("pe", lambda e, pso=pso, hc=hc, h=h, st_=first_mm[hb]: e.matmul(
                            pso[0:TS, hc * 65:(hc + 1) * 65], lhsT=PTp_[:, h, :], rhs=Vsb_[:, h, :], start=st_, stop=False, skip_group_check=True),
                             reads=[PTp_.b, Vsb_.b], writes=[pso.b])
                        first_mm[hb] = False
                    if n == NS - 1 and pg == 1:
                        for hb in range(2):
                            P.op("dve", lambda e, hb=hb: e.tensor_copy(out=Osb[:, b, hb * 4:(hb + 1) * 4, :], in_=pso2[hb][0:TS, 0:260].rearrange("p (h d) -> p h d", d=65)),
                                 reads=[pso2[hb].b], writes=[Osb.b])

                for i in range(NPG + 2):
                    if i < NPG:
                        stage_T(i)
                    if 0 <= i - 1 < NPG:
                        stage_S(i - 1)
                    if 0 <= i - 2 < NPG:
                        stage_V(i - 2)
                for h in range(8):
                    c = h // 2; r0 = (h % 2) * 64
                    pss = next_ps("s")
                    P.op("pe", lambda e, pss=pss, c=c, r0=r0: e.matmul(pss[0:TS, 0:TS], lhsT=KTn[r0:r0 + 64, c, :], rhs=QT[r0:r0 + 64, c, 0:TS], start=True, stop=True),
                         reads=[KTn.b, QT.b], writes=[pss.b])
                    P.op("act", lambda e, pss=pss, h=h: e.activation(out=PTn[:, h, :], in_=pss[0:TS, 0:TS], func=AF.Exp, scale=0.125, bias=-CC), reads=[pss.b], writes=[PTn.b])
                    P.op("pool", lambda e, h=h: e.tensor_tensor(out=PTn[:, h, :], in0=PTn[:, h, :], in1=mnew[:, :], op=ALU.mult), reads=[PTn.b, mnew.b], writes=[PTn.b])
                    pso = next_ps("o")
                    P.op("pe", lambda e, pso=pso, h=h: e.matmul(pso[0:TS, 0:65], lhsT=PTn[:, h, :], rhs=Vn[:, h, :], start=True, stop=True), reads=[PTn.b, Vn.b], writes=[pso.b])
                    psg = next_ps("m")
                    P.op("pe", lambda e, psg=psg, c=c, r0=r0: e.matmul(psg[0:TS, 0:NS * 8], lhsT=qT32[r0:r0 + 64, c, 0:TS], rhs=kms[r0:r0 + 64, c, :], start=True, stop=True),
                         reads=[qT32.b, kms.b], writes=[psg.b])
                    P.op("dve", lambda e, psg=psg: e.tensor_tensor(out=gtmp[:, :, :], in0=psg[0:TS, 0:NS * 8].rearrange("p (n b) -> p n b", b=8),
                                                                  in1=seqm[:, :].unsqueeze(2).to_broadcast([TS, NS, 8]), op=ALU.mult), reads=[psg.b, seqm.b], writes=[gtmp.b])
                    P.op("dve", lambda e, h=h: e.tensor_reduce(out=gs[:, h, :], in_=gtmp[:, :, :].rearrange("p n b -> p b n"), axis=AX.X, op=ALU.add), reads=[gtmp.b], writes=[gs.b])
                    P.op("dve", lambda e, h=h: e.max(out=top8[0:TS, :], in_=gs[:, h, :]), reads=[gs.b], writes=[top8.b])
                    P.op("dve", lambda e, h=h: e.tensor_scalar(out=gsel[:, h, 0:8], in0=gs[:, h, :], scalar1=top8[0:TS, 2:3], scalar2=None, op0=ALU.is_ge), reads=[gs.b, top8.b], writes=[gsel.b])
                    P.op("dve", lambda e, h=h: e.tensor_tensor(out=tmpO[0:TS, 0:7, :], in0=Osb[:, 0:7, h, :], in1=gsel[:, h, 0:7].unsqueeze(2).to_broadcast([TS, 7, 65]), op=ALU.mult),
                         reads=[Osb.b, gsel.b], writes=[tmpO.b])
                    P.op("dve", lambda e: e.tensor_reduce(out=acc[0:TS, :], in_=tmpO[0:TS, 0:7, :].rearrange("p b d -> p d b"), axis=AX.X, op=ALU.add), reads=[tmpO.b], writes=[acc.b])
                    P.op("dve", lambda e, h=h: e.scalar_tensor_tensor(out=acc[0:TS, :], in0=Osb[:, 7, h, :], scalar=gsel[:, h, 7:8], in1=acc[0:TS, :], op0=ALU.mult, op1=ALU.add),
                         reads=[Osb.b, gsel.b, acc.b], writes=[acc.b])
                    P.op("dve", lambda e, pso=pso: e.tensor_tensor(out=acc[0:TS, :], in0=pso[0:TS, 0:65], in1=acc[0:TS, :], op=ALU.add), reads=[pso.b, acc.b], writes=[acc.b])
                    P.op("dve", lambda e: e.reciprocal(out=rl[0:TS, :], in_=acc[0:TS, 64:65]), reads=[acc.b], writes=[rl.b])
                    P.op("dve", lambda e, h=h: e.tensor_scalar(out=oatt[0:TS, h * 64:(h + 1) * 64], in0=acc[0:TS, 0:64], scalar1=rl[0:TS, 0:1], scalar2=None, op0=ALU.mult),
                         reads=[acc.b, rl.b], writes=[oatt.b])
                att_out(TS, xt_, x1t_, x1a[SEQ:SEQ + TS, :], b_x1a[NT])
                P.barrier()
        if lvl >= 6:
          with contextlib.ExitStack() as stB:
            wC = mk(stB, "wC", [128, 8, 1536], BF16)
            woC = mk(stB, "woC", [128, 4, 1024], BF16)
            Dg = mk(stB, "Dg", [128, 124, 128], BF16)
            with contextlib.ExitStack() as stS:
                stage = [mk(stS, "stgB%d" % i, [128, 2048]) for i in range(2)]
                load_w(stS, wC, w_in_0, 0, 8, 2048, 1536, n0, stage)
                load_w(stS, woC, w_out_0, 512, 4, 0, 1024, None, stage)
                P.barrier()
            for i in range(124):
                P.op("dve", lambda e, i=i: e.tensor_scalar(out=Dg[:, i, :], in0=identb[:, :], scalar1=cwT[:, i:i + 1], scalar2=None, op0=ALU.mult),
                     reads=[identb.b, cwT.b], writes=[Dg.b])
            xt = [mk(stB, "xtB%d" % i, [128, 1024]) for i in range(2)]
            xsb = [mk(stB, "xsbB%d" % i, [128, 1024], BF16) for i in range(2)]
            hTs = [mk(stB, "hTB%d" % i, [128, 8, 512], BF16) for i in range(2)]
            hT = hTs[0]
            uT = mk(stB, "uT", [128, 4, 542], BF16)
            sgc = mk(stB, "sgc", [128, 4, 512])
            yT = mk(stB, "yT", [128, 4, 512]); ysq = mk(stB, "ysq", [128, 4, 512])
            sigt = mk(stB, "sigt", [128, 512]); mean = mk(stB, "mean", [128, 512]); msq = mk(stB, "msq", [128, 512]); rstdB = mk(stB, "rstdB", [128, 512])
            t1 = mk(stB, "t1B", [128, 512])
            mTc = mk(stB, "mTc", [128, 4, 512], BF16)
            x1aT = [mk(stB, "x1aT%d" % i, [128, 1024]) for i in range(2)]
            x1T = [mk(stB, "x1T%d" % i, [128, 1024]) for i in range(2)]
            utm = mk(stB, "utm", [128, 512]); sigm = mk(stB, "sigm", [128, 512])
            sm_rms = smalls(stB, 3, 1)
            P.op("pool", lambda e: e.memset(uT[:, :, 0:30], 0.0), writes=[uT.b])

            def u_tokmajor(R, col0):
                ps = next_ps("m"); proj_tm(ps, R, hT, col0, wC, 512, 512)
                P.op("act", lambda e, ps=ps: e.activation(out=sigm[:R, :], in_=ps[:R, :], func=AF.Sigmoid), reads=[ps.b], writes=[sigm.b])
                ps = next_ps("m"); proj_tm(ps, R, hT, col0, wC, 0, 512)
                P.op("dve", lambda e, ps=ps: e.tensor_tensor(out=utm[:R, :], in0=ps[:R, :], in1=sigm[:R, :], op=ALU.mult), reads=[ps.b, sigm.b], writes=[utm.b])

            def ln_gate(N):
                ps1 = next_ps("m"); ps2 = next_ps("m")
                for c in range(4):
                    P.op("pe", lambda e, c=c: e.matmul(ps1[:, 0:N], lhsT=ones32[:, :], rhs=yT[:, c, 0:N], start=(c == 0), stop=(c == 3)), reads=[ones32.b, yT.b], writes=[ps1.b])
                for c in range(4):
                    P.op("pe", lambda e, c=c: e.matmul(ps2[:, 0:N], lhsT=ones32[:, :], rhs=ysq[:, c, 0:N], start=(c == 0), stop=(c == 3)), reads=[ones32.b, ysq.b], writes=[ps2.b])
                P.op("dve", lambda e: e.tensor_scalar(out=mean[:, 0:N], in0=ps1[:, 0:N], scalar1=1.0 / 512, scalar2=None, op0=ALU.mult), reads=[ps1.b], writes=[mean.b])
                P.op("dve", lambda e: e.tensor_tensor(out=msq[:, 0:N], in0=mean[:, 0:N], in1=mean[:, 0:N], op=ALU.mult), reads=[mean.b], writes=[msq.b])
                P.op("dve", lambda e: e.scalar_tensor_tensor(out=msq[:, 0:N], in0=ps2[:, 0:N], scalar=1.0 / 512, in1=msq[:, 0:N], op0=ALU.mult, op1=ALU.subtract),
                     reads=[ps2.b, msq.b], writes=[msq.b])
                P.op("act", lambda e: e.activation(out=msq[:, 0:N], in_=msq[:, 0:N], func=AF.Sqrt, bias=EPS), reads=[msq.b], writes=[msq.b])
                P.op("dve", lambda e: e.reciprocal(out=rstdB[:, 0:N], in_=msq[:, 0:N]), reads=[msq.b], writes=[rstdB.b])
                for c in range(4):
                    P.op("dve", lambda e, c=c: e.tensor_tensor(out=t1[:, 0:N], in0=yT[:, c, 0:N], in1=mean[:, 0:N], op=ALU.subtract), reads=[yT.b, mean.b], writes=[t1.b])
                    P.op("dve", lambda e, c=c: e.tensor_tensor(out=t1[:, 0:N], in0=t1[:, 0:N], in1=rstdB[:, 0:N], op=ALU.mult), reads=[t1.b, rstdB.b], writes=[t1.b])
                    P.op("act", lambda e, c=c: e.activation(out=t1[:, 0:N], in_=t1[:, 0:N], func=AF.Silu, scale=lng[:, c:c + 1], bias=lnb[:, c:c + 1]),
                         reads=[t1.b, lng.b, lnb.b], writes=[t1.b])
                    P.op("dve", lambda e, c=c: e.tensor_tensor(out=mTc[:, c, 0:N], in0=t1[:, 0:N], in1=sgc[:, c, 0:N], op=ALU.mult), reads=[t1.b, sgc.b], writes=[mTc.b])

            n_st = (n_tiles + 3) // 4
            def load_rms_B(ST_):
                for j in range(4):
                    tt = ST_ * 4 + j
                    xt_ = xt[tt % 2]
                    P.dma("sp", lambda e: e.dma_start(out=xt_[:], in_=xp[tt * 128:(tt + 1) * 128, :]), writes=[xt_.b])
                    rms_T(xt_, 128, xsb, hTs[ST_ % 2], j * 128, sm_rms)

            load_rms_B(0)
            for ST in range(n_st):
                hT = hTs[ST % 2]
                for c in range(4):
                    ps = next_ps("m"); proj_fm(ps, 512, hT, 0, wC, 512 + c * 128)
                    P.op("act", lambda e, ps=ps: e.activation(out=sigt[:, :], in_=ps[:, :], func=AF.Sigmoid), reads=[ps.b], writes=[sigt.b])
                    ps = next_ps("m"); proj_fm(ps, 512, hT, 0, wC, c * 128)
                    P.op("dve", lambda e, ps=ps, c=c: e.tensor_tensor(out=uT[:, c, 30:542], in0=ps[:, :], in1=sigt[:, :], op=ALU.mult), reads=[ps.b, sigt.b], writes=[uT.b])
                    ps = next_ps("m"); proj_fm(ps, 512, hT, 0, wC, 1024 + c * 128)
                    P.op("act", lambda e, ps=ps, c=c: e.activation(out=sgc[:, c, :], in_=ps[:, :], func=AF.Silu), reads=[ps.b], writes=[sgc.b])
                for c in range(4):
                    ps = next_ps("m")
                    for j in range(31):
                        P.op("pe", lambda e, ps=ps, c=c, j=j: e.matmul(ps[:, :], lhsT=Dg[:, c * 31 + j, :], rhs=uT[:, c, j:j + 512], start=(j == 0), stop=(j == 30)),
                             reads=[Dg.b, uT.b], writes=[ps.b])
                    P.op("dve", lambda e, ps=ps, c=c: e.tensor_scalar(out=yT[:, c, :], in0=ps[:, :], scalar1=cb[:, c:c + 1], scalar2=None, op0=ALU.add), reads=[ps.b, cb.b], writes=[yT.b])
                    P.op("act", lambda e, c=c: e.activation(out=ysq[:, c, :], in_=yT[:, c, :], func=AF.Square), reads=[yT.b], writes=[ysq.b])
                P.op("pool", lambda e: e.tensor_copy(out=sigt[:, 0:120].rearrange("p (c r) -> p c r", r=30), in_=uT[:, :, 512:542]), reads=[uT.b], writes=[sigt.b])
                P.op("pool", lambda e: e.tensor_copy(out=uT[:, :, 0:30], in_=sigt[:, 0:120].rearrange("p (c r) -> p c r", r=30)), reads=[sigt.b], writes=[uT.b])
                if ST == NT // 4 - 1:
                    u_tokmajor(128, 384)
                    P.dma("pool", lambda e: e.dma_start(out=conv_p, in_=utm[98:128, :]), reads=[utm.b], writes=[])
                if ST + 1 < n_st:
                    load_rms_B(ST + 1)
                ln_gate(512)
                for j in range(4):
                    tt = ST * 4 + j
                    xa = x1aT[tt % 2]; xo = x1T[tt % 2]
                    P.dma("sp", lambda e, xa=xa, tt=tt: e.dma_start(out=xa[:], in_=x1a[tt * 128:(tt + 1) * 128, :]), reads=[b_x1a[tt]], writes=[xa.b])
                    out_proj_res(128, mTc, j * 128, 4, woC, xa, xo)
                    P.dma("pool", lambda e, xo=xo, tt=tt: e.dma_start(out=x1b[tt * 128:(tt + 1) * 128, :], in_=xo[:]), reads=[xo.b], writes=[b_x1b[tt]])
            hT = hTs[0]
            if do_sample:
              with contextlib.ExitStack() as stQ:
                stg = [mk(stQ, "stcg%d" % i, [120, 512]) for i in range(2)]
                upT = mk(stQ, "upT", [128, 4, NS, 34])
                xt_ = xt[0]
                P.dma("sp", lambda e: e.dma_start(out=xt_[0:TS, :], in_=xs_d), writes=[xt_.b])
                rms_T(xt_, TS, xsb, hT, 0, sm_rms)
                stc2 = stc.rearrange("n r c -> (n r) c")
                for g in range(4):
                    sg_ = stg[g % 2]
                    P.dma("sp", lambda e, sg_=sg_, g=g: e.dma_start(out=sg_[:, :], in_=stc2[g * 120:(g + 1) * 120, :]), writes=[sg_.b])
                    ps = next_ps("m")
                    for c in range(4):
                        P.op("pe", lambda e, ps=ps, c=c, sg_=sg_: e.transpose(out=ps[:, c * 120:(c + 1) * 120], in_=sg_[:, c * 128:(c + 1) * 128], identity=ident32[0:120, 0:120]),
                             reads=[sg_.b, ident32.b], writes=[ps.b])
                    for c in range(4):
                        P.op("dve", lambda e, ps=ps, c=c, g=g: e.tensor_copy(out=upT[:, c, g * 4:(g + 1) * 4, 0:30], in_=ps[:, c * 120:(c + 1) * 120].rearrange("p (n r) -> p n r", r=30)),
                             reads=[ps.b], writes=[upT.b])
                P.dma("pool", lambda e: e.dma_start(out=conv_s[:, 0:26, :], in_=stc[:, 4:30, :]), writes=[])
                for c in range(4):
                    ps = next_ps("m"); proj_fm(ps, TS, hT, 0, wC, 512 + c * 128)
                    P.op("act", lambda e, ps=ps: e.activation(out=sigt[:, 0:TS], in_=ps[:, 0:TS], func=AF.Sigmoid), reads=[ps.b], writes=[sigt.b])
                    ps = next_ps("m"); proj_fm(ps, TS, hT, 0, wC, c * 128)
                    P.op("dve", lambda e, ps=ps, c=c: e.tensor_tensor(out=upT[:, c, :, 30:34], in0=ps[:, 0:TS].rearrange("p (n t) -> p n t", t=4),
                                                                    in1=sigt[:, 0:TS].rearrange("p (n t) -> p n t", t=4), op=ALU.mult), reads=[ps.b, sigt.b], writes=[upT.b])
                    ps = next_ps("m"); proj_fm(ps, TS, hT, 0, wC, 1024 + c * 128)
                    P.op("act", lambda e, ps=ps, c=c: e.activation(out=sgc[:, c, 0:TS], in_=ps[:, 0:TS], func=AF.Silu), reads=[ps.b], writes=[sgc.b])
                for c in range(4):
                    yv = yT[:, c, 0:TS].rearrange("p (n t) -> p n t", t=4)
                    P.op("dve", lambda e, c=c, yv=yv: e.tensor_scalar(out=yv, in0=upT[:, c, :, 0:4], scalar1=cwT[:, c * 31:c * 31 + 1], scalar2=cb[:, c:c + 1], op0=ALU.mult, op1=ALU.add),
                         reads=[upT.b, cwT.b, cb.b], writes=[yT.b])
                    for j in range(1, 31):
                        P.op("dve", lambda e, c=c, j=j, yv=yv: e.scalar_tensor_tensor(out=yv, in0=upT[:, c, :, j:j + 4], scalar=cwT[:, c * 31 + j:c * 31 + j + 1], in1=yv, op0=ALU.mult, op1=ALU.add),
                             reads=[upT.b, cwT.b, yT.b], writes=[yT.b])
                    P.op("act", lambda e, c=c: e.activation(out=ysq[:, c, 0:TS], in_=yT[:, c, 0:TS], func=AF.Square), reads=[yT.b], writes=[ysq.b])
                ln_gate(TS)
                xa = x1aT[0]; xo = x1T[0]
                P.dma("sp", lambda e: e.dma_start(out=xa[0:TS, :], in_=x1a[SEQ:SEQ + TS, :]), reads=[b_x1a[NT]], writes=[xa.b])
                out_proj_res(TS, mTc, 0, 4, woC, xa, xo)
                P.dma("pool", lambda e: e.dma_start(out=x1b[SEQ:SEQ + TS, :], in_=xo[0:TS, :]), reads=[xo.b], writes=[b_x1b[NT]])
                u_tokmajor(TS, 0)
                for n in range(NS):
                    P.dma("pool", lambda e, n=n: e.dma_start(out=conv_s[n, 26:30, :], in_=utm[n * 4:(n + 1) * 4, :]), reads=[utm.b], writes=[])
            P.barrier()

        if lvl >= 7:
          with contextlib.ExitStack() as stC:
            w1 = mk(stC, "w1", [128, 8, 4096], BF16)
            wo1 = mk(stC, "wo1", [128, 8, 1024], BF16)
            with contextlib.ExitStack() as stS:
                stage = [mk(stS, "stgC%d" % i, [128, 2048]) for i in range(2)]
                load_w(stS, w1, w_in_1, 0, 8, 0, 4096, n1, stage)
                load_w(stS, wo1, w_out_1, 0, 8, 0, 1024, None, stage)
                P.barrier()
            xt = [mk(stC, "xtC%d" % i, [128, 1024]) for i in range(2)]
            xsb = [mk(stC, "xsbC0", [128, 1024], BF16)]
            g_extra = mk(stC, "g_extra", [128, 512])
            hT = mk(stC, "hTC", [128, 8, 512], BF16)
            qT = mk(stC, "qTC", [128, 8, 512], BF16); kT = mk(stC, "kTC", [128, 8, 512], BF16)
            dec = mk(stC, "dec", [128, 8, 16])
            sgm = mk(stC, "sgmC", [128, 512]); lf = mk(stC, "lfC", [128, 512]); bcum = mk(stC, "bcumC", [128, 512])
            ep = mk(stC, "epC", [128, 512]); em = mk(stC, "emC", [128, 512]); kk = mk(stC, "kkC", [128, 512]); sqq = mk(stC, "sqqC", [128, 512])
            vtm = mk(stC, "vtm", [128, 4, 1024], BF16); sg1 = mk(stC, "sg1", [128, 4, 1024], BF16)
            ktm = mk(stC, "ktm", [128, 8, 128], BF16)
            S = mk(stC, "S", [128, 8, 128]); Stmp = mk(stC, "Stmp", [128, 8, 128]); Sbf = [mk(stC, "Sbf%d" % i, [128, 8, 128], BF16) for i in range(2)]
            ATm = mk(stC, "ATm", [128, 8, 128], BF16)
            osb = mk(stC, "osb", [128, 1024]); osq = mk(stC, "osq", [128, 512]); gsets = [(sgm, lf, ep, kk, sqq), (bcum, em, osq, g_extra, sqq)]; mo = mk(stC, "mo", [128, 1024], BF16); mT1 = mk(stC, "mT1", [128, 8, 128], BF16)
            yt = [mk(stC, "ytC0", [128, 1024])] * 2
            sm_rms = smalls(stC, 3, 1); sm_o = smalls(stC, 3, 4)
            bA = [ps_m[0], ps_m[1]]; bO = [ps_m[2], ps_s[0]]; bKV = [ps_s[1], ps_o[0]]

            def gates(N, h, rmask):
                psq = [ps_o[1], ps_o[0], ps_s[1]][h % 3]; psf = [ps_m[0], ps_m[1], ps_m[2], ps_s[0]][h % 4]
                sgm, lf, ep, kk, sqq = gsets[h % 2]
                bcum = sgm; em = lf
                proj_fm(psf, N, hT, 0, w1, 1024 + h * 128)
                P.op("act", lambda e: e.activation(out=sgm[:, 0:N], in_=psf[:, 0:N], func=AF.Sigmoid), reads=[psf.b], writes=[sgm.b])
                P.op("act", lambda e: e.activation(out=lf[:, 0:N], in_=sgm[:, 0:N], func=AF.Identity, scale=oml[:, h:h + 1], bias=lbv[:, h:h + 1]),
                     reads=[sgm.b, oml.b, lbv.b], writes=[lf.b])
                P.op("pool", lambda e: e.tensor_scalar(out=kk[:, 0:N], in0=sgm[:, 0:N], scalar1=noml[:, h:h + 1], scalar2=oml[:, h:h + 1], op0=ALU.mult, op1=ALU.add),
                     reads=[sgm.b, noml.b, oml.b], writes=[kk.b])
                P.op("pool", lambda e: e.tensor_tensor(out=bcum[:, 0:N], in0=lf[:, 0:N], in1=rmask[:, 0:N], op=ALU.mult), reads=[lf.b, rmask.b], writes=[bcum.b])
                P.op("dve", lambda e: e.tensor_tensor_scan(out=ep[:, 0:N], data0=lf[:, 0:N], data1=bcum[:, 0:N], initial=1.0, op0=ALU.mult, op1=ALU.max),
                     reads=[bcum.b, lf.b], writes=[ep.b])
                P.op("dve", lambda e: e.reciprocal(out=em[:, 0:N], in_=ep[:, 0:N]), reads=[ep.b], writes=[em.b])
                proj_fm(psq, N, hT, 0, w1, h * 128)
                P.op("act", lambda e: e.activation(out=sqq[:, 0:N], in_=psq[:, 0:N], func=AF.Silu), reads=[psq.b], writes=[sqq.b])
                P.op("dve", lambda e: e.tensor_tensor(out=qT[:, h, 0:N], in0=sqq[:, 0:N], in1=ep[:, 0:N], op=ALU.mult), reads=[sqq.b, ep.b], writes=[qT.b])
                P.op("pool", lambda e: e.tensor_tensor(out=kT[:, h, 0:N], in0=kk[:, 0:N], in1=em[:, 0:N], op=ALU.mult), reads=[kk.b, em.b], writes=[kT.b])
                return ep

            def vg_tm(R, col0, j):
                for n in range(2):
                    ps = ps_o[1] if n == 0 else ps_m[0]
                    proj_tm(ps, R, hT, col0, w1, 2048 + n * 512, 512)
                    P.op("act", lambda e, ps=ps, n=n: e.activation(out=vtm[:R, j, n * 512:(n + 1) * 512], in_=ps[:R, :], func=AF.Copy), reads=[ps.b], writes=[vtm.b])
                for n in range(2):
                    ps = ps_m[1] if n == 0 else ps_m[2]
                    proj_tm(ps, R, hT, col0, w1, 3072 + n * 512, 512)
                    P.op("act", lambda e, ps=ps, n=n: e.activation(out=sg1[:R, j, n * 512:(n + 1) * 512], in_=ps[:R, :], func=AF.Silu), reads=[ps.b], writes=[sg1.b])

            def finish_part1(R, j, mo_):
                for hb in range(2):
                    head_norm(bO[hb], R, 4, 128, _og_half[hb], _osb_half[hb], sm_o, osq)
                P.op("dve", lambda e: e.tensor_tensor(out=mo_[:R, :], in0=osb[:R, :], in1=sg1[:R, j, :], op=ALU.mult), reads=[osb.b, sg1.b], writes=[mo_.b])

            def finish_part2(R, mo_, xres, yt_, ydst):
                for k in range(8):
                    P.op("pe", lambda e, k=k: e.transpose(out=ps_tr[:, k, 0:R], in_=mo_[:R, k * 128:(k + 1) * 128], identity=identb[:R, :R]), reads=[mo_.b, identb.b], writes=[ps_tr.b])
                P.op("dve", lambda e: e.tensor_copy(out=mT1[:, :, 0:R], in_=ps_tr[:, :, 0:R]), reads=[ps_tr.b], writes=[mT1.b])
                out_proj_res(R, mT1, 0, 8, wo1, xres, yt_)
                P.dma("pool", lambda e: e.dma_start(out=ydst, in_=yt_[:R, :]), reads=[yt_.b], writes=[])

            def finish_tokens(R, j, xres, yt_, ydst):
                finish_part1(R, j, mo)
                finish_part2(R, mo, xres, yt_, ydst)

            class _V:
                def __init__(self, parent, c0):
                    self.p = parent; self.c0 = c0; self.b = parent.b
                def __getitem__(self, k):
                    r, cs = k
                    return self.p.t[r, self.c0 + (cs.start or 0):self.c0 + cs.stop]
            _og_half = [_V(og, 0), _V(og, 512)]
            _osb_half = [_V(osb, 0), _V(osb, 512)]

            sm64 = mk(stC, "sm64", [128, 512]); sm4 = mk(stC, "sm4", [128, 64])
            P.op("dve", lambda e: e.tensor_scalar(out=sm64[:, :], in0=rm64[:, :], scalar1=-1.0, scalar2=1.0, op0=ALU.mult, op1=ALU.add), reads=[rm64.b], writes=[sm64.b])
            P.op("dve", lambda e: e.tensor_scalar(out=sm4[:, :], in0=rm4[:, :], scalar1=-1.0, scalar2=1.0, op0=ALU.mult, op1=ALU.add), reads=[rm4.b], writes=[sm4.b])
            P.op("pool", lambda e: e.memset(S[:], 0.0), writes=[S.b])
            P.op("pool", lambda e: e.memset(Sbf[0][:], 0.0), writes=[Sbf[0].b])
            sbi = 0
            pend = None
            mo2 = [mo, mk(stC, "mo_b", [128, 1024], BF16)]
            n_st = (n_tiles + 3) // 4
            for ST in range(n_st):
                for j in range(4):
                    tt = ST * 4 + j
                    xt_ = xt[tt % 2]
                    P.dma("sp", lambda e, xt_=xt_, tt=tt: e.dma_start(out=xt_[:], in_=x1b[tt * 128:(tt + 1) * 128, :]), reads=[b_x1b[tt]], writes=[xt_.b])
                    rms_T(xt_, 128, xsb, hT, j * 128, sm_rms)
                for h in range(8):
                    ep_ = gates(512, h, sm64)
                    P.op("dve", lambda e, h=h: e.tensor_copy(out=dec[:, h, 0:8], in_=ep_[:, :].rearrange("p (c t) -> p c t", t=64)[:, :, 63]), reads=[ep_.b], writes=[dec.b])
                for j in range(4):
                    vg_tm(128, j * 128, j)
                for j in range(4):
                    tt = ST * 4 + j
                    c0 = j * 128
                    for h in range(8):
                        P.op("pe", lambda e, h=h, c0=c0: e.transpose(out=ps_tr[:, h, :], in_=kT[:, h, c0:c0 + 128], identity=identb[:, :]), reads=[kT.b, identb.b], writes=[ps_tr.b])
                    P.op("act", lambda e: e.activation(out=ktm[:, :, :], in_=ps_tr[:, :, :], func=AF.Copy), reads=[ps_tr.b], writes=[ktm.b])
                    def em_inter(ch, Sb):
                        r0 = ch * 64
                        for h in range(8):
                            po = bO[h // 4]
                            P.op("pe", lambda e: e.matmul(po[r0:r0 + 64, (h % 4) * 128:(h % 4 + 1) * 128], lhsT=qT[:, h, c0 + r0:c0 + r0 + 64], rhs=Sb[:, h, :],
                                                          start=False, stop=True, skip_group_check=True), reads=[qT.b, Sb.b], writes=[po.b])

                    def em_KV(ch):
                        r0 = ch * 64
                        for h in range(8):
                            pk = bKV[h // 4]
                            P.op("pe", lambda e: e.matmul(pk[:, (h % 4) * 128:(h % 4 + 1) * 128], lhsT=ktm[r0:r0 + 64, h, :], rhs=vtm[r0:r0 + 64, j, h * 128:(h + 1) * 128],
                                                          start=True, stop=True), reads=[ktm.b, vtm.b], writes=[pk.b])

                    def em_update(ch, Sn):
                        cidx = j * 2 + ch
                        for hb in range(2):
                            P.op("dve", lambda e: e.tensor_tensor(out=Stmp[:, hb * 4:(hb + 1) * 4, :], in0=bKV[hb][:, :].rearrange("p (h v) -> p h v", v=128), in1=S[:, hb * 4:(hb + 1) * 4, :], op=ALU.add),
                                 reads=[bKV[hb].b, S.b], writes=[Stmp.b])
                        P.op("dve", lambda e: e.tensor_tensor(out=S[:, :, :], in0=Stmp[:, :, :], in1=dec[:, :, cidx:cidx + 1].to_broadcast([128, 8, 128]), op=ALU.mult),
                             reads=[Stmp.b, dec.b], writes=[S.b])
                        P.op("act", lambda e: e.activation(out=Sn[:, :, :], in_=S[:, :, :], func=AF.Copy), reads=[S.b], writes=[Sn.b])

                    for h in range(8):
                        pa = bA[h // 4]
                        P.op("pe", lambda e: e.matmul(pa[:, (h % 4) * 128:(h % 4 + 1) * 128], lhsT=kT[:, h, c0:c0 + 128], rhs=qT[:, h, c0:c0 + 128], start=True, stop=True),
                             reads=[kT.b, qT.b], writes=[pa.b])
                    em_KV(0)
                    for hb in range(2):
                        P.op("dve", lambda e: e.tensor_tensor(out=ATm[:, hb * 4:(hb + 1) * 4, :], in0=bA[hb][:, :].rearrange("p (h t) -> p h t", t=128),
                                                              in1=blkc[:, :].unsqueeze(1).to_broadcast([128, 4, 128]), op=ALU.mult), reads=[bA[hb].b, blkc.b], writes=[ATm.b])
                    for h in range(8):
                        po = bO[h // 4]
                        P.op("pe", lambda e: e.matmul(po[:, (h % 4) * 128:(h % 4 + 1) * 128], lhsT=ATm[:, h, :], rhs=vtm[:, j, h * 128:(h + 1) * 128],
                                                      start=(h % 4 == 0), stop=False, skip_group_check=True), reads=[ATm.b, vtm.b], writes=[po.b])
                    S_cur = Sbf[sbi % 2]; S_mid = Sbf[(sbi + 1) % 2]
                    em_inter(0, S_cur)
                    em_update(0, S_mid)
                    em_KV(1)
                    em_inter(1, S_mid)
                    em_update(1, S_cur)
                    mo_ = mo2[tt % 2]
                    finish_part1(128, j, mo_)
                    if pend is not None:
                        pend()
                    xr = xt[tt % 2]
                    P.dma("sp", lambda e, xr=xr, tt=tt: e.dma_start(out=xr[:], in_=x1b[tt * 128:(tt + 1) * 128, :]), reads=[b_x1b[tt]], writes=[xr.b])
                    pend = (lambda mo_=mo_, xr=xr, tt=tt: finish_part2(128, mo_, xr, yt[tt % 2], y_p[tt * 128:(tt + 1) * 128, :]))
                    if j == 3:
                        pend(); pend = None
            P.dma("pool", lambda e: e.dma_start(out=hg_p.rearrange("h k v -> k h v"), in_=S[:, :, :]), reads=[S.b], writes=[])
            if do_sample:
              with contextlib.ExitStack() as stQ:
                qpad = [mk(stQ, "qpad%d" % i, [128, 8, 64], BF16) for i in range(2)]
                S0 = [mk(stQ, "S0_%d" % i, [128, 8, 128]) for i in range(2)]
                S0b = Sbf
                class _R:
                    def __init__(self, parent):
                        self.p = parent; self.b = parent.b
                    def __getitem__(self, k):
                        return self.p.t[:, :].rearrange("p (h v) -> p h v", v=128)
                Sn_ = [S, _R(osb)]
                vmk = [mo, mo]
                decs = mk(stQ, "decs", [128, 8, NS])
                xt_ = xt[0]
                P.dma("sp", lambda e: e.dma_start(out=xt_[0:TS, :], in_=x1b[SEQ:SEQ + TS, :]), reads=[b_x1b[NT]], writes=[xt_.b])
                rms_T(xt_, TS, xsb, hT, 0, sm_rms)
                for i in range(2):
                    P.op("pool", lambda e, i=i: e.memset(qpad[i][:], 0.0), writes=[qpad[i].b])
                for h in range(8):
                    ep_ = gates(TS, h, sm4)
                    P.op("dve", lambda e, h=h: e.tensor_copy(out=decs[:, h, :], in_=ep_[:, 0:TS].rearrange("p (n t) -> p n t", t=4)[:, :, 3]), reads=[ep_.b], writes=[decs.b])
                vg_tm(TS, 0, 0)
                for h in range(8):
                    P.op("pe", lambda e, h=h: e.transpose(out=ps_tr[0:TS, h, :], in_=kT[:, h, 0:TS], identity=identb[:, :]), reads=[kT.b, identb.b], writes=[ps_tr.b])
                P.op("act", lambda e: e.activation(out=ktm[0:TS, :, :], in_=ps_tr[0:TS, :, :], func=AF.Copy), reads=[ps_tr.b], writes=[ktm.b])
                pa = bA[0]
                for h in range(8):
                    P.op("pe", lambda e, h=h: e.matmul(pa[0:TS, h * 64:(h + 1) * 64], lhsT=kT[:, h, 0:TS], rhs=qT[:, h, 0:TS], start=True, stop=True), reads=[kT.b, qT.b], writes=[pa.b])
                P.op("dve", lambda e: e.tensor_tensor(out=ATm[0:TS, :, 0:TS], in0=pa[0:TS, :].rearrange("p (h t) -> p h t", t=64), in1=mnew[:, :].unsqueeze(1).to_broadcast([TS, 8, TS]), op=ALU.mult),
                     reads=[pa.b, mnew.b], writes=[ATm.b])
                for h in range(8):
                    po = bO[h // 4]
                    P.op("pe", lambda e, h=h, po=po: e.matmul(po[0:TS, (h % 4) * 128:(h % 4 + 1) * 128], lhsT=ATm[0:TS, h, 0:TS], rhs=vtm[0:TS, 0, h * 128:(h + 1) * 128],
                                                              start=(h % 4 == 0), stop=False, skip_group_check=True), reads=[ATm.b, vtm.b], writes=[po.b])
                sth2 = sth.rearrange("n h k v -> n k h v")
                hg2 = hg_s.rearrange("n h k v -> n k h v")
                for n in range(NS):
                    s0 = S0[n % 2]; s0b = S0b[n % 2]; sn = Sn_[n % 2]; vm = vmk[n % 2]; qp = qpad[n % 2]
                    if n >= 2:
                        P.op("pool", lambda e, qp=qp, n=n: e.memset(qp[:, :, (n - 2) * 4:(n - 1) * 4], 0.0), writes=[qp.b])
                    P.op("pool", lambda e, qp=qp, n=n: e.tensor_copy(out=qp[:, :, n * 4:(n + 1) * 4], in_=qT[:, :, n * 4:(n + 1) * 4]), reads=[qT.b], writes=[qp.b])
                    P.dma("sp", lambda e, s0=s0, n=n: e.dma_start(out=s0[:, :, :], in_=sth2[n]), writes=[s0.b])
                    P.op("act", lambda e, s0=s0, s0b=s0b: e.activation(out=s0b[:, :, :], in_=s0[:, :, :], func=AF.Copy), reads=[s0.b], writes=[s0b.b])
                    for h in range(8):
                        po = bO[h // 4]
                        P.op("pe", lambda e, h=h, po=po, n=n, s0b=s0b, qp=qp: e.matmul(po[0:TS, (h % 4) * 128:(h % 4 + 1) * 128], lhsT=qp[:, h, :], rhs=s0b[:, h, :],
                                                                           start=False, stop=True, skip_group_check=True), reads=[qp.b, s0b.b], writes=[po.b])
                    P.op("dve", lambda e, vm=vm, n=n: e.tensor_scalar(out=vm[0:TS, :], in0=vtm[0:TS, 0, :], scalar1=seqm[:, n:n + 1], scalar2=None, op0=ALU.mult), reads=[vtm.b, seqm.b], writes=[vm.b])
                    for h in range(8):
                        pk = bKV[h // 4]
                        P.op("pe", lambda e, h=h, pk=pk, vm=vm: e.matmul(pk[:, (h % 4) * 128:(h % 4 + 1) * 128], lhsT=ktm[0:TS, h, :], rhs=vm[0:TS, h * 128:(h + 1) * 128], start=True, stop=True),
                             reads=[ktm.b, vm.b], writes=[pk.b])
                    for hb in range(2):
                        P.op("dve", lambda e, hb=hb, s0=s0: e.tensor_tensor(out=Stmp[:, hb * 4:(hb + 1) * 4, :], in0=bKV[hb][:, :].rearrange("p (h v) -> p h v", v=128), in1=s0[:, hb * 4:(hb + 1) * 4, :], op=ALU.add),
                             reads=[bKV[hb].b, s0.b], writes=[Stmp.b])
                    P.op("dve", lambda e, sn=sn, n=n: e.tensor_tensor(out=sn[:, :, :], in0=Stmp[:, :, :], in1=decs[:, :, n:n + 1].to_broadcast([128, 8, 128]), op=ALU.mult),
                         reads=[Stmp.b, decs.b], writes=[sn.b])
                    P.dma("pool", lambda e, sn=sn, n=n: e.dma_start(out=hg2[n], in_=sn[:, :, :]), reads=[sn.b], writes=[])
                xr = xt[1]
                P.dma("sp", lambda e: e.dma_start(out=xr[0:TS, :], in_=x1b[SEQ:SEQ + TS, :]), reads=[b_x1b[NT]], writes=[xr.b])
                finish_tokens(TS, 0, xr, yt[0], y_s)
        P.emit()
    return nc


_NC_CACHE = {}


def _consts():
    bf = ml_dtypes.bfloat16
    p = np.arange(128)
    tri = (p[:, None] <= p[None, :]).astype(np.float32)
    q = np.arange(64)
    mnew = ((q[:, None] // 4 == q[None, :] // 4) & (q[:, None] <= q[None, :])).astype(np.float32)
    blkc = ((p[:, None] // 64 == p[None, :] // 64) & (p[:, None] <= p[None, :])).astype(np.float32)
    rm64 = np.ones((128, 512), np.float32); rm64[:, ::64] = 0.0
    rm4 = np.ones((128, 64), np.float32); rm4[:, ::4] = 0.0
    seqm = (q[:, None] // 4 == np.arange(16)[None, :]).astype(np.float32)
    return {
        "identb": np.eye(128, dtype=np.float32).astype(bf), "ident32": np.eye(128, dtype=np.float32),
        "ones32": np.ones((128, 128), np.float32), "tri": tri.astype(bf), "mnew": mnew.astype(bf), "blkc": blkc.astype(bf),
        "rm64": rm64, "rm4": rm4, "seqm": seqm,
    }


def kernel(x_prompt, x_sample, cache_k, cache_v, state_conv, state_hgrn, page_table,
           norm_0, w_in_0, q_norm_0, k_norm_0, conv_w_0, conv_b_0, conv_ln_g_0, conv_ln_b_0, w_out_0,
           norm_1, w_in_1, lb_logits, o_norm_1, w_out_1):
    f = lambda a: np.ascontiguousarray(np.asarray(a, dtype=np.float32))
    if "nc" not in _NC_CACHE:
        _NC_CACHE["nc"] = build_nc()
    nc = _NC_CACHE["nc"]
    x_prompt = f(x_prompt); x_sample = f(x_sample)
    ckv = np.concatenate([f(cache_k).reshape(2560 * 128, 512), f(cache_v).reshape(2560 * 128, 512)], axis=1)
    state_conv = f(state_conv); state_hgrn = f(state_hgrn)
    page_table = np.ascontiguousarray(np.asarray(page_table, dtype=np.int32))
    pk = lambda v, k: np.ascontiguousarray(f(v).reshape(k, 128).T)
    shared = {
        "ckv": ckv,
        "w_in_0": f(w_in_0), "w_out_0": f(w_out_0), "w_in_1": f(w_in_1), "w_out_1": f(w_out_1),
        "n0": pk(norm_0, 8), "n1": pk(norm_1, 8),
        "qg": np.tile(f(q_norm_0), 8), "kg": np.tile(f(k_norm_0), 8), "og": np.tile(f(o_norm_1), 8),
        "cwT": np.ascontiguousarray(f(conv_w_0).T.reshape(4, 128, 31).transpose(1, 0, 2).reshape(128, 124)),
        "cb": pk(conv_b_0, 4), "lng": pk(conv_ln_g_0, 4), "lnb": pk(conv_ln_b_0, 4),
        "lb": np.ascontiguousarray(np.concatenate([pk(f(lb_logits)[0], 8), pk(f(lb_logits)[1], 8)], axis=1)),
    }
    shared.update(_consts())
    in_maps = []
    for c in range(8):
        m = dict(shared)
        m["xp"] = x_prompt[c // 2]
        m["xs"] = x_sample[c * NS:(c + 1) * NS].reshape(TS, 1024)
        m["stc"] = state_conv[c * NS:(c + 1) * NS]
        m["sth"] = state_hgrn[c * NS:(c + 1) * NS]
        m["ptab"] = page_table[c * NS:(c + 1) * NS].reshape(-1)
        in_maps.append(m)
    res = run_bass_kernel_spmd(nc, in_maps, core_ids=list(range(8)))
    R = res.results
    H = SEQ // 2

    def prompt(name, width):
        out = np.empty((4, SEQ, width), np.float32)
        for c in range(8):
            hlf = c % 2
            out[c // 2, hlf * H:(hlf + 1) * H] = R[c][name][hlf * H:(hlf + 1) * H]
        return out

    y_prompt = prompt("y_p", 1024)
    k_prompt = prompt("k_p", 512).reshape(4, SEQ, 8, 64)
    v_prompt = prompt("v_p", 512).reshape(4, SEQ, 8, 64)
    y_sample = np.concatenate([R[c]["y_s"].reshape(NS, 4, 1024) for c in range(8)], axis=0)
    k_sample = np.concatenate([R[c]["k_s"].reshape(NS, 4, 8, 64) for c in range(8)], axis=0)
    v_sample = np.concatenate([R[c]["v_s"].reshape(NS, 4, 8, 64) for c in range(8)], axis=0)
    conv_prompt = np.stack([R[2 * s + 1]["conv_p"] for s in range(4)], axis=0)
    conv_sample = np.concatenate([R[c]["conv_s"] for c in range(8)], axis=0)
    hgrn_prompt = np.stack([R[2 * s + 1]["hg_p"] for s in range(4)], axis=0)
    hgrn_sample = np.concatenate([R[c]["hg_s"] for c in range(8)], axis=0)
    return (y_prompt, y_sample, k_prompt, v_prompt, k_sample, v_sample,
            conv_prompt, conv_sample, hgrn_prompt, hgrn_sample)
```

```python
import contextlib
import numpy as np
import ml_dtypes
import concourse.bass as bass
import concourse.mybir as mybir
from concourse.bass_utils import run_bass_kernel_spmd

F32 = mybir.dt.float32
BF16 = mybir.dt.bfloat16
I32 = mybir.dt.int32
AF = mybir.ActivationFunctionType
ALU = mybir.AluOpType
AX = mybir.AxisListType

N_DMA_SEMS = 40
EPS = 1e-6
SEQ = 4096
NT = SEQ // 128
NS = 16
TS = 64
NEG = -1.0e30


class Buf:
    __slots__ = ("name", "last_w", "readers", "excl")

    def __init__(self, name="", excl=False):
        self.name = name
        self.last_w = None
        self.readers = []
        self.excl = excl


class Op:
    __slots__ = ("eng", "fn", "deps", "is_dma", "signal", "semval", "dsem", "dtarget", "prev_on_sem")

    def __init__(self, eng, fn, is_dma):
        self.eng = eng
        self.fn = fn
        self.deps = []
        self.is_dma = is_dma
        self.signal = False
        self.semval = None
        self.dsem = None
        self.dtarget = None
        self.prev_on_sem = None


class _Rec:
    def __init__(self):
        self.call = None

    def __getattr__(self, name):
        def f(*a, **k):
            self.call = (name, a, k)
            return None
        return f


class Prog:
    ENGS = ("pe", "act", "dve", "pool", "sp")

    def __init__(self, nc):
        self.nc = nc
        self.ops = []
        self.dma_rr = 0
        self.dma_last = [None] * N_DMA_SEMS
        self.dma_val = [0] * N_DMA_SEMS
        self.last_compute = {e: None for e in self.ENGS}

    def _add(self, eng, fn, reads, writes, is_dma):
        rec = _Rec()
        fn(rec)
        assert rec.call is not None
        op = Op(eng, rec.call, is_dma)
        ex = [b for b in reads if b.excl]
        if ex:
            reads = [b for b in reads if not b.excl]
            writes = list(writes) + ex
        deps = {}
        for b in reads:
            if b.last_w is not None:
                deps[id(b.last_w)] = b.last_w
        for b in writes:
            if b.last_w is not None:
                deps[id(b.last_w)] = b.last_w
            for r in b.readers:
                deps[id(r)] = r
        for d in deps.values():
            if d is op:
                continue
            if (not d.is_dma) and d.eng == eng and not is_dma and eng == "pe":
                continue
            op.deps.append(d)
            if not d.is_dma:
                d.signal = True
        for b in reads:
            b.readers.append(op)
        for b in writes:
            b.last_w = op
            b.readers = []
        if is_dma:
            s = self.dma_rr
            self.dma_rr = (self.dma_rr + 1) % N_DMA_SEMS
            op.dsem = s
            op.prev_on_sem = self.dma_last[s]
            self.dma_val[s] += 16
            op.dtarget = self.dma_val[s]
            self.dma_last[s] = op
        else:
            self.last_compute[eng] = op
        self.ops.append(op)
        return op

    def op(self, eng, fn, reads=(), writes=()):
        return self._add(eng, fn, reads, writes, False)

    def dma(self, eng, fn, reads=(), writes=()):
        return self._add(eng, fn, reads, writes, True)

    def barrier(self):
        deps = []
        for e in self.ENGS:
            o = self.last_compute[e]
            if o is not None:
                o.signal = True
                deps.append(o)
        for o in self.dma_last:
            if o is not None:
                deps.append(o)
        for e in self.ENGS:
            op = Op(e, None, False)
            op.deps = [d for d in deps if d.is_dma or d.eng != e]
            self.ops.append(op)

    def emit(self):
        nc = self.nc
        cnt = {e: 0 for e in self.ENGS}
        for op in self.ops:
            if op.fn is not None and not op.is_dma and op.signal:
                cnt[op.eng] += 1
                op.semval = cnt[op.eng]
        with contextlib.ExitStack() as st:
            esem = {e: st.enter_context(nc.semaphore("s_" + e)) for e in self.ENGS}
            dsem = [st.enter_context(nc.semaphore("d%d" % i)) for i in range(N_DMA_SEMS)]
            block = st.enter_context(nc.Block())
            ops = self.ops
            dma_final = [(dsem[i], self.dma_val[i]) for i in range(N_DMA_SEMS) if self.dma_val[i] > 0]

            def run(engname, eng):
                waited = {}

                def wait(key, sem, val):
                    if waited.get(key, 0) >= val:
                        return
                    eng.wait_ge(sem, val)
                    waited[key] = val

                for op in ops:
                    if op.eng != engname:
                        continue
                    for d in op.deps:
                        if d.is_dma:
                            wait(("d", d.dsem), dsem[d.dsem], d.dtarget)
                        else:
                            wait(("e", d.eng), esem[d.eng], d.semval)
                    if op.fn is None:
                        continue
                    if op.is_dma:
                        p = op.prev_on_sem
                        if p is not None:
                            wait(("d", p.dsem), dsem[p.dsem], p.dtarget)
                        nm, a_, k_ = op.fn
                        ins = getattr(eng, nm)(*a_, **k_)
                        ins.then_inc(dsem[op.dsem], 16)
                    else:
                        nm, a_, k_ = op.fn
                        ins = getattr(eng, nm)(*a_, **k_)
                        if op.signal:
                            ins.then_inc(esem[engname], 1)
                if engname == "sp":
                    for (s, v) in dma_final:
                        eng.wait_ge(s, v)

            @block.tensor
            def _(e):
                run("pe", e)

            @block.scalar
            def _(e):
                run("act", e)

            @block.vector
            def _(e):
                run("dve", e)

            @block.gpsimd
            def _(e):
                run("pool", e)

            @block.sync
            def _(e):
                run("sp", e)


class T:
    def __init__(self, t, name):
        self.t = t
        self.b = Buf(name)

    def __getitem__(self, k):
        return self.t[k]


def build_nc(n_tiles=NT, do_sample=True, n_blk_s=8, n_pool=2560, lvl=9):
    nc = bass.Bass("TRN2", target_bir_lowering=False)
    P = Prog(nc)

    def din(name, shape, dt=F32):
        return nc.dram_tensor(name, list(shape), dt, kind="ExternalInput").ap()

    def dout(name, shape, dt=F32):
        return nc.dram_tensor(name, list(shape), dt, kind="ExternalOutput").ap()

    xp = din("xp", [SEQ, 1024]); xs_d = din("xs", [TS, 1024])
    ckv = din("ckv", [n_pool * 128, 1024])
    stc = din("stc", [NS, 30, 512]); sth = din("sth", [NS, 8, 128, 128])
    ptab = din("ptab", [NS * 16], I32)
    w_in_0 = din("w_in_0", [1024, 3584]); w_out_0 = din("w_out_0", [1024, 1024])
    w_in_1 = din("w_in_1", [1024, 4096]); w_out_1 = din("w_out_1", [1024, 1024])
    n0_d = din("n0", [128, 8]); n1_d = din("n1", [128, 8])
    qg_d = din("qg", [512]); kg_d = din("kg", [512]); og_d = din("og", [1024])
    cwT_d = din("cwT", [128, 4 * 31]); cb_d = din("cb", [128, 4]); lng_d = din("lng", [128, 4]); lnb_d = din("lnb", [128, 4])
    lb_d = din("lb", [128, 16])
    identb_d = din("identb", [128, 128], BF16); ident32_d = din("ident32", [128, 128]); ones32_d = din("ones32", [128, 128])
    tri_d = din("tri", [128, 128], BF16); mnew_d = din("mnew", [64, 64], BF16); blkc_d = din("blkc", [128, 128], BF16)
    rm64_d = din("rm64", [128, 512]); rm4_d = din("rm4", [128, 64]); seqm_d = din("seqm", [64, 16])

    y_p = dout("y_p", [SEQ, 1024]); y_s = dout("y_s", [TS, 1024])
    k_p = dout("k_p", [SEQ, 512]); v_p = dout("v_p", [SEQ, 512])
    k_s = dout("k_s", [TS, 512]); v_s = dout("v_s", [TS, 512])
    conv_p = dout("conv_p", [30, 512]); conv_s = dout("conv_s", [NS, 30, 512])
    hg_p = dout("hg_p", [8, 128, 128]); hg_s = dout("hg_s", [NS, 8, 128, 128])
    x1a = nc.dram_tensor("x1a", [SEQ + TS, 1024], F32, kind="ExternalOutput").ap()
    x1b = nc.dram_tensor("x1b", [SEQ + TS, 1024], F32).ap()
    b_x1a = [Buf() for _ in range(NT + 1)]
    b_x1b = [Buf() for _ in range(NT + 1)]
    b_out = Buf("out")

    with contextlib.ExitStack() as st0:
        def mk(st, name, shape, dt=F32):
            return T(st.enter_context(nc.sbuf_tensor("sb_" + name, list(shape), dt)), name)

        def mkp(st, name, shape, dt=F32):
            t = T(st.enter_context(nc.psum_tensor("pp_" + name, list(shape), dt)), name)
            t.b.excl = True
            return t

        ps_tr = mkp(st0, "ps_tr", [128, 8, 128], BF16)
        ps_m = [mkp(st0, "ps_m%d" % i, [128, 512]) for i in range(3)]
        ps_s = [mkp(st0, "ps_s%d" % i, [128, 512]) for i in range(2)]
        ps_o = [mkp(st0, "ps_o%d" % i, [128, 512]) for i in range(2)]
        rr = {"m": 0, "s": 0, "o": 0}

        def next_ps(kind):
            lst = {"m": ps_m, "s": ps_s, "o": ps_o}[kind]
            i = rr[kind]
            rr[kind] = (i + 1) % len(lst)
            return lst[i]

        identb = mk(st0, "identb", [128, 128], BF16); ident32 = mk(st0, "ident32", [128, 128]); ones32 = mk(st0, "ones32", [128, 128])
        tri = mk(st0, "tri", [128, 128], BF16); mnew = mk(st0, "mnew", [64, 64], BF16); blkc = mk(st0, "blkc", [128, 128], BF16)
        rm64 = mk(st0, "rm64", [128, 512]); rm4 = mk(st0, "rm4", [128, 64]); seqm = mk(st0, "seqm", [64, 16])
        n0 = mk(st0, "n0", [128, 8]); n1 = mk(st0, "n1", [128, 8])
        qg = mk(st0, "qg", [128, 512]); kg = mk(st0, "kg", [128, 512]); og = mk(st0, "og", [128, 1024])
        cwT = mk(st0, "cwT", [128, 4 * 31]); cb = mk(st0, "cb", [128, 4]); lng = mk(st0, "lng", [128, 4]); lnb = mk(st0, "lnb", [128, 4])
        lbl = mk(st0, "lbl", [128, 16]); lbv = mk(st0, "lbv", [128, 8]); oml = mk(st0, "oml", [128, 8]); noml = mk(st0, "noml", [128, 8])
        for (t, d) in [(identb, identb_d), (ident32, ident32_d), (ones32, ones32_d), (tri, tri_d), (mnew, mnew_d), (blkc, blkc_d),
                       (rm64, rm64_d), (rm4, rm4_d), (seqm, seqm_d), (n0, n0_d), (n1, n1_d), (cwT, cwT_d), (cb, cb_d),
                       (lng, lng_d), (lnb, lnb_d), (lbl, lb_d)]:
            P.dma("sp", lambda e, t=t, d=d: e.dma_start(out=t[:], in_=d), writes=[t.b])
        for (t, d) in [(qg, qg_d), (kg, kg_d), (og, og_d)]:
            P.dma("sp", lambda e, t=t, d=d: e.dma_start(out=t[:], in_=d.partition_broadcast(128)), writes=[t.b])
        P.op("dve", lambda e: e.tensor_sub(out=lbv[:], in0=lbl[:, 8:16], in1=lbl[:, 0:8]), reads=[lbl.b], writes=[lbv.b])
        P.op("act", lambda e: e.activation(out=lbv[:], in_=lbv[:], func=AF.Sigmoid), reads=[lbv.b], writes=[lbv.b])
        P.op("dve", lambda e: e.tensor_scalar(out=oml[:], in0=lbv[:], scalar1=-1.0, scalar2=1.0, op0=ALU.mult, op1=ALU.add), reads=[lbv.b], writes=[oml.b])
        P.op("dve", lambda e: e.tensor_scalar(out=noml[:], in0=oml[:], scalar1=-1.0, scalar2=None, op0=ALU.mult), reads=[oml.b], writes=[noml.b])

        def load_w(st, dst, wd, row0, nk, col0, ncols, scale, stage):
            i = 0
            for k in range(nk):
                for c0 in range(0, ncols, 2048):
                    cw = min(2048, ncols - c0)
                    sg = stage[i % 2]; i += 1
                    P.dma("sp", lambda e, sg=sg, k=k, c0=c0, cw=cw: e.dma_start(
                        out=sg[:, 0:cw], in_=wd[row0 + k * 128:row0 + (k + 1) * 128, col0 + c0:col0 + c0 + cw]), writes=[sg.b])
                    if scale is None:
                        P.op("act", lambda e, sg=sg, k=k, c0=c0, cw=cw: e.activation(out=dst[:, k, c0:c0 + cw], in_=sg[:, 0:cw], func=AF.Copy),
                             reads=[sg.b], writes=[dst.b])
                    else:
                        P.op("act", lambda e, sg=sg, k=k, c0=c0, cw=cw: e.activation(out=dst[:, k, c0:c0 + cw], in_=sg[:, 0:cw], func=AF.Copy,
                                                                                 scale=scale[:, k:k + 1]),
                             reads=[sg.b, scale.b], writes=[dst.b])

        rms_rr = [0]

        def rms_T(xt, R, xsb, hT, col0, small):
            ssq, rt, rstd = small
            if isinstance(xsb, list):
                rms_rr[0] += 1
                xsb = xsb[rms_rr[0] % len(xsb)]
            P.op("act", lambda e: e.activation(out=xsb[:R, :], in_=xt[:R, :], func=AF.Square, accum_out=ssq[:R, 0:1]),
                 reads=[xt.b], writes=[xsb.b, ssq.b])
            P.op("act", lambda e: e.activation(out=rt[:R, 0:1], in_=ssq[:R, 0:1], func=AF.Sqrt, scale=1.0 / 1024, bias=EPS),
                 reads=[ssq.b], writes=[rt.b])
            P.op("dve", lambda e: e.reciprocal(out=rstd[:R, 0:1], in_=rt[:R, 0:1]), reads=[rt.b], writes=[rstd.b])
            P.op("act", lambda e: e.activation(out=xsb[:R, :], in_=xt[:R, :], func=AF.Copy, scale=rstd[:R, 0:1]),
                 reads=[xt.b, rstd.b], writes=[xsb.b])
            for k in range(8):
                P.op("pe", lambda e, k=k: e.transpose(out=ps_tr[:, k, 0:R], in_=xsb[:R, k * 128:(k + 1) * 128], identity=identb[:R, :R]),
                     reads=[xsb.b, identb.b], writes=[ps_tr.b])
            P.op("dve", lambda e: e.tensor_copy(out=hT[:, :, col0:col0 + R], in_=ps_tr[:, :, 0:R]), reads=[ps_tr.b], writes=[hT.b])

        def proj_tm(ps, R, hT, col0, w, wc0, ncols, nk=8):
            for k in range(nk):
                P.op("pe", lambda e, k=k: e.matmul(ps[:R, 0:ncols], lhsT=hT[:, k, col0:col0 + R], rhs=w[:, k, wc0:wc0 + ncols],
                                                   start=(k == 0), stop=(k == nk - 1)),
                     reads=[hT.b, w.b], writes=[ps.b])

        def proj_fm(ps, N, hT, col0, w, wc0, nk=8):
            for k in range(nk):
                P.op("pe", lambda e, k=k: e.matmul(ps[:, 0:N], lhsT=w[:, k, wc0:wc0 + 128], rhs=hT[:, k, col0:col0 + N],
                                                   start=(k == 0), stop=(k == nk - 1)),
                     reads=[hT.b, w.b], writes=[ps.b])

        def head_norm(ps, R, nh, d, gt, out, small, sq):
            ss, rt, rs = small
            W = nh * d
            P.op("act", lambda e: e.activation(out=sq[:R, 0:W], in_=ps[:R, 0:W], func=AF.Square), reads=[ps.b], writes=[sq.b])
            P.op("dve", lambda e: e.tensor_reduce(out=ss[:R, 0:nh], in_=sq[:R, 0:W].rearrange("p (h d) -> p h d", d=d), axis=AX.X, op=ALU.add),
                 reads=[sq.b], writes=[ss.b])
            P.op("act", lambda e: e.activation(out=rt[:R, 0:nh], in_=ss[:R, 0:nh], func=AF.Sqrt, scale=1.0 / d, bias=EPS),
                 reads=[ss.b], writes=[rt.b])
            P.op("dve", lambda e: e.reciprocal(out=rs[:R, 0:nh], in_=rt[:R, 0:nh]), reads=[rt.b], writes=[rs.b])
            P.op("dve", lambda e: e.tensor_tensor(out=out[:R, 0:W].rearrange("p (h d) -> p h d", d=d),
                                                  in0=ps[:R, 0:W].rearrange("p (h d) -> p h d", d=d),
                                                  in1=rs[:R, 0:nh].unsqueeze(2).to_broadcast([R, nh, d]), op=ALU.mult),
                 reads=[ps.b, rs.b], writes=[out.b])
            P.op("dve", lambda e: e.tensor_tensor(out=out[:R, 0:W], in0=out[:R, 0:W], in1=gt[:R, 0:W], op=ALU.mult),
                 reads=[out.b, gt.b], writes=[out.b])

        def out_proj_res(R, mT, mcol0, nkc, wo, res, ydst):
            for n in range(2):
                ps = next_ps("m")
                for c in range(nkc):
                    P.op("pe", lambda e, c=c, n=n, ps=ps: e.matmul(ps[:R, :], lhsT=mT[:, c, mcol0:mcol0 + R], rhs=wo[:, c, n * 512:(n + 1) * 512],
                                                                   start=(c == 0), stop=(c == nkc - 1)),
                         reads=[mT.b, wo.b], writes=[ps.b])
                P.op("dve", lambda e, n=n, ps=ps: e.tensor_tensor(out=ydst[:R, n * 512:(n + 1) * 512], in0=ps[:R, :], in1=res[:R, n * 512:(n + 1) * 512], op=ALU.add),
                     reads=[ps.b, res.b], writes=[ydst.b])

        small_i = [0]

        def smalls(st, n, w):
            small_i[0] += 1
            return tuple(mk(st, "sm%d_%d" % (small_i[0], j), [128, w]) for j in range(n))

        with contextlib.ExitStack() as stA:
            wA = mk(stA, "wA", [128, 8, 2048], BF16)
            woA = mk(stA, "woA", [128, 4, 1024], BF16)
            xt = [mk(stA, "xtA%d" % i, [128, 1024]) for i in range(2)]
            xsb = mk(stA, "xsbA", [128, 1024], BF16)
            hT = [mk(stA, "hTA%d" % i, [128, 8, 128], BF16) for i in range(2)]
            sq = mk(stA, "sqA", [128, 512])
            qn = mk(stA, "qnA", [128, 512])
            kn = [mk(stA, "knA%d" % i, [128, 512]) for i in range(2)]
            vsb = [mk(stA, "vsbA%d" % i, [128, 512]) for i in range(2)]
            sgt = mk(stA, "sgA", [128, 512])
            QT = mk(stA, "QTA", [128, 4, 128], BF16)
            qT32 = mk(stA, "qT32A", [128, 4, 128])
            x1t = [mk(stA, "x1tA%d" % i, [128, 1024]) for i in range(2)]
            sel = mk(stA, "selA", [128, 17]); sel_default = sel; gm = mk(stA, "gmA", [128, 16]); top8 = mk(stA, "top8A", [128, 8])
            acc = mk(stA, "accA", [128, 65]); tmpO = mk(stA, "tmpOA", [128, 7, 65]); part = mk(stA, "partA", [128, 65]); rl = mk(stA, "rlA", [128, 1])
            oatt = mk(stA, "oattA", [128, 512]); matt = mk(stA, "mattA", [128, 512], BF16); mT = mk(stA, "mTA", [128, 4, 128], BF16)
            sm_rms = smalls(stA, 3, 1); sm_q = smalls(stA, 3, 8); sm_k = smalls(stA, 3, 8)
            with contextlib.ExitStack() as stS:
                stage = [mk(stS, "stgA%d" % i, [128, 2048]) for i in range(2)]
                load_w(stS, wA, w_in_0, 0, 8, 0, 2048, n0, stage)
                load_w(stS, woA, w_out_0, 0, 4, 0, 1024, None, stage)
                P.barrier()

            def qkvg(R, xt_, hT_, kn_, vsb_, kdst, vdst):
                rms_T(xt_, R, xsb, hT_, 0, sm_rms)
                ps = next_ps("m"); proj_tm(ps, R, hT_, 0, wA, 0, 512); head_norm(ps, R, 8, 64, qg, qn, sm_q, sq)
                ps = next_ps("m"); proj_tm(ps, R, hT_, 0, wA, 512, 512); head_norm(ps, R, 8, 64, kg, kn_, sm_k, sq)
                P.dma("pool", lambda e: e.dma_start(out=kdst, in_=kn_[:R, :]), reads=[kn_.b], writes=[])
                ps = next_ps("m"); proj_tm(ps, R, hT_, 0, wA, 1024, 512)
                P.op("act", lambda e, ps=ps: e.activation(out=vsb_[:R, :], in_=ps[:R, :], func=AF.Copy), reads=[ps.b], writes=[vsb_.b])
                P.dma("pool", lambda e: e.dma_start(out=vdst, in_=vsb_[:R, :]), reads=[vsb_.b], writes=[])
                ps = next_ps("m"); proj_tm(ps, R, hT_, 0, wA, 1536, 512)
                P.op("act", lambda e, ps=ps: e.activation(out=sgt[:R, :], in_=ps[:R, :], func=AF.Silu), reads=[ps.b], writes=[sgt.b])
                ps = next_ps("m")
                for c in range(4):
                    P.op("pe", lambda e, c=c, ps=ps: e.transpose(out=ps[:, c * 128:c * 128 + R], in_=qn[:R, c * 128:(c + 1) * 128], identity=ident32[:R, :R]),
                         reads=[qn.b, ident32.b], writes=[ps.b])
                pv = ps[:, :].rearrange("p (c t) -> p c t", t=128)
                P.op("dve", lambda e, pv=pv, ps=ps: e.tensor_copy(out=qT32[:, :, 0:R], in_=pv[:, :, 0:R]), reads=[ps.b], writes=[qT32.b])
                P.op("act", lambda e, pv=pv, ps=ps: e.activation(out=QT[:, :, 0:R], in_=pv[:, :, 0:R], func=AF.Copy), reads=[ps.b], writes=[QT.b])

            def finish_head(R, h, O_list, nsel, sel=None):
                sel = sel_default if sel is None else sel
                first = True
                for (ps, nb, c0) in O_list:
                    ov = ps[:, 0:nb * 65].rearrange("p (b d) -> p b d", d=65)
                    P.op("dve", lambda e, ov=ov, nb=nb, c0=c0: e.tensor_tensor(out=tmpO[:R, 0:nb, :], in0=ov[:R], in1=sel[:R, c0:c0 + nb].unsqueeze(2).to_broadcast([R, nb, 65]), op=ALU.mult),
                         reads=[ps.b, sel.b], writes=[tmpO.b])
                    dst = acc if first else part
                    P.op("dve", lambda e, nb=nb, dst=dst: e.tensor_reduce(out=dst[:R, :], in_=tmpO[:R, 0:nb, :].rearrange("p b d -> p d b"), axis=AX.X, op=ALU.add),
                         reads=[tmpO.b], writes=[dst.b])
                    if not first:
                        P.op("dve", lambda e: e.tensor_add(out=acc[:R, :], in0=acc[:R, :], in1=part[:R, :]), reads=[acc.b, part.b], writes=[acc.b])
                    first = False
                P.op("dve", lambda e: e.reciprocal(out=rl[:R, :], in_=acc[:R, 64:65]), reads=[acc.b], writes=[rl.b])
                P.op("act", lambda e: e.activation(out=oatt[:R, h * 64:(h + 1) * 64], in_=acc[:R, 0:64], func=AF.Copy, scale=rl[:R, 0:1]),
                     reads=[acc.b, rl.b], writes=[oatt.b])

            def att_out(R, xt_, x1t_, dst, bdst):
                P.op("dve", lambda e: e.tensor_tensor(out=matt[:R, :], in0=oatt[:R, :], in1=sgt[:R, :], op=ALU.mult), reads=[oatt.b, sgt.b], writes=[matt.b])
                for c in range(4):
                    P.op("pe", lambda e, c=c: e.transpose(out=ps_tr[:, c, 0:R], in_=matt[:R, c * 128:(c + 1) * 128], identity=identb[:R, :R]),
                         reads=[matt.b, identb.b], writes=[ps_tr.b])
                P.op("dve", lambda e: e.tensor_copy(out=mT[:, :, 0:R], in_=ps_tr[:, 0:4, 0:R]), reads=[ps_tr.b], writes=[mT.b])
                out_proj_res(R, mT, 0, 4, woA, xt_, x1t_)
                P.dma("pool", lambda e: e.dma_start(out=dst, in_=x1t_[:R, :]), reads=[x1t_.b], writes=[bdst])

            CC = 8.0 * 1.3

            with contextlib.ExitStack() as stP:
                KT = mk(stP, "KT", [128, 4, SEQ], BF16)
                kmT = mk(stP, "kmT", [128, 4, 16]); kmp = mk(stP, "kmp", [128, 4])
                Vaug = mk(stP, "Vaug", [128, NT, 8, 65], BF16)
                PT4 = [mk(stP, "PT%d" % i, [128, NT, 128], BF16) for i in range(4)]
                P.op("pool", lambda e: e.memset(Vaug[:, :, :, 64:65], 1.0), writes=[Vaug.b])
                sel4 = [sel] + [mk(stP, "selB%d" % i, [128, 17]) for i in range(3)]
                s4i = [0]
                for t_ in sel4:
                    P.op("pool", lambda e, t_=t_: e.memset(t_[:], 1.0), writes=[t_.b])
                P.op("pool", lambda e: e.memset(gm[:], NEG), writes=[gm.b])
                for tt in range(n_tiles):
                    jb = tt // 2
                    xt_ = xt[tt % 2]; hT_ = hT[tt % 2]; kn_ = kn[tt % 2]; vsb_ = vsb[tt % 2]; x1t_ = x1t[tt % 2]
                    P.dma("sp", lambda e, xt_=xt_, tt=tt: e.dma_start(out=xt_[:], in_=xp[tt * 128:(tt + 1) * 128, :]), writes=[xt_.b])
                    qkvg(128, xt_, hT_, kn_, vsb_, k_p[tt * 128:(tt + 1) * 128, :], v_p[tt * 128:(tt + 1) * 128, :])
                    if lvl < 1:
                        continue
                    ps = next_ps("m")
                    for c in range(4):
                        P.op("pe", lambda e, c=c, ps=ps, kn_=kn_: e.transpose(out=ps[:, c * 128:(c + 1) * 128], in_=kn_[:, c * 128:(c + 1) * 128], identity=ident32[:, :]),
                             reads=[kn_.b, ident32.b], writes=[ps.b])
                    pv = ps[:, :].rearrange("p (c t) -> p c t", t=128)
                    P.op("act", lambda e, pv=pv, tt=tt: e.activation(out=KT[:, :, tt * 128:(tt + 1) * 128], in_=pv, func=AF.Copy), reads=[ps.b], writes=[KT.b])
                    if tt % 2 == 0:
                        P.op("dve", lambda e, pv=pv: e.tensor_reduce(out=kmp[:, :], in_=pv, axis=AX.X, op=ALU.add), reads=[ps.b], writes=[kmp.b])
                    else:
                        P.op("dve", lambda e, pv=pv, jb=jb: e.tensor_reduce(out=kmT[:, :, jb], in_=pv, axis=AX.X, op=ALU.add), reads=[ps.b], writes=[kmT.b])
                        P.op("dve", lambda e, jb=jb: e.tensor_tensor(out=kmT[:, :, jb], in0=kmT[:, :, jb], in1=kmp[:, :], op=ALU.add), reads=[kmT.b, kmp.b], writes=[kmT.b])
                        P.op("dve", lambda e, jb=jb: e.tensor_scalar(out=kmT[:, :, jb], in0=kmT[:, :, jb], scalar1=1.0 / 256, scalar2=None, op0=ALU.mult), reads=[kmT.b], writes=[kmT.b])
                    P.op("pool", lambda e, tt=tt, vsb_=vsb_: e.tensor_copy(out=Vaug[:, tt, :, 0:64], in_=vsb_[:, :].rearrange("p (h d) -> p h d", d=64)),
                         reads=[vsb_.b], writes=[Vaug.b])
                    nkt = tt + 1
                    if lvl < 2:
                        continue
                    ps_s4 = [ps_s[0], ps_s[1], ps_m[1], ps_m[2]]

                    def st_scores(c):
                        pb = (c % 2) * 2
                        for hl in range(2):
                            r0 = hl * 64
                            sel_ = sel4[pb + hl]
                            if jb >= 1:
                                psg = ps_m[0]
                                P.op("pe", lambda e: e.matmul(psg[:, 0:jb], lhsT=qT32[r0:r0 + 64, c, :], rhs=kmT[r0:r0 + 64, c, 0:jb], start=True, stop=True),
                                     reads=[qT32.b, kmT.b], writes=[psg.b])
                                P.op("dve", lambda e: e.tensor_copy(out=gm[:, 0:jb], in_=psg[:, 0:jb]), reads=[psg.b], writes=[gm.b])
                                P.op("dve", lambda e: e.max(out=top8[:], in_=gm[:]), reads=[gm.b], writes=[top8.b])
                                P.op("dve", lambda e: e.tensor_scalar(out=sel_[:, 0:jb], in0=gm[:, 0:jb], scalar1=top8[:, 2:3], scalar2=None, op0=ALU.is_ge),
                                     reads=[gm.b, top8.b], writes=[sel_.b])
                        for k0 in range(0, nkt, 4):
                            nk4 = min(4, nkt - k0)
                            pp = [ps_s4[s4i[0] % 4], ps_s4[(s4i[0] + 1) % 4]]
                            s4i[0] += 2
                            for i in range(nk4):
                                kt = k0 + i
                                for hl in range(2):
                                    r0 = hl * 64
                                    P.op("pe", lambda e: e.matmul(pp[hl][:, i * 128:(i + 1) * 128], lhsT=KT[r0:r0 + 64, c, kt * 128:(kt + 1) * 128],
                                                                  rhs=QT[r0:r0 + 64, c, :], start=True, stop=True),
                                         reads=[KT.b, QT.b], writes=[pp[hl].b])
                            for hl in range(2):
                                PT_ = PT4[pb + hl]
                                P.op("act", lambda e: e.activation(out=PT_[:, k0:k0 + nk4, :], in_=pp[hl][:, 0:nk4 * 128].rearrange("p (k t) -> p k t", t=128),
                                                                   func=AF.Exp, scale=0.125, bias=-CC),
                                     reads=[pp[hl].b], writes=[PT_.b])
                        for hl in range(2):
                            PT_ = PT4[pb + hl]
                            P.op("pool", lambda e: e.tensor_tensor(out=PT_[:, tt, :], in0=PT_[:, tt, :], in1=tri[:, :], op=ALU.mult),
                                 reads=[PT_.b, tri.b], writes=[PT_.b])

                    def st_pv(c):
                        pb = (c % 2) * 2
                        for hl in range(2):
                            h = 2 * c + hl
                            PT_ = PT4[pb + hl]
                            O_list = []
                            for b0 in range(0, jb + 1, 7):
                                nb = min(7, jb + 1 - b0)
                                pso = next_ps("o")
                                for bi in range(nb):
                                    b = b0 + bi
                                    kts = [kt for kt in (2 * b, 2 * b + 1) if kt <= tt]
                                    for j, kt in enumerate(kts):
                                        P.op("pe", lambda e: e.matmul(pso[:, bi * 65:(bi + 1) * 65], lhsT=PT_[:, kt, :], rhs=Vaug[:, kt, h, :],
                                                                      start=(j == 0), stop=(j == len(kts) - 1)),
                                             reads=[PT_.b, Vaug.b], writes=[pso.b])
                                O_list.append((pso, nb, b0))
                            finish_head(128, h, O_list, jb + 1, sel4[pb + hl])

                    st_scores(0)
                    for c in range(4):
                        if c + 1 < 4:
                            st_scores(c + 1)
                        st_pv(c)
                    if lvl >= 5:
                        att_out(128, xt_, x1t_, x1a[tt * 128:(tt + 1) * 128, :], b_x1a[tt])
                P.barrier()

            with contextlib.ExitStack() as stQ:
              if do_sample:
                KVpg = [mk(stQ, "KVpg%d" % i, [128, 1024]) for i in range(4)]
                KTs = [mk(stQ, "KTs%d" % i, [128, 4, 128], BF16) for i in range(2)]
                Vsb = [mk(stQ, "Vsb%d" % i, [128, 8, 65], BF16) for i in range(3)]
                PTp = [mk(stQ, "PTp%d" % i, [128, 8, 64], BF16) for i in range(2)]
                ptp_seq = [None, None]
                Osb = mk(stQ, "Osb", [64, 8, 8, 65])
                kms = mk(stQ, "kms", [128, 4, NS * 8]); kmsp = mk(stQ, "kmsp", [128, 4])
                KTn = mk(stQ, "KTn", [128, 4, 64], BF16); Vn = mk(stQ, "Vn", [64, 8, 65], BF16); PTn = mk(stQ, "PTn", [64, 8, 64], BF16)
                ptb = mk(stQ, "ptb", [128, NS * 16], I32); ptf = mk(stQ, "ptf", [128, NS * 16]); iot = mk(stQ, "iot", [128, 1], I32); iotf = mk(stQ, "iotf", [128, 1])
                idxf = mk(stQ, "idxf", [128, NS * 16]); idxi = mk(stQ, "idxi", [128, NS * 16], I32)
                idk = [mk(stQ, "idk%d" % i, [128, 1], I32) for i in range(NS * 16)]
                gs = mk(stQ, "gs", [64, 8, 8]); gsel = mk(stQ, "gsel", [64, 8, 9]); gtmp = mk(stQ, "gtmp", [64, NS, 8])
                for i in range(2):
                    P.op("pool", lambda e, i=i: e.memset(PTp[i][:], 0.0), writes=[PTp[i].b])
                for i in range(3):
                    P.op("pool", lambda e, i=i: e.memset(Vsb[i][:, :, 64:65], 1.0), writes=[Vsb[i].b])
                P.op("pool", lambda e: e.memset(Vn[:, :, 64:65], 1.0), writes=[Vn.b])
                P.op("pool", lambda e: e.memset(gsel[:], 1.0), writes=[gsel.b])
                P.dma("sp", lambda e: e.dma_start(out=ptb[:], in_=ptab.partition_broadcast(128)), writes=[ptb.b])
                P.op("pool", lambda e: e.iota(iot[:], pattern=[[0, 1]], base=0, channel_multiplier=1), writes=[iot.b])
                P.op("dve", lambda e: e.tensor_copy(out=ptf[:], in_=ptb[:]), reads=[ptb.b], writes=[ptf.b])
                P.op("dve", lambda e: e.tensor_copy(out=iotf[:], in_=iot[:]), reads=[iot.b], writes=[iotf.b])
                P.op("dve", lambda e: e.tensor_scalar(out=idxf[:], in0=ptf[:], scalar1=128.0, scalar2=iotf[:, 0:1], op0=ALU.mult, op1=ALU.add), reads=[ptf.b, iotf.b], writes=[idxf.b])
                P.op("dve", lambda e: e.tensor_copy(out=idxi[:], in_=idxf[:]), reads=[idxf.b], writes=[idxi.b])
                for col in range(NS * 16):
                    P.op("dve", lambda e, col=col: e.tensor_copy(out=idk[col][:], in_=idxi[:, col:col + 1]), reads=[idxi.b], writes=[idk[col].b])
                xt_ = xt[0]; hT_ = hT[0]; kn_ = kn[0]; vsb_ = vsb[0]; x1t_ = x1t[0]
                P.dma("sp", lambda e: e.dma_start(out=xt_[0:TS, :], in_=xs_d), writes=[xt_.b])
                qkvg(TS, xt_, hT_, kn_, vsb_, k_s, v_s)
                ps = next_ps("m")
                for c in range(4):
                    P.op("pe", lambda e, c=c, ps=ps: e.transpose(out=ps[:, c * 128:c * 128 + TS], in_=kn_[:TS, c * 128:(c + 1) * 128], identity=ident32[:TS, :TS]),
                         reads=[kn_.b, ident32.b], writes=[ps.b])
                P.op("act", lambda e, ps=ps: e.activation(out=KTn[:, :, :], in_=ps[:, :].rearrange("p (c t) -> p c t", t=128)[:, :, 0:TS], func=AF.Copy), reads=[ps.b], writes=[KTn.b])
                P.op("pool", lambda e: e.tensor_copy(out=Vn[:, :, 0:64], in_=vsb_[:TS, :].rearrange("p (h d) -> p h d", d=64)), reads=[vsb_.b], writes=[Vn.b])
                Qblk = mk(stQ, "Qblk", [128, 4, NS, 8], BF16)
                P.op("pool", lambda e: e.memset(Qblk[:], 0.0), writes=[Qblk.b])
                P.op("dve", lambda e: e.tensor_copy(out=Qblk[0:64, :, :, 0:4], in_=QT[0:64, :, 0:TS].rearrange("p c (n t) -> p c n t", t=4)), reads=[QT.b], writes=[Qblk.b])
                P.op("dve", lambda e: e.tensor_copy(out=Qblk[64:128, :, :, 4:8], in_=QT[64:128, :, 0:TS].rearrange("p c (n t) -> p c n t", t=4)), reads=[QT.b], writes=[Qblk.b])
                pages = [(b, n, pg) for b in range(8) for n in range(NS) for pg in range(2)]
                NPG = len(pages)
                st_ps = {}
                blk_state = {}

                def stage_T(i):
                    b, n, pg = pages[i]
                    col = n * 16 + 2 * b + pg
                    KVp = KVpg[i % 4]; KTs_ = KTs[i % 2]; Vsb_ = Vsb[i % 3]
                    ik = idk[col]
                    P.dma("pool", lambda e: e.indirect_dma_start(out=KVp[:], out_offset=None, in_=ckv, in_offset=bass.IndirectOffsetOnAxis(ap=ik[:, 0:1], axis=0)),
                          reads=[ik.b], writes=[KVp.b])
                    ps = next_ps("m")
                    for c in range(4):
                        P.op("pe", lambda e, c=c: e.transpose(out=ps[:, c * 128:(c + 1) * 128], in_=KVp[:, c * 128:(c + 1) * 128], identity=ident32[:, :]),
                             reads=[KVp.b, ident32.b], writes=[ps.b])
                    pv = ps[:, :].rearrange("p (c t) -> p c t", t=128)
                    P.op("act", lambda e: e.activation(out=KTs_[:, :, :], in_=pv, func=AF.Copy), reads=[ps.b], writes=[KTs_.b])
                    if pg == 0:
                        P.op("dve", lambda e: e.tensor_reduce(out=kmsp[:, :], in_=pv, axis=AX.X, op=ALU.add), reads=[ps.b], writes=[kmsp.b])
                    else:
                        kc = n * 8 + b
                        P.op("dve", lambda e: e.tensor_reduce(out=kms[:, :, kc], in_=pv, axis=AX.X, op=ALU.add), reads=[ps.b], writes=[kms.b])
                        P.op("dve", lambda e: e.tensor_tensor(out=kms[:, :, kc], in0=kms[:, :, kc], in1=kmsp[:, :], op=ALU.add), reads=[kms.b, kmsp.b], writes=[kms.b])
                        P.op("dve", lambda e: e.tensor_scalar(out=kms[:, :, kc], in0=kms[:, :, kc], scalar1=1.0 / 256, scalar2=None, op0=ALU.mult), reads=[kms.b], writes=[kms.b])
                    P.op("dve", lambda e: e.tensor_copy(out=Vsb_[:, :, 0:64], in_=KVp[:, 512:1024].rearrange("p (h d) -> p h d", d=64)), reads=[KVp.b], writes=[Vsb_.b])

                def stage_S(i):
                    b, n, pg = pages[i]
                    KTs_ = KTs[i % 2]; PTp_ = PTp[i % 2]
                    pss = next_ps("s")
                    for c in range(4):
                        P.op("pe", lambda e, c=c: e.matmul(pss[:, c * 8:(c + 1) * 8], lhsT=KTs_[:, c, :], rhs=Qblk[:, c, n, :], start=True, stop=True),
                             reads=[KTs_.b, Qblk.b], writes=[pss.b])
                    ls = ptp_seq[i % 2]
                    if ls is not None and ls != n:
                        P.op("dve", lambda e: e.memset(PTp_[:, :, ls * 4:(ls + 1) * 4], 0.0), writes=[PTp_.b])
                    ptp_seq[i % 2] = n
                    P.op("act", lambda e: e.activation(out=PTp_[:, :, n * 4:(n + 1) * 4], in_=pss[:, 0:32].rearrange("p (h t) -> p h t", t=4),
                                                       func=AF.Exp, scale=0.125, bias=-CC),
                         reads=[pss.b], writes=[PTp_.b])

                def stage_V(i):
                    b, n, pg = pages[i]
                    PTp_ = PTp[i % 2]; Vsb_ = Vsb[i % 3]
                    if b not in blk_state:
                        blk_state[b] = ([next_ps("o"), next_ps("o")], [True, True])
                    pso2, first_mm = blk_state[b]
                    for h in range(8):
                        hb = h // 4; hc = h % 4
                        pso = pso2[hb]
                        P.op("pe", lambda e, pso=pso, hc=hc, h=h, st_=first_mm[hb]: e.matmul(
                            pso[0:TS, hc * 65:(hc + 1) * 65], lhsT=PTp_[:, h, :], rhs=Vsb_[:, h, :], start=st_, stop=False, skip_group_check=True),
                             reads=[PTp_.b, Vsb_.b], writes=[pso.b])
                        first_mm[hb] = False
                    if n == NS - 1 and pg == 1:
                        for hb in range(2):
                            P.op("dve", lambda e, hb=hb: e.tensor_copy(out=Osb[:, b, hb * 4:(hb + 1) * 4, :], in_=pso2[hb][0:TS, 0:260].rearrange("p (h d) -> p h d", d=65)),
                                 reads=[pso2[hb].b], writes=[Osb.b])

                for i in range(NPG + 2):
                    if i < NPG:
                        stage_T(i)
                    if 0 <= i - 1 < NPG:
                        stage_S(i - 1)
                    if 0 <= i - 2 < NPG:
                        stage_V(i - 2)
                for h in range(8):
                    c = h // 2; r0 = (h % 2) * 64
                    pss = next_ps("s")
                    P.op("pe", lambda e, pss=pss, c=c, r0=r0: e.matmul(pss[0:TS, 0:TS], lhsT=KTn[r0:r0 + 64, c, :], rhs=QT[r0:r0 + 64, c, 0:TS], start=True, stop=True),
                         reads=[KTn.b, QT.b], writes=[pss.b])
                    P.op("act", lambda e, pss=pss, h=h: e.activation(out=PTn[:, h, :], in_=pss[0:TS, 0:TS], func=AF.Exp, scale=0.125, bias=-CC), reads=[pss.b], writes=[PTn.b])
                    P.op("pool", lambda e, h=h: e.tensor_tensor(out=PTn[:, h, :], in0=PTn[:, h, :], in1=mnew[:, :], op=ALU.mult), reads=[PTn.b, mnew.b], writes=[PTn.b])
                    pso = next_ps("o")
                    P.op("pe", lambda e, pso=pso, h=h: e.matmul(pso[0:TS, 0:65], lhsT=PTn[:, h, :], rhs=Vn[:, h, :], start=True, stop=True), reads=[PTn.b, Vn.b], writes=[pso.b])
                    psg = next_ps("m")
                    P.op("pe", lambda e, psg=psg, c=c, r0=r0: e.matmul(psg[0:TS, 0:NS * 8], lhsT=qT32[r0:r0 + 64, c, 0:TS], rhs=kms[r0:r0 + 64, c, :], start=True, stop=True),
                         reads=[qT32.b, kms.b], writes=[psg.b])
                    P.op("dve", lambda e, psg=psg: e.tensor_tensor(out=gtmp[:, :, :], in0=psg[0:TS, 0:NS * 8].rearrange("p (n b) -> p n b", b=8),
                                                                  in1=seqm[:, :].unsqueeze(2).to_broadcast([TS, NS, 8]), op=ALU.mult), reads=[psg.b, seqm.b], writes=[gtmp.b])
                    P.op("dve", lambda e, h=h: e.tensor_reduce(out=gs[:, h, :], in_=gtmp[:, :, :].rearrange("p n b -> p b n"), axis=AX.X, op=ALU.add), reads=[gtmp.b], writes=[gs.b])
                    P.op("dve", lambda e, h=h: e.max(out=top8[0:TS, :], in_=gs[:, h, :]), reads=[gs.b], writes=[top8.b])
                    P.op("dve", lambda e, h=h: e.tensor_scalar(out=gsel[:, h, 0:8], in0=gs[:, h, :], scalar1=top8[0:TS, 2:3], scalar2=None, op0=ALU.is_ge), reads=[gs.b, top8.b], writes=[gsel.b])
                    P.op("dve", lambda e, h=h: e.tensor_tensor(out=tmpO[0:TS, 0:7, :], in0=Osb[:, 0:7, h, :], in1=gsel[:, h, 0:7].unsqueeze(2).to_broadcast([TS, 7, 65]), op=ALU.mult),
                         reads=[Osb.b, gsel.b], writes=[tmpO.b])
                    P.op("dve", lambda e: e.tensor_reduce(out=acc[0:TS, :], in_=tmpO[0:TS, 0:7, :].rearrange("p b d -> p d b"), axis=AX.X, op=ALU.add), reads=[tmpO.b], writes=[acc.b])
                    P.op("dve", lambda e, h=h: e.scalar_tensor_tensor(out=acc[0:TS, :], in0=Osb[:, 7, h, :], scalar=gsel[:, h, 7:8], in1=acc[0:TS, :], op0=ALU.mult, op1=ALU.add),
                         reads=[Osb.b, gsel.b, acc.b], writes=[acc.b])
                    P.op("dve", lambda e, pso=pso: e.tensor_tensor(out=acc[0:TS, :], in0=pso[0:TS, 0:65], in1=acc[0:TS, :], op=ALU.add), reads=[pso.b, acc.b], writes=[acc.b])
                    P.op("dve", lambda e: e.reciprocal(out=rl[0:TS, :], in_=acc[0:TS, 64:65]), reads=[acc.b], writes=[rl.b])
                    P.op("dve", lambda e, h=h: e.tensor_scalar(out=oatt[0:TS, h * 64:(h + 1) * 64], in0=acc[0:TS, 0:64], scalar1=rl[0:TS, 0:1], scalar2=None, op0=ALU.mult),
                         reads=[acc.b, rl.b], writes=[oatt.b])
                att_out(TS, xt_, x1t_, x1a[SEQ:SEQ + TS, :], b_x1a[NT])
                P.barrier()
        if lvl >= 6:
          with contextlib.ExitStack() as stB:
            wC = mk(stB, "wC", [128, 8, 1536], BF16)
            woC = mk(stB, "woC", [128, 4, 1024], BF16)
            Dg = mk(stB, "Dg", [128, 124, 128], BF16)
            with contextlib.ExitStack() as stS:
                stage = [mk(stS, "stgB%d" % i, [128, 2048]) for i in range(2)]
                load_w(stS, wC, w_in_0, 0, 8, 2048, 1536, n0, stage)
                load_w(stS, woC, w_out_0, 512, 4, 0, 1024, None, stage)
                P.barrier()
            for i in range(124):
                P.op("dve", lambda e, i=i: e.tensor_scalar(out=Dg[:, i, :], in0=identb[:, :], scalar1=cwT[:, i:i + 1], scalar2=None, op0=ALU.mult),
                     reads=[identb.b, cwT.b], writes=[Dg.b])
            xt = [mk(stB, "xtB%d" % i, [128, 1024]) for i in range(2)]
            xsb = [mk(stB, "xsbB%d" % i, [128, 1024], BF16) for i in range(2)]
            hTs = [mk(stB, "hTB%d" % i, [128, 8, 512], BF16) for i in range(2)]
            hT = hTs[0]
            uT = mk(stB, "uT", [128, 4, 542], BF16)
            sgc = mk(stB, "sgc", [128, 4, 512])
            yT = mk(stB, "yT", [128, 4, 512]); ysq = mk(stB, "ysq", [128, 4, 512])
            sigt = mk(stB, "sigt", [128, 512]); mean = mk(stB, "mean", [128, 512]); msq = mk(stB, "msq", [128, 512]); rstdB = mk(stB, "rstdB", [128, 512])
            t1 = mk(stB, "t1B", [128, 512])
            mTc = mk(stB, "mTc", [128, 4, 512], BF16)
            x1aT = [mk(stB, "x1aT%d" % i, [128, 1024]) for i in range(2)]
            x1T = [mk(stB, "x1T%d" % i, [128, 1024]) for i in range(2)]
            utm = mk(stB, "utm", [128, 512]); sigm = mk(stB, "sigm", [128, 512])
            sm_rms = smalls(stB, 3, 1)
            P.op("pool", lambda e: e.memset(uT[:, :, 0:30], 0.0), writes=[uT.b])

            def u_tokmajor(R, col0):
                ps = next_ps("m"); proj_tm(ps, R, hT, col0, wC, 512, 512)
                P.op("act", lambda e, ps=ps: e.activation(out=sigm[:R, :], in_=ps[:R, :], func=AF.Sigmoid), reads=[ps.b], writes=[sigm.b])
                ps = next_ps("m"); proj_tm(ps, R, hT, col0, wC, 0, 512)
                P.op("dve", lambda e, ps=ps: e.tensor_tensor(out=utm[:R, :], in0=ps[:R, :], in1=sigm[:R, :], op=ALU.mult), reads=[ps.b, sigm.b], writes=[utm.b])

            def ln_gate(N):
                ps1 = next_ps("m"); ps2 = next_ps("m")
                for c in range(4):
                    P.op("pe", lambda e, c=c: e.matmul(ps1[:, 0:N], lhsT=ones32[:, :], rhs=yT[:, c, 0:N], start=(c == 0), stop=(c == 3)), reads=[ones32.b, yT.b], writes=[ps1.b])
                for c in range(4):
                    P.op("pe", lambda e, c=c: e.matmul(ps2[:, 0:N], lhsT=ones32[:, :], rhs=ysq[:, c, 0:N], start=(c == 0), stop=(c == 3)), reads=[ones32.b, ysq.b], writes=[ps2.b])
                P.op("dve", lambda e: e.tensor_scalar(out=mean[:, 0:N], in0=ps1[:, 0:N], scalar1=1.0 / 512, scalar2=None, op0=ALU.mult), reads=[ps1.b], writes=[mean.b])
                P.op("dve", lambda e: e.tensor_tensor(out=msq[:, 0:N], in0=mean[:, 0:N], in1=mean[:, 0:N], op=ALU.mult), reads=[mean.b], writes=[msq.b])
                P.op("dve", lambda e: e.scalar_tensor_tensor(out=msq[:, 0:N], in0=ps2[:, 0:N], scalar=1.0 / 512, in1=msq[:, 0:N], op0=ALU.mult, op1=ALU.subtract),
                     reads=[ps2.b, msq.b], writes=[msq.b])
                P.op("act", lambda e: e.activation(out=msq[:, 0:N], in_=msq[:, 0:N], func=AF.Sqrt, bias=EPS), reads=[msq.b], writes=[msq.b])
                P.op("dve", lambda e: e.reciprocal(out=rstdB[:, 0:N], in_=msq[:, 0:N]), reads=[msq.b], writes=[rstdB.b])
                for c in range(4):
                    P.op("dve", lambda e, c=c: e.tensor_tensor(out=t1[:, 0:N], in0=yT[:, c, 0:N], in1=mean[:, 0:N], op=ALU.subtract), reads=[yT.b, mean.b], writes=[t1.b])
                    P.op("dve", lambda e, c=c: e.tensor_tensor(out=t1[:, 0:N], in0=t1[:, 0:N], in1=rstdB[:, 0:N], op=ALU.mult), reads=[t1.b, rstdB.b], writes=[t1.b])
                    P.op("act", lambda e, c=c: e.activation(out=t1[:, 0:N], in_=t1[:, 0:N], func=AF.Silu, scale=lng[:, c:c + 1], bias=lnb[:, c:c + 1]),
                         reads=[t1.b, lng.b, lnb.b], writes=[t1.b])
                    P.op("dve", lambda e, c=c: e.tensor_tensor(out=mTc[:, c, 0:N], in0=t1[:, 0:N], in1=sgc[:, c, 0:N], op=ALU.mult), reads=[t1.b, sgc.b], writes=[mTc.b])

            n_st = (n_tiles + 3) // 4
            def load_rms_B(ST_):
                for j in range(4):
                    tt = ST_ * 4 + j
                    xt_ = xt[tt % 2]
                    P.dma("sp", lambda e: e.dma_start(out=xt_[:], in_=xp[tt * 128:(tt + 1) * 128, :]), writes=[xt_.b])
                    rms_T(xt_, 128, xsb, hTs[ST_ % 2], j * 128, sm_rms)

            load_rms_B(0)
            for ST in range(n_st):
                hT = hTs[ST % 2]
                for c in range(4):
                    ps = next_ps("m"); proj_fm(ps, 512, hT, 0, wC, 512 + c * 128)
                    P.op("act", lambda e, ps=ps: e.activation(out=sigt[:, :], in_=ps[:, :], func=AF.Sigmoid), reads=[ps.b], writes=[sigt.b])
                    ps = next_ps("m"); proj_fm(ps, 512, hT, 0, wC, c * 128)
                    P.op("dve", lambda e, ps=ps, c=c: e.tensor_tensor(out=uT[:, c, 30:542], in0=ps[:, :], in1=sigt[:, :], op=ALU.mult), reads=[ps.b, sigt.b], writes=[uT.b])
                    ps = next_ps("m"); proj_fm(ps, 512, hT, 0, wC, 1024 + c * 128)
                    P.op("act", lambda e, ps=ps, c=c: e.activation(out=sgc[:, c, :], in_=ps[:, :], func=AF.Silu), reads=[ps.b], writes=[sgc.b])
                for c in range(4):
                    ps = next_ps("m")
                    for j in range(31):
                        P.op("pe", lambda e, ps=ps, c=c, j=j: e.matmul(ps[:, :], lhsT=Dg[:, c * 31 + j, :], rhs=uT[:, c, j:j + 512], start=(j == 0), stop=(j == 30)),
                             reads=[Dg.b, uT.b], writes=[ps.b])
                    P.op("dve", lambda e, ps=ps, c=c: e.tensor_scalar(out=yT[:, c, :], in0=ps[:, :], scalar1=cb[:, c:c + 1], scalar2=None, op0=ALU.add), reads=[ps.b, cb.b], writes=[yT.b])
                    P.op("act", lambda e, c=c: e.activation(out=ysq[:, c, :], in_=yT[:, c, :], func=AF.Square), reads=[yT.b], writes=[ysq.b])
                P.op("pool", lambda e: e.tensor_copy(out=sigt[:, 0:120].rearrange("p (c r) -> p c r", r=30), in_=uT[:, :, 512:542]), reads=[uT.b], writes=[sigt.b])
                P.op("pool", lambda e: e.tensor_copy(out=uT[:, :, 0:30], in_=sigt[:, 0:120].rearrange("p (c r) -> p c r", r=30)), reads=[sigt.b], writes=[uT.b])
                if ST == NT // 4 - 1:
                    u_tokmajor(128, 384)
                    P.dma("pool", lambda e: e.dma_start(out=conv_p, in_=utm[98:128, :]), reads=[utm.b], writes=[])
                if ST + 1 < n_st:
                    load_rms_B(ST + 1)
                ln_gate(512)
                for j in range(4):
                    tt = ST * 4 + j
                    xa = x1aT[tt % 2]; xo = x1T[tt % 2]
                    P.dma("sp", lambda e, xa=xa, tt=tt: e.dma_start(out=xa[:], in_=x1a[tt * 128:(tt + 1) * 128, :]), reads=[b_x1a[tt]], writes=[xa.b])
                    out_proj_res(128, mTc, j * 128, 4, woC, xa, xo)
                    P.dma("pool", lambda e, xo=xo, tt=tt: e.dma_start(out=x1b[tt * 128:(tt + 1) * 128, :], in_=xo[:]), reads=[xo.b], writes=[b_x1b[tt]])
            hT = hTs[0]
            if do_sample:
              with contextlib.ExitStack() as stQ:
                stg = [mk(stQ, "stcg%d" % i, [120, 512]) for i in range(2)]
                upT = mk(stQ, "upT", [128, 4, NS, 34])
                xt_ = xt[0]
                P.dma("sp", lambda e: e.dma_start(out=xt_[0:TS, :], in_=xs_d), writes=[xt_.b])
                rms_T(xt_, TS, xsb, hT, 0, sm_rms)
                stc2 = stc.rearrange("n r c -> (n r) c")
                for g in range(4):
                    sg_ = stg[g % 2]
                    P.dma("sp", lambda e, sg_=sg_, g=g: e.dma_start(out=sg_[:, :], in_=stc2[g * 120:(g + 1) * 120, :]), writes=[sg_.b])
                    ps = next_ps("m")
                    for c in range(4):
                        P.op("pe", lambda e, ps=ps, c=c, sg_=sg_: e.transpose(out=ps[:, c * 120:(c + 1) * 120], in_=sg_[:, c * 128:(c + 1) * 128], identity=ident32[0:120, 0:120]),
                             reads=[sg_.b, ident32.b], writes=[ps.b])
                    for c in range(4):
                        P.op("dve", lambda e, ps=ps, c=c, g=g: e.tensor_copy(out=upT[:, c, g * 4:(g + 1) * 4, 0:30], in_=ps[:, c * 120:(c + 1) * 120].rearrange("p (n r) -> p n r", r=30)),
                             reads=[ps.b], writes=[upT.b])
                P.dma("pool", lambda e: e.dma_start(out=conv_s[:, 0:26, :], in_=stc[:, 4:30, :]), writes=[])
                for c in range(4):
                    ps = next_ps("m"); proj_fm(ps, TS, hT, 0, wC, 512 + c * 128)
                    P.op("act", lambda e, ps=ps: e.activation(out=sigt[:, 0:TS], in_=ps[:, 0:TS], func=AF.Sigmoid), reads=[ps.b], writes=[sigt.b])
                    ps = next_ps("m"); proj_fm(ps, TS, hT, 0, wC, c * 128)
                    P.op("dve", lambda e, ps=ps, c=c: e.tensor_tensor(out=upT[:, c, :, 30:34], in0=ps[:, 0:TS].rearrange("p (n t) -> p n t", t=4),
                                                                    in1=sigt[:, 0:TS].rearrange("p (n t) -> p n t", t=4), op=ALU.mult), reads=[ps.b, sigt.b], writes=[upT.b])
                    ps = next_ps("m"); proj_fm(ps, TS, hT, 0, wC, 1024 + c * 128)
                    P.op("act", lambda e, ps=ps, c=c: e.activation(out=sgc[:, c, 0:TS], in_=ps[:, 0:TS], func=AF.Silu), reads=[ps.b], writes=[sgc.b])
                for c in range(4):
                    yv = yT[:, c, 0:TS].rearrange("p (n t) -> p n t", t=4)
                    P.op("dve", lambda e, c=c, yv=yv: e.tensor_scalar(out=yv, in0=upT[:, c, :, 0:4], scalar1=cwT[:, c * 31:c * 31 + 1], scalar2=cb[:, c:c + 1], op0=ALU.mult, op1=ALU.add),
                         reads=[upT.b, cwT.b, cb.b], writes=[yT.b])
                    for j in range(1, 31):
                        P.op("dve", lambda e, c=c, j=j, yv=yv: e.scalar_tensor_tensor(out=yv, in0=upT[:, c, :, j:j + 4], scalar=cwT[:, c * 31 + j:c * 31 + j + 1], in1=yv, op0=ALU.mult, op1=ALU.add),
                             reads=[upT.b, cwT.b, yT.b], writes=[yT.b])
                    P.op("act", lambda e, c=c: e.activation(out=ysq[:, c, 0:TS], in_=yT[:, c, 0:TS], func=AF.Square), reads=[yT.b], writes=[ysq.b])
                ln_gate(TS)
                xa = x1aT[0]; xo = x1T[0]
                P.dma("sp", lambda e: e.dma_start(out=xa[0:TS, :], in_=x1a[SEQ:SEQ + TS, :]), reads=[b_x1a[NT]], writes=[xa.b])
                out_proj_res(TS, mTc, 0, 4, woC, xa, xo)
                P.dma("pool", lambda e: e.dma_start(out=x1b[SEQ:SEQ + TS, :], in_=xo[0:TS, :]), reads=[xo.b], writes=[b_x1b[NT]])
                u_tokmajor(TS, 0)
                for n in range(NS):
                    P.dma("pool", lambda e, n=n: e.dma_start(out=conv_s[n, 26:30, :], in_=utm[n * 4:(n + 1) * 4, :]), reads=[utm.b], writes=[])
            P.barrier()

        if lvl >= 7:
          with contextlib.ExitStack() as stC:
            w1 = mk(stC, "w1", [128, 8, 4096], BF16)
            wo1 = mk(stC, "wo1", [128, 8, 1024], BF16)
            with contextlib.ExitStack() as stS:
                stage = [mk(stS, "stgC%d" % i, [128, 2048]) for i in range(2)]
                load_w(stS, w1, w_in_1, 0, 8, 0, 4096, n1, stage)
                load_w(stS, wo1, w_out_1, 0, 8, 0, 1024, None, stage)
                P.barrier()
            xt = [mk(stC, "xtC%d" % i, [128, 1024]) for i in range(2)]
            xsb = [mk(stC, "xsbC0", [128, 1024], BF16)]
            g_extra = mk(stC, "g_extra", [128, 512])
            hT = mk(stC, "hTC", [128, 8, 512], BF16)
            qT = mk(stC, "qTC", [128, 8, 512], BF16); kT = mk(stC, "kTC", [128, 8, 512], BF16)
            dec = mk(stC, "dec", [128, 8, 16])
            sgm = mk(stC, "sgmC", [128, 512]); lf = mk(stC, "lfC", [128, 512]); bcum = mk(stC, "bcumC", [128, 512])
            ep = mk(stC, "epC", [128, 512]); em = mk(stC, "emC", [128, 512]); kk = mk(stC, "kkC", [128, 512]); sqq = mk(stC, "sqqC", [128, 512])
            vtm = mk(stC, "vtm", [128, 4, 1024], BF16); sg1 = mk(stC, "sg1", [128, 4, 1024], BF16)
            ktm = mk(stC, "ktm", [128, 8, 128], BF16)
            S = mk(stC, "S", [128, 8, 128]); Stmp = mk(stC, "Stmp", [128, 8, 128]); Sbf = [mk(stC, "Sbf%d" % i, [128, 8, 128], BF16) for i in range(2)]
            ATm = mk(stC, "ATm", [128, 8, 128], BF16)
            osb = mk(stC, "osb", [128, 1024]); osq = mk(stC, "osq", [128, 512]); gsets = [(sgm, lf, ep, kk, sqq), (bcum, em, osq, g_extra, sqq)]; mo = mk(stC, "mo", [128, 1024], BF16); mT1 = mk(stC, "mT1", [128, 8, 128], BF16)
            yt = [mk(stC, "ytC0", [128, 1024])] * 2
            sm_rms = smalls(stC, 3, 1); sm_o = smalls(stC, 3, 4)
            bA = [ps_m[0], ps_m[1]]; bO = [ps_m[2], ps_s[0]]; bKV = [ps_s[1], ps_o[0]]

            gate_pend = [None]

            def flush_gates():
                if gate_pend[0] is not None:
                    gate_pend[0]()
                    gate_pend[0] = None

            def gates(N, h, rmask):
                psq = [ps_o[1], ps_o[0], ps_s[1]][h % 3]; psf = [ps_m[0], ps_m[1], ps_m[2], ps_s[0]][h % 4]
                sgm, lf, ep, kk, sqq = gsets[h % 2]
                bcum = sgm; em = lf
                proj_fm(psf, N, hT, 0, w1, 1024 + h * 128)
                P.op("act", lambda e: e.activation(out=sgm[:, 0:N], in_=psf[:, 0:N], func=AF.Sigmoid), reads=[psf.b], writes=[sgm.b])
                P.op("act", lambda e: e.activation(out=lf[:, 0:N], in_=sgm[:, 0:N], func=AF.Identity, scale=oml[:, h:h + 1], bias=lbv[:, h:h + 1]),
                     reads=[sgm.b, oml.b, lbv.b], writes=[lf.b])
                P.op("pool", lambda e: e.tensor_scalar(out=kk[:, 0:N], in0=sgm[:, 0:N], scalar1=noml[:, h:h + 1], scalar2=oml[:, h:h + 1], op0=ALU.mult, op1=ALU.add),
                     reads=[sgm.b, noml.b, oml.b], writes=[kk.b])
                P.op("pool", lambda e: e.tensor_tensor(out=bcum[:, 0:N], in0=lf[:, 0:N], in1=rmask[:, 0:N], op=ALU.mult), reads=[lf.b, rmask.b], writes=[bcum.b])
                flush_gates()
                P.op("dve", lambda e: e.tensor_tensor_scan(out=ep[:, 0:N], data0=lf[:, 0:N], data1=bcum[:, 0:N], initial=1.0, op0=ALU.mult, op1=ALU.max),
                     reads=[bcum.b, lf.b], writes=[ep.b])
                P.op("dve", lambda e: e.reciprocal(out=em[:, 0:N], in_=ep[:, 0:N]), reads=[ep.b], writes=[em.b])
                proj_fm(psq, N, hT, 0, w1, h * 128)
                P.op("act", lambda e: e.activation(out=sqq[:, 0:N], in_=psq[:, 0:N], func=AF.Silu), reads=[psq.b], writes=[sqq.b])
                P.op("dve", lambda e: e.tensor_tensor(out=qT[:, h, 0:N], in0=sqq[:, 0:N], in1=ep[:, 0:N], op=ALU.mult), reads=[sqq.b, ep.b], writes=[qT.b])
                gate_pend[0] = (lambda: P.op("pool", lambda e: e.tensor_tensor(out=kT[:, h, 0:N], in0=kk[:, 0:N], in1=em[:, 0:N], op=ALU.mult), reads=[kk.b, em.b], writes=[kT.b]))
                return ep

            def vg_tm(R, col0, j):
                for n in range(2):
                    ps = ps_o[1] if n == 0 else ps_m[0]
                    proj_tm(ps, R, hT, col0, w1, 2048 + n * 512, 512)
                    P.op("act", lambda e, ps=ps, n=n: e.activation(out=vtm[:R, j, n * 512:(n + 1) * 512], in_=ps[:R, :], func=AF.Copy), reads=[ps.b], writes=[vtm.b])
                for n in range(2):
                    ps = ps_m[1] if n == 0 else ps_m[2]
                    proj_tm(ps, R, hT, col0, w1, 3072 + n * 512, 512)
                    P.op("act", lambda e, ps=ps, n=n: e.activation(out=sg1[:R, j, n * 512:(n + 1) * 512], in_=ps[:R, :], func=AF.Silu), reads=[ps.b], writes=[sg1.b])

            def finish_part1(R, j, mo_):
                for hb in range(2):
                    head_norm(bO[hb], R, 4, 128, _og_half[hb], _osb_half[hb], sm_o, osq)
                P.op("dve", lambda e: e.tensor_tensor(out=mo_[:R, :], in0=osb[:R, :], in1=sg1[:R, j, :], op=ALU.mult), reads=[osb.b, sg1.b], writes=[mo_.b])

            def finish_part2(R, mo_, xres, yt_, ydst):
                for k in range(8):
                    P.op("pe", lambda e, k=k: e.transpose(out=ps_tr[:, k, 0:R], in_=mo_[:R, k * 128:(k + 1) * 128], identity=identb[:R, :R]), reads=[mo_.b, identb.b], writes=[ps_tr.b])
                P.op("dve", lambda e: e.tensor_copy(out=mT1[:, :, 0:R], in_=ps_tr[:, :, 0:R]), reads=[ps_tr.b], writes=[mT1.b])
                out_proj_res(R, mT1, 0, 8, wo1, xres, yt_)
                P.dma("pool", lambda e: e.dma_start(out=ydst, in_=yt_[:R, :]), reads=[yt_.b], writes=[])

            def finish_tokens(R, j, xres, yt_, ydst):
                finish_part1(R, j, mo)
                finish_part2(R, mo, xres, yt_, ydst)

            class _V:
                def __init__(self, parent, c0):
                    self.p = parent; self.c0 = c0; self.b = parent.b
                def __getitem__(self, k):
                    r, cs = k
                    return self.p.t[r, self.c0 + (cs.start or 0):self.c0 + cs.stop]
            _og_half = [_V(og, 0), _V(og, 512)]
            _osb_half = [_V(osb, 0), _V(osb, 512)]

            sm64 = mk(stC, "sm64", [128, 512]); sm4 = mk(stC, "sm4", [128, 64])
            P.op("dve", lambda e: e.tensor_scalar(out=sm64[:, :], in0=rm64[:, :], scalar1=-1.0, scalar2=1.0, op0=ALU.mult, op1=ALU.add), reads=[rm64.b], writes=[sm64.b])
            P.op("dve", lambda e: e.tensor_scalar(out=sm4[:, :], in0=rm4[:, :], scalar1=-1.0, scalar2=1.0, op0=ALU.mult, op1=ALU.add), reads=[rm4.b], writes=[sm4.b])
            P.op("pool", lambda e: e.memset(S[:], 0.0), writes=[S.b])
            P.op("pool", lambda e: e.memset(Sbf[0][:], 0.0), writes=[Sbf[0].b])
            sbi = 0
            pend = None
            mo2 = [mo, mk(stC, "mo_b", [128, 1024], BF16)]
            n_st = (n_tiles + 3) // 4
            for ST in range(n_st):
                for j in range(4):
                    tt = ST * 4 + j
                    xt_ = xt[tt % 2]
                    P.dma("sp", lambda e, xt_=xt_, tt=tt: e.dma_start(out=xt_[:], in_=x1b[tt * 128:(tt + 1) * 128, :]), reads=[b_x1b[tt]], writes=[xt_.b])
                    rms_T(xt_, 128, xsb, hT, j * 128, sm_rms)
                for h in range(8):
                    ep_ = gates(512, h, sm64)
                    P.op("dve", lambda e, h=h: e.tensor_copy(out=dec[:, h, 0:8], in_=ep_[:, :].rearrange("p (c t) -> p c t", t=64)[:, :, 63]), reads=[ep_.b], writes=[dec.b])
                flush_gates()
                for j in range(4):
                    vg_tm(128, j * 128, j)
                for j in range(4):
                    tt = ST * 4 + j
                    c0 = j * 128
                    for h in range(8):
                        P.op("pe", lambda e, h=h, c0=c0: e.transpose(out=ps_tr[:, h, :], in_=kT[:, h, c0:c0 + 128], identity=identb[:, :]), reads=[kT.b, identb.b], writes=[ps_tr.b])
                    P.op("act", lambda e: e.activation(out=ktm[:, :, :], in_=ps_tr[:, :, :], func=AF.Copy), reads=[ps_tr.b], writes=[ktm.b])
                    def em_inter(ch, Sb):
                        r0 = ch * 64
                        for h in range(8):
                            po = bO[h // 4]
                            P.op("pe", lambda e: e.matmul(po[r0:r0 + 64, (h % 4) * 128:(h % 4 + 1) * 128], lhsT=qT[:, h, c0 + r0:c0 + r0 + 64], rhs=Sb[:, h, :],
                                                          start=False, stop=True, skip_group_check=True), reads=[qT.b, Sb.b], writes=[po.b])

                    def em_KV(ch):
                        r0 = ch * 64
                        for h in range(8):
                            pk = bKV[h // 4]
                            P.op("pe", lambda e: e.matmul(pk[:, (h % 4) * 128:(h % 4 + 1) * 128], lhsT=ktm[r0:r0 + 64, h, :], rhs=vtm[r0:r0 + 64, j, h * 128:(h + 1) * 128],
                                                          start=True, stop=True), reads=[ktm.b, vtm.b], writes=[pk.b])

                    def em_update(ch, Sn):
                        cidx = j * 2 + ch
                        for hb in range(2):
                            P.op("dve", lambda e: e.tensor_tensor(out=Stmp[:, hb * 4:(hb + 1) * 4, :], in0=bKV[hb][:, :].rearrange("p (h v) -> p h v", v=128), in1=S[:, hb * 4:(hb + 1) * 4, :], op=ALU.add),
                                 reads=[bKV[hb].b, S.b], writes=[Stmp.b])
                        P.op("dve", lambda e: e.tensor_tensor(out=S[:, :, :], in0=Stmp[:, :, :], in1=dec[:, :, cidx:cidx + 1].to_broadcast([128, 8, 128]), op=ALU.mult),
                             reads=[Stmp.b, dec.b], writes=[S.b])
                        P.op("act", lambda e: e.activation(out=Sn[:, :, :], in_=S[:, :, :], func=AF.Copy), reads=[S.b], writes=[Sn.b])

                    for h in range(8):
                        pa = bA[h // 4]
                        P.op("pe", lambda e: e.matmul(pa[:, (h % 4) * 128:(h % 4 + 1) * 128], lhsT=kT[:, h, c0:c0 + 128], rhs=qT[:, h, c0:c0 + 128], start=True, stop=True),
                             reads=[kT.b, qT.b], writes=[pa.b])
                    em_KV(0)
                    for hb in range(2):
                        P.op("dve", lambda e: e.tensor_tensor(out=ATm[:, hb * 4:(hb + 1) * 4, :], in0=bA[hb][:, :].rearrange("p (h t) -> p h t", t=128),
                                                              in1=blkc[:, :].unsqueeze(1).to_broadcast([128, 4, 128]), op=ALU.mult), reads=[bA[hb].b, blkc.b], writes=[ATm.b])
                    for h in range(8):
                        po = bO[h // 4]
                        P.op("pe", lambda e: e.matmul(po[:, (h % 4) * 128:(h % 4 + 1) * 128], lhsT=ATm[:, h, :], rhs=vtm[:, j, h * 128:(h + 1) * 128],
                                                      start=(h % 4 == 0), stop=False, skip_group_check=True), reads=[ATm.b, vtm.b], writes=[po.b])
                    S_cur = Sbf[sbi % 2]; S_mid = Sbf[(sbi + 1) % 2]
                    em_inter(0, S_cur)
                    em_update(0, S_mid)
                    em_KV(1)
                    em_inter(1, S_mid)
                    em_update(1, S_cur)
                    mo_ = mo2[tt % 2]
                    finish_part1(128, j, mo_)
                    if pend is not None:
                        pend()
                    xr = xt[tt % 2]
                    P.dma("sp", lambda e, xr=xr, tt=tt: e.dma_start(out=xr[:], in_=x1b[tt * 128:(tt + 1) * 128, :]), reads=[b_x1b[tt]], writes=[xr.b])
                    pend = (lambda mo_=mo_, xr=xr, tt=tt: finish_part2(128, mo_, xr, yt[tt % 2], y_p[tt * 128:(tt + 1) * 128, :]))
                    if j == 3:
                        pend(); pend = None
            P.dma("pool", lambda e: e.dma_start(out=hg_p.rearrange("h k v -> k h v"), in_=S[:, :, :]), reads=[S.b], writes=[])
            if do_sample:
              with contextlib.ExitStack() as stQ:
                qpad = [mk(stQ, "qpad%d" % i, [128, 8, 64], BF16) for i in range(2)]
                S0 = [mk(stQ, "S0_%d" % i, [128, 8, 128]) for i in range(2)]
                S0b = Sbf
                class _R:
                    def __init__(self, parent):
                        self.p = parent; self.b = parent.b
                    def __getitem__(self, k):
                        return self.p.t[:, :].rearrange("p (h v) -> p h v", v=128)
                Sn_ = [S, _R(osb)]
                vmk = [mo, mo]
                decs = mk(stQ, "decs", [128, 8, NS])
                xt_ = xt[0]
                P.dma("sp", lambda e: e.dma_start(out=xt_[0:TS, :], in_=x1b[SEQ:SEQ + TS, :]), reads=[b_x1b[NT]], writes=[xt_.b])
                rms_T(xt_, TS, xsb, hT, 0, sm_rms)
                for i in range(2):
                    P.op("pool", lambda e, i=i: e.memset(qpad[i][:], 0.0), writes=[qpad[i].b])
                for h in range(8):
                    ep_ = gates(TS, h, sm4)
                    P.op("dve", lambda e, h=h: e.tensor_copy(out=decs[:, h, :], in_=ep_[:, 0:TS].rearrange("p (n t) -> p n t", t=4)[:, :, 3]), reads=[ep_.b], writes=[decs.b])
                flush_gates()
                vg_tm(TS, 0, 0)
                for h in range(8):
                    P.op("pe", lambda e, h=h: e.transpose(out=ps_tr[0:TS, h, :], in_=kT[:, h, 0:TS], identity=identb[:, :]), reads=[kT.b, identb.b], writes=[ps_tr.b])
                P.op("act", lambda e: e.activation(out=ktm[0:TS, :, :], in_=ps_tr[0:TS, :, :], func=AF.Copy), reads=[ps_tr.b], writes=[ktm.b])
                pa = bA[0]
                for h in range(8):
                    P.op("pe", lambda e, h=h: e.matmul(pa[0:TS, h * 64:(h + 1) * 64], lhsT=kT[:, h, 0:TS], rhs=qT[:, h, 0:TS], start=True, stop=True), reads=[kT.b, qT.b], writes=[pa.b])
                P.op("dve", lambda e: e.tensor_tensor(out=ATm[0:TS, :, 0:TS], in0=pa[0:TS, :].rearrange("p (h t) -> p h t", t=64), in1=mnew[:, :].unsqueeze(1).to_broadcast([TS, 8, TS]), op=ALU.mult),
                     reads=[pa.b, mnew.b], writes=[ATm.b])
                for h in range(8):
                    po = bO[h // 4]
                    P.op("pe", lambda e, h=h, po=po: e.matmul(po[0:TS, (h % 4) * 128:(h % 4 + 1) * 128], lhsT=ATm[0:TS, h, 0:TS], rhs=vtm[0:TS, 0, h * 128:(h + 1) * 128],
                                                              start=(h % 4 == 0), stop=False, skip_group_check=True), reads=[ATm.b, vtm.b], writes=[po.b])
                sth2 = sth.rearrange("n h k v -> n k h v")
                hg2 = hg_s.rearrange("n h k v -> n k h v")
                for n in range(NS):
                    s0 = S0[n % 2]; s0b = S0b[n % 2]; sn = Sn_[n % 2]; vm = vmk[n % 2]; qp = qpad[n % 2]
                    if n >= 2:
                        P.op("pool", lambda e, qp=qp, n=n: e.memset(qp[:, :, (n - 2) * 4:(n - 1) * 4], 0.0), writes=[qp.b])
                    P.op("pool", lambda e, qp=qp, n=n: e.tensor_copy(out=qp[:, :, n * 4:(n + 1) * 4], in_=qT[:, :, n * 4:(n + 1) * 4]), reads=[qT.b], writes=[qp.b])
                    P.dma("sp", lambda e, s0=s0, n=n: e.dma_start(out=s0[:, :, :], in_=sth2[n]), writes=[s0.b])
                    P.op("act", lambda e, s0=s0, s0b=s0b: e.activation(out=s0b[:, :, :], in_=s0[:, :, :], func=AF.Copy), reads=[s0.b], writes=[s0b.b])
                    for h in range(8):
                        po = bO[h // 4]
                        P.op("pe", lambda e, h=h, po=po, n=n, s0b=s0b, qp=qp: e.matmul(po[0:TS, (h % 4) * 128:(h % 4 + 1) * 128], lhsT=qp[:, h, :], rhs=s0b[:, h, :],
                                                                           start=False, stop=True, skip_group_check=True), reads=[qp.b, s0b.b], writes=[po.b])
                    P.op("dve", lambda e, vm=vm, n=n: e.tensor_scalar(out=vm[0:TS, :], in0=vtm[0:TS, 0, :], scalar1=seqm[:, n:n + 1], scalar2=None, op0=ALU.mult), reads=[vtm.b, seqm.b], writes=[vm.b])
                    for h in range(8):
                        pk = bKV[h // 4]
                        P.op("pe", lambda e, h=h, pk=pk, vm=vm: e.matmul(pk[:, (h % 4) * 128:(h % 4 + 1) * 128], lhsT=ktm[0:TS, h, :], rhs=vm[0:TS, h * 128:(h + 1) * 128], start=True, stop=True),
                             reads=[ktm.b, vm.b], writes=[pk.b])
                    for hb in range(2):
                        P.op("dve", lambda e, hb=hb, s0=s0: e.tensor_tensor(out=Stmp[:, hb * 4:(hb + 1) * 4, :], in0=bKV[hb][:, :].rearrange("p (h v) -> p h v", v=128), in1=s0[:, hb * 4:(hb + 1) * 4, :], op=ALU.add),
                             reads=[bKV[hb].b, s0.b], writes=[Stmp.b])
                    P.op("dve", lambda e, sn=sn, n=n: e.tensor_tensor(out=sn[:, :, :], in0=Stmp[:, :, :], in1=decs[:, :, n:n + 1].to_broadcast([128, 8, 128]), op=ALU.mult),
                         reads=[Stmp.b, decs.b], writes=[sn.b])
                    P.dma("pool", lambda e, sn=sn, n=n: e.dma_start(out=hg2[n], in_=sn[:, :, :]), reads=[sn.b], writes=[])
                xr = xt[1]
                P.dma("sp", lambda e: e.dma_start(out=xr[0:TS, :], in_=x1b[SEQ:SEQ + TS, :]), reads=[b_x1b[NT]], writes=[xr.b])
                finish_tokens(TS, 0, xr, yt[0], y_s)
        P.emit()
    return nc


_NC_CACHE = {}


def _consts():
    bf = ml_dtypes.bfloat16
    p = np.arange(128)
    tri = (p[:, None] <= p[None, :]).astype(np.float32)
    q = np.arange(64)
    mnew = ((q[:, None] // 4 == q[None, :] // 4) & (q[:, None] <= q[None, :])).astype(np.float32)
    blkc = ((p[:, None] // 64 == p[None, :] // 64) & (p[:, None] <= p[None, :])).astype(np.float32)
    rm64 = np.ones((128, 512), np.float32); rm64[:, ::64] = 0.0
    rm4 = np.ones((128, 64), np.float32); rm4[:, ::4] = 0.0
    seqm = (q[:, None] // 4 == np.arange(16)[None, :]).astype(np.float32)
    return {
        "identb": np.eye(128, dtype=np.float32).astype(bf), "ident32": np.eye(128, dtype=np.float32),
        "ones32": np.ones((128, 128), np.float32), "tri": tri.astype(bf), "mnew": mnew.astype(bf), "blkc": blkc.astype(bf),
        "rm64": rm64, "rm4": rm4, "seqm": seqm,
    }


def kernel(x_prompt, x_sample, cache_k, cache_v, state_conv, state_hgrn, page_table,
           norm_0, w_in_0, q_norm_0, k_norm_0, conv_w_0, conv_b_0, conv_ln_g_0, conv_ln_b_0, w_out_0,
           norm_1, w_in_1, lb_logits, o_norm_1, w_out_1):
    f = lambda a: np.ascontiguousarray(np.asarray(a, dtype=np.float32))
    if "nc" not in _NC_CACHE:
        _NC_CACHE["nc"] = build_nc()
    nc = _NC_CACHE["nc"]
    x_prompt = f(x_prompt); x_sample = f(x_sample)
    ckv = np.concatenate([f(cache_k).reshape(2560 * 128, 512), f(cache_v).reshape(2560 * 128, 512)], axis=1)
    state_conv = f(state_conv); state_hgrn = f(state_hgrn)
    page_table = np.ascontiguousarray(np.asarray(page_table, dtype=np.int32))
    pk = lambda v, k: np.ascontiguousarray(f(v).reshape(k, 128).T)
    shared = {
        "ckv": ckv,
        "w_in_0": f(w_in_0), "w_out_0": f(w_out_0), "w_in_1": f(w_in_1), "w_out_1": f(w_out_1),
        "n0": pk(norm_0, 8), "n1": pk(norm_1, 8),
        "qg": np.tile(f(q_norm_0), 8), "kg": np.tile(f(k_norm_0), 8), "og": np.tile(f(o_norm_1), 8),
        "cwT": np.ascontiguousarray(f(conv_w_0).T.reshape(4, 128, 31).transpose(1, 0, 2).reshape(128, 124)),
        "cb": pk(conv_b_0, 4), "lng": pk(conv_ln_g_0, 4), "lnb": pk(conv_ln_b_0, 4),
        "lb": np.ascontiguousarray(np.concatenate([pk(f(lb_logits)[0], 8), pk(f(lb_logits)[1], 8)], axis=1)),
    }
    shared.update(_consts())
    in_maps = []
    for c in range(8):
        m = dict(shared)
        m["xp"] = x_prompt[c // 2]
        m["xs"] = x_sample[c * NS:(c + 1) * NS].reshape(TS, 1024)
        m["stc"] = state_conv[c * NS:(c + 1) * NS]
        m["sth"] = state_hgrn[c * NS:(c + 1) * NS]
        m["ptab"] = page_table[c * NS:(c + 1) * NS].reshape(-1)
        in_maps.append(m)
    res = run_bass_kernel_spmd(nc, in_maps, core_ids=list(range(8)))
    R = res.results
    H = SEQ // 2

    def prompt(name, width):
        out = np.empty((4, SEQ, width), np.float32)
        for c in range(8):
            hlf = c % 2
            out[c // 2, hlf * H:(hlf + 1) * H] = R[c][name][hlf * H:(hlf + 1) * H]
        return out

    y_prompt = prompt("y_p", 1024)
    k_prompt = prompt("k_p", 512).reshape(4, SEQ, 8, 64)
    v_prompt = prompt("v_p", 512).reshape(4, SEQ, 8, 64)
    y_sample = np.concatenate([R[c]["y_s"].reshape(NS, 4, 1024) for c in range(8)], axis=0)
    k_sample = np.concatenate([R[c]["k_s"].reshape(NS, 4, 8, 64) for c in range(8)], axis=0)
    v_sample = np.concatenate([R[c]["v_s"].reshape(NS, 4, 8, 64) for c in range(8)], axis=0)
    conv_prompt = np.stack([R[2 * s + 1]["conv_p"] for s in range(4)], axis=0)
    conv_sample = np.concatenate([R[c]["conv_s"] for c in range(8)], axis=0)
    hgrn_prompt = np.stack([R[2 * s + 1]["hg_p"] for s in range(4)], axis=0)
    hgrn_sample = np.concatenate([R[c]["hg_s"] for c in range(8)], axis=0)
    return (y_prompt, y_sample, k_prompt, v_prompt, k_sample, v_sample,
            conv_prompt, conv_sample, hgrn_prompt, hgrn_sample)
```
